# Optimizing a Trainium2 kernel written in Bass

```python
import math
import jax
import jax.numpy as jnp
from jax import lax
import numpy as np

D_MODEL = 2048
BATCH = 4
SEQ = 2048
DEPTH = 1
DEC_BATCH = 128
DEC_SEQ = 1
PAST_LEN = 16384
PAGE_SIZE = 128

POOL_WIDTH = D_MODEL // 2
POOL_WINDOWS = (2, 4, 8, 16)
POOL_GROUPS = len(POOL_WINDOWS)
POOL_GROUP_DIM = POOL_WIDTH // POOL_GROUPS
POOL_BUF = max(POOL_WINDOWS) - 1

SSM_WIDTH = D_MODEL // 2
SSM_GROUP_CH = 16
SSM_GROUPS = SSM_WIDTH // SSM_GROUP_CH
SSM_STATE = 64
SSM_DT_MIN = 0.001
SSM_DT_MAX = 0.1

XA_HEADS = 4
XA_HEAD_DIM = D_MODEL // 8
XA_WIDTH = XA_HEADS * XA_HEAD_DIM
XA_SCALE = XA_HEAD_DIM ** -0.5
N_MEM = 256

N_BRANCH = 3
OFF_SSM = POOL_WIDTH
OFF_XA = POOL_WIDTH + SSM_WIDTH
OFF_GATE = POOL_WIDTH + SSM_WIDTH + XA_WIDTH
IN_WIDTH = OFF_GATE + N_BRANCH * D_MODEL

D_FF = 256 * ((8 * D_MODEL // 3 + 255) // 256)
RMS_EPS = 1e-6

kernel_name = 'hybrid_pool_s5_memxattn_step'


def rmsnorm(x, g):
    xf = x.astype(jnp.float32)
    y = xf * lax.rsqrt(jnp.mean(xf * xf, axis=-1, keepdims=True) + RMS_EPS)
    return (y * g.astype(jnp.float32)).astype(x.dtype)


def swiglu(x, w_gate, w_up, w_down):
    return (jax.nn.silu(x @ w_gate) * (x @ w_up)) @ w_down


def multiscale_pool(u, buf, pos, w_grp, scale, w_proj):
    b, l, _ = u.shape
    ext = jnp.concatenate([buf.astype(jnp.float32), u.astype(jnp.float32)], axis=1)
    cs = jnp.concatenate([jnp.zeros_like(ext[:, :1]), jnp.cumsum(ext, axis=1)], axis=1)
    end = cs[:, POOL_BUF + 1:]
    means = []
    for k, w in enumerate(POOL_WINDOWS):
        ch = slice(k * POOL_GROUP_DIM, (k + 1) * POOL_GROUP_DIM)
        start = POOL_BUF + 1 - w
        win_sum = end[..., ch] - cs[:, start:start + l, ch]
        count = jnp.minimum(pos + 1, w).astype(jnp.float32)[None, :, None]
        means.append(win_sum / count)
    diff = jnp.concatenate(means, axis=-1) - ext[:, POOL_BUF:]
    diff = diff.astype(u.dtype).reshape(b, l, POOL_GROUPS, POOL_GROUP_DIM)
    z = jnp.einsum('blgc,gcd->blgd', diff, w_grp).reshape(b, l, POOL_WIDTH) * scale
    return z @ w_proj, ext[:, -POOL_BUF:].astype(buf.dtype)


def _cplx_affine_combine(e1, e2):
    a1r, a1i, b1r, b1i = e1
    a2r, a2i, b2r, b2i = e2
    return (a1r * a2r - a1i * a2i,
            a1r * a2i + a1i * a2r,
            a2r * b1r - a2i * b1i + b2r,
            a2r * b1i + a2i * b1r + b2i)


def s5_ssm(u, h_re, h_im, a_re, a_im, log_step, b_re, b_im, c_re, c_im, d_skip):
    bt, l, _ = u.shape
    f32 = jnp.float32
    uf = u.astype(f32).reshape(bt, l, SSM_GROUPS, SSM_GROUP_CH)
    a_re = a_re.astype(f32)
    a_im = a_im.astype(f32)
    dt = jnp.exp(log_step.astype(f32))[:, None]
    mag = jnp.exp(a_re * dt)
    ang = a_im * dt
    lb_re = mag * jnp.cos(ang)
    lb_im = mag * jnp.sin(ang)
    den = a_re * a_re + a_im * a_im
    n_re = lb_re - 1.0
    f_re = (n_re * a_re + lb_im * a_im) / den
    f_im = (lb_im * a_re - n_re * a_im) / den
    b_re = b_re.astype(f32)
    b_im = b_im.astype(f32)
    bb_re = f_re[..., None] * b_re - f_im[..., None] * b_im
    bb_im = f_re[..., None] * b_im + f_im[..., None] * b_re
    bu_re = jnp.einsum('blgh,gnh->blgn', uf, bb_re)
    bu_im = jnp.einsum('blgh,gnh->blgn', uf, bb_im)
    h_re = h_re.astype(f32)
    h_im = h_im.astype(f32)
    bu_re = bu_re.at[:, 0].add(lb_re * h_re - lb_im * h_im)
    bu_im = bu_im.at[:, 0].add(lb_re * h_im + lb_im * h_re)
    a_seq_re = jnp.broadcast_to(lb_re, (1, l) + lb_re.shape)
    a_seq_im = jnp.broadcast_to(lb_im, (1, l) + lb_im.shape)
    _, _, x_re, x_im = lax.associative_scan(
        _cplx_affine_combine, (a_seq_re, a_seq_im, bu_re, bu_im), axis=1)
    y = (jnp.einsum('blgn,ghn->blgh', x_re, c_re.astype(f32))
         - jnp.einsum('blgn,ghn->blgh', x_im, c_im.astype(f32)))
    y = y.reshape(bt, l, SSM_WIDTH) + d_skip.astype(f32) * uf.reshape(bt, l, SSM_WIDTH)
    return y.astype(u.dtype), x_re[:, -1], x_im[:, -1]


def memory_kv(mem, g_mem, w_mem_k, w_mem_v):
    b, m, _ = mem.shape
    mn = rmsnorm(mem, g_mem)
    k = (mn @ w_mem_k).reshape(b, m, XA_HEADS, XA_HEAD_DIM)
    v = (mn @ w_mem_v).reshape(b, m, XA_HEADS, XA_HEAD_DIM)
    return k, v


def cross_attend(q, mem_k, mem_v, w_o):
    b, l, _ = q.shape
    qh = q.reshape(b, l, XA_HEADS, XA_HEAD_DIM).astype(jnp.float32)
    s = jnp.einsum('blhd,bmhd->bhlm', qh, mem_k.astype(jnp.float32)) * XA_SCALE
    p = jax.nn.softmax(s, axis=-1)
    o = jnp.einsum('bhlm,bmhd->blhd', p, mem_v.astype(jnp.float32)).astype(q.dtype)
    return o.reshape(b, l, XA_WIDTH) @ w_o


def decoder_layer(x, pos, mem_k, mem_v, pool_buf, h_re, h_im, p):
    b, l, _ = x.shape
    h = x + 0.5 * rmsnorm(swiglu(rmsnorm(x, p['g_ff1_pre']), p['w_ff1_gate'], p['w_ff1_up'],
                                  p['w_ff1_down']), p['g_ff1_post'])
    xn = rmsnorm(h, p['g_mix_pre'])
    proj = xn @ p['w_in']
    u_pool, u_ssm, q, gate_logits = jnp.split(proj, [OFF_SSM, OFF_XA, OFF_GATE], axis=-1)
    o_pool, new_buf = multiscale_pool(u_pool, pool_buf, pos, p['w_pool_grp'], p['pool_scale'],
                                      p['w_pool_out'])
    y_ssm, new_re, new_im = s5_ssm(u_ssm, h_re, h_im, p['ssm_a_re'], p['ssm_a_im'], p['ssm_log_step'],
                                   p['ssm_b_re'], p['ssm_b_im'], p['ssm_c_re'], p['ssm_c_im'],
                                   p['ssm_d'])
    g = jax.nn.gelu(y_ssm)
    o_ssm = (g @ p['w_glu_val']) * jax.nn.sigmoid(g @ p['w_glu_gate'])
    o_xa = cross_attend(q, mem_k, mem_v, p['w_xa_out'])
    gates = jax.nn.sigmoid(gate_logits.reshape(b, l, N_BRANCH, D_MODEL))
    merged = gates[:, :, 0] * o_pool + gates[:, :, 1] * o_ssm + gates[:, :, 2] * o_xa
    h = h + rmsnorm(merged @ p['w_out'], p['g_mix_post'])
    h = h + 0.5 * rmsnorm(swiglu(rmsnorm(h, p['g_ff2_pre']), p['w_ff2_gate'], p['w_ff2_up'],
                                  p['w_ff2_down']), p['g_ff2_post'])
    return h, new_buf, new_re, new_im


def setup_inputs(seed: int = 0) -> dict:
    key = jax.random.key(seed)
    keys = jax.random.split(key, 48)
    counter = [0]
    f32 = jnp.float32

    def nk():
        k = keys[counter[0]]
        counter[0] += 1
        return k

    def dense(shape, fan_in):
        return jax.random.normal(nk(), shape, f32) * fan_in ** -0.5

    def gain():
        return 1.0 + 0.02 * jax.random.normal(nk(), (DEPTH, D_MODEL), f32)

    L = DEPTH
    inp = {}
    inp['x_prompt'] = jax.random.normal(nk(), (BATCH, SEQ, D_MODEL), f32)
    inp['x_sample'] = jax.random.normal(nk(), (DEC_BATCH, DEC_SEQ, D_MODEL), f32)
    inp['mem_prompt'] = jax.random.normal(nk(), (BATCH, N_MEM, D_MODEL), f32)
    inp['cache_mem_k'] = jax.random.normal(nk(), (L, DEC_BATCH, N_MEM, XA_HEADS, XA_HEAD_DIM), f32)
    inp['cache_mem_v'] = jax.random.normal(nk(), (L, DEC_BATCH, N_MEM, XA_HEADS, XA_HEAD_DIM), f32)
    inp['state_pool'] = jax.random.normal(nk(), (L, DEC_BATCH, POOL_BUF, POOL_WIDTH), f32)
    inp['state_ssm_re'] = 0.1 * jax.random.normal(nk(), (L, DEC_BATCH, SSM_GROUPS, SSM_STATE), f32)
    inp['state_ssm_im'] = 0.1 * jax.random.normal(nk(), (L, DEC_BATCH, SSM_GROUPS, SSM_STATE), f32)
    inp['g_ff1_pre'] = gain()
    inp['w_ff1_gate'] = dense((L, D_MODEL, D_FF), D_MODEL)
    inp['w_ff1_up'] = dense((L, D_MODEL, D_FF), D_MODEL)
    inp['w_ff1_down'] = dense((L, D_FF, D_MODEL), D_FF)
    inp['g_ff1_post'] = gain()
    inp['g_mix_pre'] = gain()
    inp['w_in'] = dense((L, D_MODEL, IN_WIDTH), D_MODEL)
    inp['w_pool_grp'] = dense((L, POOL_GROUPS, POOL_GROUP_DIM, POOL_GROUP_DIM), POOL_GROUP_DIM)
    inp['pool_scale'] = 1.0 + 0.1 * jax.random.normal(nk(), (L, POOL_WIDTH), f32)
    inp['w_pool_out'] = dense((L, POOL_WIDTH, D_MODEL), POOL_WIDTH)
    inp['ssm_a_re'] = -0.5 * jnp.exp(0.05 * jax.random.normal(nk(), (L, SSM_GROUPS, SSM_STATE), f32))
    n_idx = jnp.arange(SSM_STATE, dtype=f32)[None, None, :]
    inp['ssm_a_im'] = math.pi * n_idx + 0.01 * jax.random.normal(nk(), (L, SSM_GROUPS, SSM_STATE), f32)
    inp['ssm_log_step'] = jax.random.uniform(nk(), (L, SSM_GROUPS), f32,
                                             minval=math.log(SSM_DT_MIN), maxval=math.log(SSM_DT_MAX))
    inp['ssm_b_re'] = dense((L, SSM_GROUPS, SSM_STATE, SSM_GROUP_CH), 2 * SSM_GROUP_CH)
    inp['ssm_b_im'] = dense((L, SSM_GROUPS, SSM_STATE, SSM_GROUP_CH), 2 * SSM_GROUP_CH)
    inp['ssm_c_re'] = dense((L, SSM_GROUPS, SSM_GROUP_CH, SSM_STATE), 2 * SSM_STATE)
    inp['ssm_c_im'] = dense((L, SSM_GROUPS, SSM_GROUP_CH, SSM_STATE), 2 * SSM_STATE)
    inp['ssm_d'] = jax.random.normal(nk(), (L, SSM_WIDTH), f32)
    inp['w_glu_val'] = dense((L, SSM_WIDTH, D_MODEL), SSM_WIDTH)
    inp['w_glu_gate'] = dense((L, SSM_WIDTH, D_MODEL), SSM_WIDTH)
    inp['g_mem'] = gain()
    inp['w_mem_k'] = dense((L, D_MODEL, XA_WIDTH), D_MODEL)
    inp['w_mem_v'] = dense((L, D_MODEL, XA_WIDTH), D_MODEL)
    inp['w_xa_out'] = dense((L, XA_WIDTH, D_MODEL), XA_WIDTH)
    inp['w_out'] = dense((L, D_MODEL, D_MODEL), D_MODEL)
    inp['g_mix_post'] = gain()
    inp['g_ff2_pre'] = gain()
    inp['w_ff2_gate'] = dense((L, D_MODEL, D_FF), D_MODEL)
    inp['w_ff2_up'] = dense((L, D_MODEL, D_FF), D_MODEL)
    inp['w_ff2_down'] = dense((L, D_FF, D_MODEL), D_FF)
    inp['g_ff2_post'] = gain()
    return inp


def reference(x_prompt, x_sample, mem_prompt, cache_mem_k, cache_mem_v, state_pool, state_ssm_re,
              state_ssm_im, g_ff1_pre, w_ff1_gate, w_ff1_up, w_ff1_down, g_ff1_post, g_mix_pre, w_in,
              w_pool_grp, pool_scale, w_pool_out, ssm_a_re, ssm_a_im, ssm_log_step, ssm_b_re, ssm_b_im,
              ssm_c_re, ssm_c_im, ssm_d, w_glu_val, w_glu_gate, g_mem, w_mem_k, w_mem_v, w_xa_out,
              w_out, g_mix_post, g_ff2_pre, w_ff2_gate, w_ff2_up, w_ff2_down, g_ff2_post):
    weights = dict(g_ff1_pre=g_ff1_pre, w_ff1_gate=w_ff1_gate, w_ff1_up=w_ff1_up, w_ff1_down=w_ff1_down,
                   g_ff1_post=g_ff1_post, g_mix_pre=g_mix_pre, w_in=w_in, w_pool_grp=w_pool_grp,
                   pool_scale=pool_scale, w_pool_out=w_pool_out, ssm_a_re=ssm_a_re, ssm_a_im=ssm_a_im,
                   ssm_log_step=ssm_log_step, ssm_b_re=ssm_b_re, ssm_b_im=ssm_b_im, ssm_c_re=ssm_c_re,
                   ssm_c_im=ssm_c_im, ssm_d=ssm_d, w_glu_val=w_glu_val, w_glu_gate=w_glu_gate,
                   w_xa_out=w_xa_out, w_out=w_out, g_mix_post=g_mix_post, g_ff2_pre=g_ff2_pre,
                   w_ff2_gate=w_ff2_gate, w_ff2_up=w_ff2_up, w_ff2_down=w_ff2_down, g_ff2_post=g_ff2_post)
    bp, lp, _ = x_prompt.shape
    ls = x_sample.shape[1]
    pos_p = jnp.arange(lp, dtype=jnp.int32)
    pos_s = PAST_LEN + jnp.arange(ls, dtype=jnp.int32)
    hp, hs = x_prompt, x_sample
    mk_l, mv_l, pp_l, rp_l, ip_l, ps_l, rs_l, is_l = [], [], [], [], [], [], [], []
    for layer in range(DEPTH):
        p = {name: w[layer] for name, w in weights.items()}
        mk, mv = memory_kv(mem_prompt, g_mem[layer], w_mem_k[layer], w_mem_v[layer])
        pool0 = jnp.zeros((bp, POOL_BUF, POOL_WIDTH), x_prompt.dtype)
        ssm0 = jnp.zeros((bp, SSM_GROUPS, SSM_STATE), jnp.float32)
        hp, pb, sr, si = decoder_layer(hp, pos_p, mk, mv, pool0, ssm0, ssm0, p)
        hs, pbs, srs, sis = decoder_layer(hs, pos_s, cache_mem_k[layer], cache_mem_v[layer],
                                          state_pool[layer], state_ssm_re[layer], state_ssm_im[layer], p)
        mk_l.append(mk.astype(cache_mem_k.dtype))
        mv_l.append(mv.astype(cache_mem_v.dtype))
        pp_l.append(pb.astype(state_pool.dtype))
        rp_l.append(sr.astype(state_ssm_re.dtype))
        ip_l.append(si.astype(state_ssm_im.dtype))
        ps_l.append(pbs.astype(state_pool.dtype))
        rs_l.append(srs.astype(state_ssm_re.dtype))
        is_l.append(sis.astype(state_ssm_im.dtype))
    return (hp, hs, jnp.stack(mk_l), jnp.stack(mv_l), jnp.stack(pp_l), jnp.stack(rp_l), jnp.stack(ip_l),
            jnp.stack(ps_l), jnp.stack(rs_l), jnp.stack(is_l))
```

```python
import numpy as np
import concourse.bass as bass
import concourse.mybir as mybir

F32 = mybir.dt.float32
BF16 = mybir.dt.bfloat16
I32 = mybir.dt.int32
ALU = mybir.AluOpType
AF = mybir.ActivationFunctionType
AX = mybir.AxisListType


class _Op:
    __slots__ = ("eng", "fn", "reads", "writes", "dma", "deps", "waits", "inc", "seq", "dval", "idx")


class Prog:
    ENGS = ("pe", "act", "dve", "pool", "sp")

    def __init__(self, nc, same_engine_sync=True):
        self.nc = nc
        self.ops = []
        self.same_engine_sync = same_engine_sync

    def add(self, eng, fn, reads=(), writes=(), dma=None):
        op = _Op()
        op.eng = eng
        op.fn = fn
        op.reads = tuple(reads)
        op.writes = tuple(writes)
        op.dma = dma
        op.inc = False
        op.idx = len(self.ops)
        self.ops.append(op)
        return op

    def pe(self, fn, reads=(), writes=()):
        return self.add("pe", fn, reads, writes)

    def act(self, fn, reads=(), writes=()):
        return self.add("act", fn, reads, writes)

    def dve(self, fn, reads=(), writes=()):
        return self.add("dve", fn, reads, writes)

    def pool(self, fn, reads=(), writes=()):
        return self.add("pool", fn, reads, writes)

    def dma(self, q, key, fn, reads=(), writes=()):
        return self.add(q, fn, reads, writes, dma=key)

    def emit(self, final_wait_keys=None):
        nc = self.nc
        ops = self.ops
        last_w = {}
        readers = {}
        for i, op in enumerate(ops):
            deps = set()
            for k in op.reads:
                j = last_w.get(k)
                if j is not None:
                    deps.add(j)
            for k in op.writes:
                j = last_w.get(k)
                if j is not None:
                    deps.add(j)
                lastr = {}
                for j in readers.get(k, ()):
                    oj = ops[j]
                    if oj.dma is not None:
                        deps.add(j)
                    else:
                        lastr[oj.eng] = j
                deps.update(lastr.values())
            deps.discard(i)
            op.deps = deps
            for k in op.reads:
                readers.setdefault(k, []).append(i)
            for k in op.writes:
                last_w[k] = i
                readers[k] = []
        buf_ops = {}
        for i, op in enumerate(ops):
            for k in op.reads + op.writes:
                if isinstance(k, tuple) and k and isinstance(k[0], Buf):
                    buf_ops.setdefault(k[0], []).append(i)
        alias_deps = {}

        def final_deps(A):
            if A not in alias_deps:
                last = {}
                for i in buf_ops.get(A, ()):
                    o = ops[i]
                    kk = ("d", o.dma) if o.dma is not None else ("e", o.eng)
                    last[kk] = i
                alias_deps[A] = list(last.values())
            return alias_deps[A]

        for i, op in enumerate(ops):
            seen = set()
            for k in op.reads + op.writes:
                if isinstance(k, tuple) and k and isinstance(k[0], Buf):
                    B = k[0]
                    if B in seen:
                        continue
                    seen.add(B)
                    for A in B.aliases:
                        op.deps.update(j for j in final_deps(A) if j < i)
        dcount = {}
        for op in ops:
            if op.dma is not None:
                dcount[op.dma] = dcount.get(op.dma, 0) + 1
                op.dval = 16 * dcount[op.dma]
        for op in ops:
            keep = []
            for j in op.deps:
                pj = ops[j]
                if pj.dma is None and pj.eng == op.eng:
                    if op.eng in ("pe", "sp") or not self.same_engine_sync:
                        continue
                keep.append(j)
                if pj.dma is None:
                    pj.inc = True
            op.deps = keep
        seqc = {e: 0 for e in self.ENGS}
        for op in ops:
            if op.dma is None and op.inc:
                seqc[op.eng] += 1
                op.seq = seqc[op.eng]
        dma_keys = sorted(dcount.keys(), key=str)
        self.n_sems = len(dma_keys) + 4
        import contextlib

        with contextlib.ExitStack() as st:
            esem = {e: st.enter_context(nc.semaphore("s_" + e)) for e in ("pe", "act", "dve", "pool")}
            dsem = {k: st.enter_context(nc.semaphore("d_%d" % n)) for n, k in enumerate(dma_keys)}
            per_eng = {e: [] for e in self.ENGS}
            for op in ops:
                per_eng[op.eng].append(op)
            for e in self.ENGS:
                waited = {}
                for op in per_eng[e]:
                    w = {}
                    for j in op.deps:
                        pj = ops[j]
                        if pj.dma is not None:
                            sk, sv = ("d", pj.dma), pj.dval
                        else:
                            sk, sv = ("e", pj.eng), pj.seq
                        if waited.get(sk, 0) >= sv:
                            continue
                        if w.get(sk, 0) < sv:
                            w[sk] = sv
                    for sk, sv in w.items():
                        waited[sk] = sv
                    op.waits = [((dsem[sk[1]] if sk[0] == "d" else esem[sk[1]]), sv) for sk, sv in w.items()]
            fin = []
            if final_wait_keys is None:
                final_wait_keys = dma_keys
            for k in final_wait_keys:
                fin.append((dsem[k], 16 * dcount[k]))
            block = st.enter_context(nc.Block())

            def run(eng_obj, e):
                for op in per_eng[e]:
                    for s, v in op.waits:
                        eng_obj.wait_ge(s, v)
                    ins = op.fn(eng_obj)
                    if op.dma is not None:
                        ins.then_inc(dsem[op.dma], 16)
                    elif op.inc:
                        ins.then_inc(esem[e], 1)
                if e == "sp":
                    for s, v in fin:
                        eng_obj.wait_ge(s, v)

            @block.tensor
            def _(eng):
                run(eng, "pe")

            @block.scalar
            def _(eng):
                run(eng, "act")

            @block.vector
            def _(eng):
                run(eng, "dve")

            @block.gpsimd
            def _(eng):
                run(eng, "pool")

            @block.sync
            def _(eng):
                run(eng, "sp")


class Buf:
    def __init__(self, arena, name, off, nbytes, dtype, nelem):
        self.arena = arena
        self.name = name
        self.off = off
        self.nbytes = nbytes
        self.dtype = dtype
        self.nelem = nelem
        self.aliases = []
        esz = 4 if dtype in (F32, I32) else 2
        nw = (nelem * esz + 3) // 4
        base = arena.big[:, off // 4:off // 4 + nw]
        self.ap = base if dtype == F32 else base.bitcast(dtype)[:, 0:nelem]

    def k(self, *sub):
        return (self,) + tuple(sub)

    def __repr__(self):
        return "Buf(%s)" % self.name


class Arena:
    def __init__(self, nc, big, total_bytes):
        self.nc = nc
        self.big = big
        self.total = total_bytes
        self.live = []
        self.freed = []

    def alloc(self, name, nelem, dtype=F32):
        esz = 4 if dtype in (F32, I32) else 2
        nbytes = (nelem * esz + 63) // 64 * 64
        spans = sorted((b.off, b.off + b.nbytes) for b in self.live)
        pos = 0
        off = None
        for a, e in spans:
            if a - pos >= nbytes:
                off = pos
                break
            pos = max(pos, e)
        if off is None:
            if self.total - pos >= nbytes:
                off = pos
            else:
                raise RuntimeError("arena OOM allocating %s (%d B); live=%s" % (
                    name, nbytes, [(b.name, b.off, b.nbytes) for b in self.live]))
        b = Buf(self, name, off, nbytes, dtype, nelem)
        b.aliases = [f for f in self.freed if f.off < off + nbytes and off < f.off + f.nbytes]
        self.live.append(b)
        return b

    def free(self, *bufs):
        for b in bufs:
            self.live.remove(b)
            self.freed.append(b)

import contextlib
import math
from concourse.bass_utils import run_bass_kernel_spmd

D = 2048
DFF = 5632
NPR = 1024
NS = 16
NM = NPR + NS
INW = 9216
OFF_SSM, OFF_XA, OFF_GATE = 1024, 2048, 3072
EPS = 1e-6
XA_SCALE = 256 ** -0.5
WSLOT = 5632
NWSLOT = 4
CW = 457


def tt_list(n, step=352):
    out, s = [], 0
    while s < n:
        e = min(n, s + step)
        out.append((s, e))
        s = e
    return out


import os


class _Stop(Exception):
    pass


class MK:
    def stage(self, k):
        if k > int(os.environ.get('MK_STOP', '999')):
            raise _Stop()

    def __init__(self, nc):
        self.nc = nc
        self.P = Prog(nc)
        self.rr = 0
        self.reserved = set()
        self.ev = 0

    def bank(self):
        while True:
            b = self.rr % 8
            self.rr += 1
            if b not in self.reserved:
                return b

    def psb(self, b):
        return self.ps[:, b * 512:(b + 1) * 512]

    def pk(self, b):
        return ("ps", b)

    def wload(self, view, kc, ncol):
        i = self.wi % len(self.wslots)
        self.wi += 1
        slot = self.wslots[i]
        v = slot.ap[:, 0:kc * ncol].rearrange("p (c n) -> p c n", c=kc)
        self.P.dma("pool", ("w", i), lambda e, v=v, view=view: e.dma_start(out=v, in_=view), writes=[slot.k()])
        return v, slot.k()

    def evac_eng(self):
        self.ev += 1
        return "act" if self.ev % 2 else "dve"

    def copy(self, eng, out, in_, reads, writes):
        if eng == "act":
            self.P.act(lambda e: e.copy(out=out, in_=in_), reads, writes)
        elif eng == "dve":
            self.P.dve(lambda e: e.tensor_copy(out=out, in_=in_), reads, writes)
        else:
            self.P.pool(lambda e: e.tensor_copy(out=out, in_=in_), reads, writes)

    def row_tiles(self, nrows):
        r = list(range(0, nrows - 127, 128))
        if r[-1] + 128 < nrows:
            r.append(nrows - 128)
        return r

    def transpose_in(self, src, nrows, dst, dkey):
        P = self.P
        for ti, r0 in enumerate(self.row_tiles(nrows)):
            xt = self.xt[ti % 2]
            P.dma("sp", ("xt", ti % 2), lambda e, xt=xt, r0=r0: e.dma_start(out=xt.ap, in_=src[r0:r0 + 128, :]), writes=[xt.k()])
            for c4 in range(4):
                b = self.bank()
                for j in range(4):
                    c = 4 * c4 + j
                    P.pe(lambda e, b=b, j=j, c=c, xt=xt: e.transpose(self.psb(b)[:, j * 128:(j + 1) * 128], xt.ap[:, c * 128:(c + 1) * 128], self.ident.ap),
                         reads=[xt.k(), self.ident.k()], writes=[self.pk(b)])
                o = dst[:, 4 * c4:4 * c4 + 4, r0:r0 + 128]
                i = self.psb(b).rearrange("p (j n) -> p j n", j=4)
                self.copy(self.evac_eng(), o, i, [self.pk(b)], [dkey(c) for c in range(4 * c4, 4 * c4 + 4)])

    def transpose_out(self, srcv, skey, nrows, dst):
        P = self.P
        for ti, r0 in enumerate(self.row_tiles(nrows)):
            yt = self.xt[ti % 2]
            for c4 in range(4):
                b = self.bank()
                for j in range(4):
                    c = 4 * c4 + j
                    P.pe(lambda e, b=b, j=j, c=c, r0=r0: e.transpose(self.psb(b)[:, j * 128:(j + 1) * 128], srcv[:, c, r0:r0 + 128], self.ident.ap),
                         reads=[skey(c), self.ident.k()], writes=[self.pk(b)])
                self.copy(self.evac_eng(), yt.ap[:, c4 * 512:(c4 + 1) * 512], self.psb(b), [self.pk(b)], [yt.k()])
            P.dma("sp", ("xt", ti % 2), lambda e, yt=yt, r0=r0: e.dma_start(out=dst[r0:r0 + 128, :], in_=yt.ap),
                  reads=[yt.k()], writes=[("dram", "y")])

    def stats(self, srcv, skey, n, nch, rstdb, denom):
        P = self.P
        tts = tt_list(n)
        bs = [self.bank() for _ in tts]
        for b in bs:
            self.reserved.add(b)
        for c in range(nch):
            sq = self.sq[c % 2]
            P.act(lambda e, sq=sq, c=c: e.activation(out=sq.ap[:, 0:n], in_=srcv[:, c, :], func=AF.Square),
                  reads=[skey(c)], writes=[sq.k()])
            for ti, (s, e_) in enumerate(tts):
                P.pe(lambda e, b=bs[ti], sq=sq, s=s, e_=e_, c=c: e.matmul(
                    self.psb(b)[:, 0:e_ - s], lhsT=self.onesb.ap, rhs=sq.ap[:, s:e_], start=(c == 0), stop=(c == nch - 1)),
                    reads=[sq.k(), self.onesb.k()], writes=[self.pk(bs[ti])])
        for ti, (s, e_) in enumerate(tts):
            b = bs[ti]
            P.act(lambda e, b=b, s=s, e_=e_: e.activation(out=rstdb.ap[:, s:e_], in_=self.psb(b)[:, 0:e_ - s], func=AF.Sqrt,
                                                          bias=self.epsb.ap[:, 0:1], scale=1.0 / denom),
                  reads=[self.pk(b), self.epsb.k()], writes=[rstdb.k()])
            self.reserved.discard(b)
        P.dve(lambda e: e.reciprocal(out=rstdb.ap[:, 0:n], in_=rstdb.ap[:, 0:n]), reads=[rstdb.k()], writes=[rstdb.k()])

    def normalize(self, srcv, skey, n, nch, rstdb, gi, dstv, dkey):
        for c in range(nch):
            self.P.dve(lambda e, c=c: e.scalar_tensor_tensor(out=dstv[:, c, :], in0=srcv[:, c, :], scalar=self.gT.ap[:, gi * 16 + c:gi * 16 + c + 1],
                                                             in1=rstdb.ap[:, 0:n], op0=ALU.mult, op1=ALU.mult),
                       reads=[skey(c), rstdb.k(), self.gT.k()], writes=[dkey(c)])

    def project(self, wfn, kcn, rhsfn, n_oc, tts, evac):
        for oc in range(n_oc):
            lfn, wkeys = wfn(oc)
            for ti, (s, e_) in enumerate(tts):
                b = self.bank()
                for kc in range(kcn):
                    rhs, rkeys = rhsfn(kc, s, e_)
                    self.P.pe(lambda e, b=b, lfn=lfn, kc=kc, rhs=rhs, w=e_ - s: e.matmul(
                        self.psb(b)[:, 0:w], lhsT=lfn(kc), rhs=rhs, start=(kc == 0), stop=(kc == kcn - 1)),
                        reads=list(wkeys) + list(rkeys), writes=[self.pk(b)])
                evac(oc, ti, s, e_, b)

    def ffn(self, xnv, xkey, n, Wg, Wu, Wd, fov, fkey):
        P = self.P
        tts = tt_list(n)
        HB = 22
        for half in range(2):
            hid = self.ar.alloc("hid", HB * n, BF16)
            hv = hid.ap.rearrange("p (c n) -> p c n", c=HB)
            for t in range(11):
                col0 = half * 2816 + t * 256
                wg, wgk = self.wload(Wg[:, col0:col0 + 256].rearrange("(c p) n -> p c n", p=128), 16, 256)
                wu, wuk = self.wload(Wu[:, col0:col0 + 256].rearrange("(c p) n -> p c n", p=128), 16, 256)
                for b2 in range(2):
                    blk = 2 * t + b2
                    for ti, (s, e_) in enumerate(tts):
                        w = e_ - s
                        bg, bu = self.bank(), self.bank()
                        for kc in range(16):
                            P.pe(lambda e, bg=bg, wg=wg, kc=kc, b2=b2, s=s, e_=e_, w=w: e.matmul(
                                self.psb(bg)[:, 0:w], lhsT=wg[:, kc, b2 * 128:(b2 + 1) * 128], rhs=xnv[:, kc, s:e_], start=(kc == 0), stop=(kc == 15)),
                                reads=[wgk, xkey(kc)], writes=[self.pk(bg)])
                        for kc in range(16):
                            P.pe(lambda e, bu=bu, wu=wu, kc=kc, b2=b2, s=s, e_=e_, w=w: e.matmul(
                                self.psb(bu)[:, 0:w], lhsT=wu[:, kc, b2 * 128:(b2 + 1) * 128], rhs=xnv[:, kc, s:e_], start=(kc == 0), stop=(kc == 15)),
                                reads=[wuk, xkey(kc)], writes=[self.pk(bu)])
                        tmp = self.tmp[self.ev % 2]
                        self.ev += 1
                        P.act(lambda e, bg=bg, tmp=tmp, w=w: e.activation(out=tmp.ap[:, 0:w], in_=self.psb(bg)[:, 0:w], func=AF.Silu),
                              reads=[self.pk(bg)], writes=[tmp.k()])
                        P.dve(lambda e, bu=bu, tmp=tmp, blk=blk, s=s, e_=e_, w=w: e.tensor_tensor(
                            out=hv[:, blk, s:e_], in0=tmp.ap[:, 0:w], in1=self.psb(bu)[:, 0:w], op=ALU.mult),
                            reads=[self.pk(bu), tmp.k()], writes=[hid.k(blk)])
            for t in range(8):
                wd, wdk = self.wload(Wd[half * 2816:(half + 1) * 2816, t * 256:(t + 1) * 256].rearrange("(c p) n -> p c n", p=128), HB, 256)
                for f2 in range(2):
                    f = 2 * t + f2
                    for ti, (s, e_) in enumerate(tts):
                        w = e_ - s
                        b = self.bank()
                        for kc in range(HB):
                            P.pe(lambda e, b=b, wd=wd, kc=kc, f2=f2, s=s, e_=e_, w=w: e.matmul(
                                self.psb(b)[:, 0:w], lhsT=wd[:, kc, f2 * 128:(f2 + 1) * 128], rhs=hv[:, kc, s:e_], start=(kc == 0), stop=(kc == HB - 1)),
                                reads=[wdk, hid.k(kc)], writes=[self.pk(b)])
                        if half == 0:
                            P.act(lambda e, b=b, f=f, s=s, e_=e_, w=w: e.copy(out=fov[:, f, s:e_], in_=self.psb(b)[:, 0:w]),
                                  reads=[self.pk(b)], writes=[fkey(f)])
                        else:
                            P.dve(lambda e, b=b, f=f, s=s, e_=e_, w=w: e.tensor_tensor(
                                out=fov[:, f, s:e_], in0=fov[:, f, s:e_], in1=self.psb(b)[:, 0:w], op=ALU.add),
                                reads=[self.pk(b), fkey(f)], writes=[fkey(f)])
            self.ar.free(hid)

    def epilogue(self, fo, n, gpost_col, resid_dram, resid_key, out_dram, out_key, gi_next, xn_next):
        P = self.P
        fov = fo.ap.rearrange("p (c n) -> p c n", c=16)
        fkey = lambda c: fo.k(c)
        rs1 = self.ar.alloc("rs1", n, F32)
        self.stats(fov, fkey, n, 16, rs1, float(D))
        for c in range(16):
            rt = self.rt[c % 2]
            P.dma("sp", ("rt", c % 2), lambda e, rt=rt, c=c: e.dma_start(out=rt.ap[:, 0:n], in_=resid_dram[:, c, 0:n]),
                  reads=[("dram", resid_key)], writes=[rt.k()])
            P.dve(lambda e, c=c: e.tensor_tensor(out=fov[:, c, :], in0=fov[:, c, :], in1=rs1.ap[:, 0:n], op=ALU.mult),
                  reads=[fkey(c), rs1.k()], writes=[fkey(c)])
            P.dve(lambda e, c=c, rt=rt: e.scalar_tensor_tensor(out=fov[:, c, :], in0=fov[:, c, :], scalar=self.gS.ap[:, gpost_col * 16 + c:gpost_col * 16 + c + 1],
                                                               in1=rt.ap[:, 0:n], op0=ALU.mult, op1=ALU.add),
                  reads=[fkey(c), rt.k(), self.gS.k()], writes=[fkey(c)])
            if out_dram is not None:
                P.dma("sp", ("hout", c % 2), lambda e, c=c: e.dma_start(out=out_dram[:, c, 0:n], in_=fov[:, c, :]),
                      reads=[fkey(c)], writes=[("dram", out_key)])
        self.ar.free(rs1)
        if xn_next is not None:
            rs2 = self.ar.alloc("rs2", n, F32)
            self.stats(fov, fkey, n, 16, rs2, float(D))
            xv = xn_next.ap.rearrange("p (c n) -> p c n", c=16)
            self.normalize(fov, fkey, n, 16, rs2, gi_next, xv, lambda c: xn_next.k(c))
            self.ar.free(rs2)

    def front(self, src, n, gi, xn, scr, scr_key):
        xT = self.ar.alloc("xT", 16 * n, F32)
        xTv = xT.ap.rearrange("p (c n) -> p c n", c=16)
        self.transpose_in(src, n, xTv, lambda c: xT.k(c))
        rs = self.ar.alloc("rs0", n, F32)
        self.stats(xTv, lambda c: xT.k(c), n, 16, rs, float(D))
        xv = xn.ap.rearrange("p (c n) -> p c n", c=16)
        self.normalize(xTv, lambda c: xT.k(c), n, 16, rs, gi, xv, lambda c: xn.k(c))
        for c in range(16 if scr is not None else 0):
            self.P.dma("sp", ("hout", c % 2), lambda e, c=c: e.dma_start(out=scr[:, c, 0:n], in_=xTv[:, c, :]),
                       reads=[xT.k(c)], writes=[("dram", scr_key)])
        self.ar.free(rs, xT)

    def win_proj(self, xn, n, col0, n_oc, evac, tts=None):
        xv = xn.ap.rearrange("p (c n) -> p c n", c=16)
        tts = tts or tt_list(n)
        for t in range(0, n_oc, 2):
            wv, wk = self.wload(self.win[:, col0 + t * 128:col0 + t * 128 + 256].rearrange("(c p) n -> p c n", p=128), 16, 256)
            self.project(lambda oc, wv=wv, wk=wk: ((lambda kc, oc=oc: wv[:, kc, oc * 128:(oc + 1) * 128]), [wk]),
                         16, lambda kc, s, e_: (xv[:, kc, s:e_], [xn.k(kc)]), 2, tts,
                         lambda oc, ti, s, e_, b, t=t: evac(t + oc, ti, s, e_, b))

    def setup_consts(self):
        P, ar = self.P, self.ar
        self.cst = ar.alloc("cst", CW, F32)
        P.dma("sp", "cst", lambda e: e.dma_start(out=self.cst.ap, in_=self.cst_d), writes=[self.cst.k()])
        self.ident = ar.alloc("ident", 128, F32)
        self.copy("dve", self.ident.ap, self.cst.ap[:, 0:128], [self.cst.k()], [self.ident.k()])
        self.identb = ar.alloc("identb", 128, BF16)
        self.copy("dve", self.identb.ap, self.cst.ap[:, 0:128], [self.cst.k()], [self.identb.k()])
        self.ones = ar.alloc("ones", 128, F32)
        P.dve(lambda e: e.memset(self.ones.ap, 1.0), writes=[self.ones.k()])
        self.onesb = ar.alloc("onesb", 128, BF16)
        P.dve(lambda e: e.memset(self.onesb.ap, 1.0), writes=[self.onesb.k()])
        self.zerob = ar.alloc("zerob", 128, BF16)
        P.dve(lambda e: e.memset(self.zerob.ap, 0.0), writes=[self.zerob.k()])
        self.epsb = ar.alloc("epsb", 1, F32)
        P.dve(lambda e: e.memset(self.epsb.ap, EPS), writes=[self.epsb.k()])
        self.gT = ar.alloc("gT", 7 * 16, F32)
        P.dma("sp", "gT", lambda e: e.dma_start(out=self.gT.ap.rearrange("p (g c) -> p g c", g=7),
                                                in_=self.gains_d.rearrange("g (c p) -> p g c", p=128), allow_slow_non_contiguous=True),
              writes=[self.gT.k()])
        self.gS = ar.alloc("gS", 3 * 16, F32)
        for k, (gi, sc) in enumerate(((1, 0.5), (3, 1.0), (5, 0.5))):
            P.dve(lambda e, k=k, gi=gi, sc=sc: e.tensor_scalar(out=self.gS.ap[:, k * 16:(k + 1) * 16], in0=self.gT.ap[:, gi * 16:(gi + 1) * 16],
                                                               scalar1=sc, scalar2=None, op0=ALU.mult),
                  reads=[self.gT.k()], writes=[self.gS.k()])
        self.pvec = ar.alloc("pvec", 16, F32)
        P.dma("sp", "pvec", lambda e: e.dma_start(out=self.pvec.ap[:, 0:8], in_=self.pscale_d.rearrange("(c p) -> p c", p=128),
                                                  allow_slow_non_contiguous=True), writes=[self.pvec.k()])
        P.dma("sp", "pvec", lambda e: e.dma_start(out=self.pvec.ap[:, 8:16], in_=self.ssmd_d.rearrange("(c p) -> p c", p=128),
                                                  allow_slow_non_contiguous=True), writes=[self.pvec.k()])

    def mem_kv(self):
        P, ar = self.P, self.ar
        n = 256
        mn = ar.alloc("mn", 16 * n, BF16)
        self.front(self.mem_d, n, 6, mn, None, None)
        mv = mn.ap.rearrange("p (c n) -> p c n", c=16)
        self.kT = ar.alloc("kT", 8 * 256, BF16)
        self.vb = ar.alloc("vb", 2 * 1024, BF16)
        kTv = self.kT.ap.rearrange("p (c n) -> p c n", c=8)
        vbv = self.vb.ap.rearrange("p (m f) -> p m f", m=2)
        ktm = ar.alloc("ktm", 2 * 1024, F32)
        ktmv = ktm.ap.rearrange("p (m f) -> p m f", m=2)
        kbf = ar.alloc("kbf", 2 * 1024, BF16)
        kbv = kbf.ap.rearrange("p (m f) -> p m f", m=2)
        for which, W, outd in ((0, self.wmk, self.mk_d), (1, self.wmv, self.mv_d)):
            for t in range(4):
                wv, wk = self.wload(W[:, t * 256:(t + 1) * 256].rearrange("(c p) n -> p c n", p=128), 16, 256)
                for mt in range(2):
                    b = self.bank()
                    for kc in range(16):
                        P.pe(lambda e, b=b, kc=kc, mt=mt, wv=wv: e.matmul(self.psb(b)[:, 0:256], lhsT=mv[:, kc, mt * 128:(mt + 1) * 128], rhs=wv[:, kc, :],
                                                                          start=(kc == 0), stop=(kc == 15)),
                             reads=[wk, mn.k(kc)], writes=[self.pk(b)])
                    self.copy("act", ktmv[:, mt, t * 256:(t + 1) * 256], self.psb(b)[:, 0:256], [self.pk(b)], [ktm.k()])
                    dst = (kbv if which == 0 else vbv)[:, mt, t * 256:(t + 1) * 256]
                    self.copy("dve", dst, ktmv[:, mt, t * 256:(t + 1) * 256], [ktm.k()], [kbf.k() if which == 0 else self.vb.k()])
            P.dma("sp", ("kv", which), lambda e, outd=outd: e.dma_start(out=outd.rearrange("(m p) f -> p m f", p=128), in_=ktmv),
                  reads=[ktm.k()], writes=[("dram", "mkv", which)])
            if which == 0:
                for c in range(8):
                    b = self.bank()
                    pb = self.psb(b)[:, 0:128].bitcast(BF16)
                    for mt in range(2):
                        P.pe(lambda e, pb=pb, c=c, mt=mt: e.transpose(pb[:, mt * 128:(mt + 1) * 128], kbv[:, mt, c * 128:(c + 1) * 128], self.identb.ap),
                             reads=[kbf.k(), self.identb.k()], writes=[self.pk(b)])
                    self.copy(self.evac_eng(), kTv[:, c, :], pb, [self.pk(b)], [self.kT.k()])
        ar.free(mn, ktm, kbf)

    def pool_branch(self, xn2):
        P, ar = self.P, self.ar
        L = 15 + NPR
        UPW = 15 + NM + 112
        UP = ar.alloc("UP", 8 * UPW, F32)
        UPv = UP.ap.rearrange("p (c n) -> p c n", c=8)
        P.dve(lambda e: e.memset(UPv[:, :, 15 + NM:UPW], 0.0), writes=[UP.k(c) for c in range(8)])
        ukey = lambda c: UP.k(c)

        def ev(oc, ti, s, e_, b):
            self.copy(self.evac_eng(), UPv[:, oc, 15 + s:15 + e_], self.psb(b)[:, 0:e_ - s], [self.pk(b)], [ukey(oc)])
        self.win_proj(xn2, NM, 0, 8, ev)
        self.copy("dve", UPv[:, :, 0:15], self.halo.ap.rearrange("p (c n) -> p c n", c=8), [self.halo.k()], [ukey(c) for c in range(8)])
        pp = ar.alloc("pp", 1024, F32)
        utm = ar.alloc("utm", 1024, F32)
        for dst, c0 in ((pp, 15 + NPR - 128), (utm, 15 + NPR)):
            for hb in range(2):
                b = self.bank()
                for j in range(4):
                    c = 4 * hb + j
                    P.pe(lambda e, b=b, j=j, c=c, c0=c0: e.transpose(self.psb(b)[:, j * 128:(j + 1) * 128], UPv[:, c, c0:c0 + 128], self.ident.ap),
                         reads=[ukey(c), self.ident.k()], writes=[self.pk(b)])
                self.copy(self.evac_eng(), dst.ap[:, hb * 512:(hb + 1) * 512], self.psb(b), [self.pk(b)], [dst.k()])
        P.dma("sp", "pp", lambda e: e.dma_start(out=self.poolp_d, in_=pp.ap[113:128, :]), reads=[pp.k()], writes=[("dram", "poolp")])
        P.dma("sp", "ps1", lambda e: e.dma_start(out=self.pools_d[:, 0:14, :], in_=self.spool_d[:, 1:15, :]), writes=[("dram", "pools")])
        P.dma("sp", "ps2", lambda e: e.dma_start(out=self.pools_d[:, 14, :], in_=utm.ap[0:16, :]), reads=[utm.k()], writes=[("dram", "pools2")])
        DF = ar.alloc("DF", 8 * NM, BF16)
        DFv = DF.ap.rearrange("p (c n) -> p c n", c=8)
        WA = ar.alloc("WA", 2 * L, F32)
        WB = ar.alloc("WB", 2 * L, F32)
        WAv = WA.ap.rearrange("p (c n) -> p c n", c=2)
        WBv = WB.ap.rearrange("p (c n) -> p c n", c=2)
        invc = self.cst.ap[:, 386:450].rearrange("p (k j) -> p k j", k=4)
        for k in range(4):
            w = 2 << k
            u2 = UPv[:, 2 * k:2 * k + 2, :]
            uk = [ukey(2 * k), ukey(2 * k + 1)]
            P.dve(lambda e, u2=u2: e.tensor_tensor(out=WAv[:, :, 1:L], in0=u2[:, :, 1:L], in1=u2[:, :, 0:L - 1], op=ALU.add), reads=uk, writes=[WA.k()])
            cur, curb = WAv, WA
            if k >= 1:
                P.dve(lambda e: e.tensor_tensor(out=WBv[:, :, 3:L], in0=WAv[:, :, 3:L], in1=WAv[:, :, 1:L - 2], op=ALU.add), reads=[WA.k()], writes=[WB.k()])
                cur, curb = WBv, WB
            if k >= 2:
                P.dve(lambda e: e.tensor_tensor(out=WAv[:, :, 7:L], in0=WBv[:, :, 7:L], in1=WBv[:, :, 3:L - 4], op=ALU.add), reads=[WB.k()], writes=[WA.k()])
                cur, curb = WAv, WA
            if k >= 3:
                P.dve(lambda e: e.tensor_tensor(out=WBv[:, :, 15:L], in0=WAv[:, :, 15:L], in1=WAv[:, :, 7:L - 8], op=ALU.add), reads=[WA.k()], writes=[WB.k()])
                cur, curb = WBv, WB
            dk = [DF.k(2 * k), DF.k(2 * k + 1)]
            P.dve(lambda e, cur=cur, u2=u2, k=k, w=w: e.scalar_tensor_tensor(out=DFv[:, 2 * k:2 * k + 2, 16:NPR], in0=cur[:, :, 31:L], scalar=1.0 / w,
                                                                             in1=u2[:, :, 31:L], op0=ALU.mult, op1=ALU.subtract),
                  reads=[curb.k()] + uk, writes=dk)
            t16 = self.t16
            P.dve(lambda e, cur=cur, k=k: e.tensor_tensor(out=t16.ap.rearrange("p (c n) -> p c n", c=2), in0=cur[:, :, 15:31],
                                                          in1=invc[:, k:k + 1, :].to_broadcast([128, 2, 16]), op=ALU.mult),
                  reads=[curb.k(), self.cst.k()], writes=[t16.k()])
            P.dve(lambda e, u2=u2, k=k: e.tensor_tensor(out=DFv[:, 2 * k:2 * k + 2, 0:16], in0=t16.ap.rearrange("p (c n) -> p c n", c=2),
                                                        in1=u2[:, :, 15:31], op=ALU.subtract),
                  reads=[t16.k()] + uk, writes=dk)
        ar.free(WA, WB)
        SPB = ar.alloc("SPB", 26 * 256, F32)
        bsum = ar.alloc("bsum", 1024, F32)
        P.dve(lambda e: e.memset(bsum.ap, 0.0), writes=[bsum.k()])
        off = 0
        for k in range(4):
            w = 2 << k
            r = w - 1
            v = SPB.ap[0:16, off * 256:(off + r) * 256].rearrange("p (r c) -> p r c", r=r)
            P.dma("sp", ("spb", k), lambda e, v=v, k=k, r=r: e.dma_start(out=v, in_=self.spool_d[:, 15 - r:15, k * 256:(k + 1) * 256]), writes=[SPB.k(k)])
            if r == 1:
                self.copy("dve", bsum.ap[0:16, k * 256:(k + 1) * 256], v[:, 0, :], [SPB.k(k)], [bsum.k()])
            else:
                P.dve(lambda e, v=v, k=k: e.tensor_reduce(out=bsum.ap[0:16, k * 256:(k + 1) * 256], in_=v.rearrange("p r c -> p c r"), axis=AX.X, op=ALU.add),
                      reads=[SPB.k(k)], writes=[bsum.k()])
            off += r
        P.dve(lambda e: e.tensor_tensor(out=bsum.ap[0:16, :], in0=bsum.ap[0:16, :], in1=utm.ap[0:16, :], op=ALU.add), reads=[bsum.k(), utm.k()], writes=[bsum.k()])
        for k in range(4):
            w = 2 << k
            P.dve(lambda e, k=k, w=w: e.scalar_tensor_tensor(out=bsum.ap[0:16, k * 256:(k + 1) * 256], in0=bsum.ap[0:16, k * 256:(k + 1) * 256], scalar=1.0 / w,
                                                             in1=utm.ap[0:16, k * 256:(k + 1) * 256], op0=ALU.mult, op1=ALU.subtract),
                  reads=[bsum.k(), utm.k()], writes=[bsum.k()])
        for hb in range(2):
            b = self.bank()
            for j in range(4):
                c = 4 * hb + j
                P.pe(lambda e, b=b, j=j, c=c: e.transpose(self.psb(b)[:, j * 128:(j + 1) * 128], bsum.ap[:, c * 128:(c + 1) * 128], self.ident.ap),
                     reads=[bsum.k(), self.ident.k()], writes=[self.pk(b)])
            self.copy("dve", DFv[:, 4 * hb:4 * hb + 4, NPR:NM], self.psb(b).rearrange("p (j n) -> p j n", j=4)[:, :, 0:16], [self.pk(b)],
                      [DF.k(c) for c in range(4 * hb, 4 * hb + 4)])
        ar.free(SPB, bsum, pp, utm, UP)
        Z = ar.alloc("Z", 8 * NM, BF16)
        Zv = Z.ap.rearrange("p (c n) -> p c n", c=8)
        tts = tt_list(NM)
        for k in range(4):
            wv, wk = self.wload(self.wgrp[k].rearrange("(c p) n -> p c n", p=128), 2, 256)

            def ev(oc, ti, s, e_, b, k=k):
                o = 2 * k + oc
                if self.evac_eng() == "act":
                    P.act(lambda e: e.mul(out=Zv[:, o, s:e_], in_=self.psb(b)[:, 0:e_ - s], mul=self.pvec.ap[:, o:o + 1]),
                          reads=[self.pk(b), self.pvec.k()], writes=[Z.k(o)])
                else:
                    P.dve(lambda e: e.tensor_scalar(out=Zv[:, o, s:e_], in0=self.psb(b)[:, 0:e_ - s], scalar1=self.pvec.ap[:, o:o + 1], scalar2=None, op0=ALU.mult),
                          reads=[self.pk(b), self.pvec.k()], writes=[Z.k(o)])
            self.project(lambda oc, wv=wv, wk=wk: ((lambda kc, oc=oc: wv[:, kc, oc * 128:(oc + 1) * 128]), [wk]),
                         2, lambda kc, s, e_, k=k: (DFv[:, 2 * k + kc, s:e_], [DF.k(2 * k + kc)]), 2, tts, ev)
        ar.free(DF)
        return Z

    def merge_all(self, xn2, Z, G, O, mb):
        P = self.P
        xv = xn2.ap.rearrange("p (c n) -> p c n", c=16)
        mv = mb.ap.rearrange("p (c n) -> p c n", c=16)
        tts = tt_list(NM)
        srcs = {0: Z, 1: G, 2: O}
        for t in range(8):
            tiles = {}
            for br, wA in ((1, self.wgv), (0, self.wpo), (2, self.wxo)):
                c0 = OFF_GATE + br * D + t * 256
                wg = self.wload(self.win[:, c0:c0 + 256].rearrange("(c p) n -> p c n", p=128), 16, 256)
                wa = self.wload(wA[:, t * 256:(t + 1) * 256].rearrange("(c p) n -> p c n", p=128), 8, 256)
                tiles[br] = (wg, wa)
                for f2 in range(2):
                    f = 2 * t + f2
                    if br == 1 and f2 == 0:
                        wb2 = self.wload(self.wgg[:, t * 256:(t + 1) * 256].rearrange("(c p) n -> p c n", p=128), 8, 256)
                    sv = srcs[br].ap.rearrange("p (c n) -> p c n", c=8)
                    for ti, (s, e_) in enumerate(tts):
                        w = e_ - s
                        acc = self.acc[ti]
                        bA = self.bank()
                        for kc in range(8):
                            P.pe(lambda e, bA=bA, kc=kc, wa=wa, f2=f2, s=s, e_=e_, w=w, sv=sv: e.matmul(
                                self.psb(bA)[:, 0:w], lhsT=wa[0][:, kc, f2 * 128:(f2 + 1) * 128], rhs=sv[:, kc, s:e_], start=(kc == 0), stop=(kc == 7)),
                                reads=[wa[1], srcs[br].k(kc)], writes=[self.pk(bA)])
                        if br == 1:
                            bB = self.bank()
                            for kc in range(8):
                                P.pe(lambda e, bB=bB, kc=kc, wb2=wb2, f2=f2, s=s, e_=e_, w=w, sv=sv: e.matmul(
                                    self.psb(bB)[:, 0:w], lhsT=wb2[0][:, kc, f2 * 128:(f2 + 1) * 128], rhs=sv[:, kc, s:e_], start=(kc == 0), stop=(kc == 7)),
                                    reads=[wb2[1], srcs[br].k(kc)], writes=[self.pk(bB)])
                        bG = self.bank()
                        for kc in range(16):
                            P.pe(lambda e, bG=bG, kc=kc, wg=wg, f2=f2, s=s, e_=e_, w=w: e.matmul(
                                self.psb(bG)[:, 0:w], lhsT=wg[0][:, kc, f2 * 128:(f2 + 1) * 128], rhs=xv[:, kc, s:e_], start=(kc == 0), stop=(kc == 15)),
                                reads=[wg[1], xn2.k(kc)], writes=[self.pk(bG)])
                        tg = self.tmp[self.ev % 2]
                        tb = self.tmp2[self.ev % 2]
                        self.ev += 1
                        P.act(lambda e, bG=bG, tg=tg, w=w: e.activation(out=tg.ap[:, 0:w], in_=self.psb(bG)[:, 0:w], func=AF.Sigmoid),
                              reads=[self.pk(bG)], writes=[tg.k()])
                        if br == 1:
                            P.act(lambda e, bB=bB, tb=tb, w=w: e.activation(out=tb.ap[:, 0:w], in_=self.psb(bB)[:, 0:w], func=AF.Sigmoid),
                                  reads=[self.pk(bB)], writes=[tb.k()])
                            P.dve(lambda e, bA=bA, tb=tb, w=w: e.tensor_tensor(out=tb.ap[:, 0:w], in0=tb.ap[:, 0:w], in1=self.psb(bA)[:, 0:w], op=ALU.mult),
                                  reads=[self.pk(bA), tb.k()], writes=[tb.k()])
                            P.dve(lambda e, tg=tg, tb=tb, acc=acc, f2=f2, w=w: e.tensor_tensor(out=acc.ap[:, f2 * 352:f2 * 352 + w], in0=tb.ap[:, 0:w], in1=tg.ap[:, 0:w], op=ALU.mult),
                                  reads=[tg.k(), tb.k()], writes=[acc.k(f2)])
                        else:
                            P.dve(lambda e, bA=bA, tg=tg, w=w: e.tensor_tensor(out=tg.ap[:, 0:w], in0=tg.ap[:, 0:w], in1=self.psb(bA)[:, 0:w], op=ALU.mult),
                                  reads=[self.pk(bA), tg.k()], writes=[tg.k()])
                            if br == 0:
                                P.dve(lambda e, tg=tg, acc=acc, f2=f2, w=w: e.tensor_tensor(out=acc.ap[:, f2 * 352:f2 * 352 + w], in0=acc.ap[:, f2 * 352:f2 * 352 + w], in1=tg.ap[:, 0:w], op=ALU.add),
                                      reads=[tg.k(), acc.k(f2)], writes=[acc.k(f2)])
                            else:
                                P.dve(lambda e, tg=tg, acc=acc, f=f, f2=f2, s=s, e_=e_, w=w: e.tensor_tensor(out=mv[:, f, s:e_], in0=acc.ap[:, f2 * 352:f2 * 352 + w], in1=tg.ap[:, 0:w], op=ALU.add),
                                      reads=[tg.k(), acc.k(f2)], writes=[mb.k(f)])

    def bank2(self):
        while True:
            b = self.rr % 8
            if b % 2 == 0 and b not in self.reserved and (b + 1) not in self.reserved:
                self.rr += 2
                return b
            self.rr += 1

    def attn_branch(self, xn2):
        P, ar = self.P, self.ar
        Q = ar.alloc("Q", 8 * NM, BF16)
        Qv = Q.ap.rearrange("p (c n) -> p c n", c=8)

        def ev(oc, ti, s, e_, b):
            if self.evac_eng() == "act":
                P.act(lambda e: e.mul(out=Qv[:, oc, s:e_], in_=self.psb(b)[:, 0:e_ - s], mul=XA_SCALE), reads=[self.pk(b)], writes=[Q.k(oc)])
            else:
                P.dve(lambda e: e.tensor_scalar(out=Qv[:, oc, s:e_], in0=self.psb(b)[:, 0:e_ - s], scalar1=XA_SCALE, scalar2=None, op0=ALU.mult),
                      reads=[self.pk(b)], writes=[Q.k(oc)])
        self.win_proj(xn2, NM, OFF_XA, 8, ev)
        O = ar.alloc("O", 8 * NM, BF16)
        Ov = O.ap.rearrange("p (c n) -> p c n", c=8)
        kTv = self.kT.ap.rearrange("p (c n) -> p c n", c=8)
        vbv = self.vb.ap.rearrange("p (m f) -> p m f", m=2)
        pT = ar.alloc("pT", 2 * 4 * 128, BF16)
        pTv = pT.ap.rearrange("p (m h n) -> p m h n", m=2, h=4)
        ex = ar.alloc("ex", 256, F32)
        pn = ar.alloc("pn", 256, BF16)
        st = ar.alloc("st", 16, F32)
        for tq in range(NPR // 128):
            t0 = tq * 128
            P.dve(lambda e: e.memset(st.ap[:, 8:12], 0.0), writes=[st.k()])
            for h in range(4):
                b = self.bank()
                for dc in range(2):
                    P.pe(lambda e, b=b, h=h, dc=dc, t0=t0: e.matmul(self.psb(b)[:, 0:256], lhsT=Qv[:, 2 * h + dc, t0:t0 + 128], rhs=kTv[:, 2 * h + dc, :],
                                                                    start=(dc == 0), stop=(dc == 1)),
                         reads=[Q.k(2 * h + dc), self.kT.k()], writes=[self.pk(b)])
                P.dve(lambda e, b=b, h=h: e.reduce_max(out=st.ap[:, h:h + 1], in_=self.psb(b)[:, 0:256], axis=AX.X), reads=[self.pk(b)], writes=[st.k()])
                P.dve(lambda e, h=h: e.tensor_scalar(out=st.ap[:, 4 + h:5 + h], in0=st.ap[:, h:h + 1], scalar1=-1.0, scalar2=None, op0=ALU.mult),
                      reads=[st.k()], writes=[st.k()])
                P.act(lambda e, b=b, h=h: e.activation(out=ex.ap, in_=self.psb(b)[:, 0:256], func=AF.Exp, bias=st.ap[:, 4 + h:5 + h], scale=1.0,
                                                       accum_out=st.ap[:, 8 + h:9 + h]),
                      reads=[self.pk(b), st.k()], writes=[ex.k(), st.k()])
                P.dve(lambda e, h=h: e.reciprocal(out=st.ap[:, 12 + h:13 + h], in_=st.ap[:, 8 + h:9 + h]), reads=[st.k()], writes=[st.k()])
                P.dve(lambda e, h=h: e.tensor_scalar(out=pn.ap, in0=ex.ap, scalar1=st.ap[:, 12 + h:13 + h], scalar2=None, op0=ALU.mult),
                      reads=[ex.k(), st.k()], writes=[pn.k()])
                b2 = self.bank()
                pb = self.psb(b2)[:, 0:128].bitcast(BF16)
                for mt in range(2):
                    P.pe(lambda e, pb=pb, mt=mt: e.transpose(pb[:, mt * 128:(mt + 1) * 128], pn.ap[:, mt * 128:(mt + 1) * 128], self.identb.ap),
                         reads=[pn.k(), self.identb.k()], writes=[self.pk(b2)])
                self.copy("act", pTv[:, :, h, :], pb.rearrange("p (m n) -> p m n", m=2), [self.pk(b2)], [pT.k()])
            for g in range(2):
                b = self.bank()
                for j in range(4):
                    c = 4 * g + j
                    for mt in range(2):
                        P.pe(lambda e, b=b, j=j, c=c, mt=mt: e.matmul(self.psb(b)[:, j * 128:(j + 1) * 128], lhsT=vbv[:, mt, c * 128:(c + 1) * 128],
                                                                      rhs=pTv[:, mt, c // 2, :], start=(mt == 0), stop=(mt == 1)),
                             reads=[self.vb.k(), pT.k()], writes=[self.pk(b)])
                self.copy(self.evac_eng(), Ov[:, 4 * g:4 * g + 4, t0:t0 + 128], self.psb(b).rearrange("p (j n) -> p j n", j=4), [self.pk(b)],
                          [O.k(c) for c in range(4 * g, 4 * g + 4)])
        ar.free(pT, ex, pn, st)
        qtm = ar.alloc("qtm", 1024, BF16)
        b = self.bank()
        pb = self.psb(b).bitcast(BF16)
        for c in range(8):
            P.pe(lambda e, pb=pb, c=c: e.transpose(pb[0:16, c * 128:(c + 1) * 128], Qv[:, c, NPR:NM], self.identb.ap),
                 reads=[Q.k(c), self.identb.k()], writes=[self.pk(b)])
        self.copy("dve", qtm.ap[0:16, :], pb[0:16, :], [self.pk(b)], [qtm.k()])
        sel = ar.alloc("sel", 16 * 128, BF16)
        selv = sel.ap.rearrange("p (s n) -> p s n", s=16)
        self.copy("dve", selv[0:16], self.ident.ap[0:16, 0:16].unsqueeze(2).to_broadcast([16, 16, 128]), [self.ident.k()], [sel.k()])
        KS = [ar.alloc("KS%d" % i, 2048, F32) for i in range(2)]
        prod = ar.alloc("prod", 1024, F32)
        sc = ar.alloc("sc", 128, F32)
        scv = sc.ap.rearrange("p (s m h) -> p s m h", s=16, m=2)
        for s_ in range(NS):
            ks = KS[s_ % 2]
            ksv = ks.ap.rearrange("p (m f) -> p m f", m=2)
            P.dma("sp", ("ks", s_ % 2), lambda e, ksv=ksv, s_=s_: e.dma_start(out=ksv, in_=self.ck_d[s_].rearrange("(m p) f -> p m f", p=128)), writes=[ks.k()])
            bq = self.bank2()
            for k in range(2):
                P.pe(lambda e, bq=bq, k=k, s_=s_: e.matmul(self.psb(bq + k), lhsT=selv[0:16, s_, :], rhs=qtm.ap[0:16, k * 512:(k + 1) * 512], start=True, stop=True),
                     reads=[sel.k(), qtm.k()], writes=[self.pk(bq + k)])
            for mt in range(2):
                for k in range(2):
                    P.dve(lambda e, ksv=ksv, mt=mt, k=k, bq=bq: e.tensor_tensor(out=prod.ap[:, k * 512:(k + 1) * 512], in0=ksv[:, mt, k * 512:(k + 1) * 512],
                                                                                 in1=self.psb(bq + k), op=ALU.mult),
                          reads=[ks.k(), self.pk(bq + k)], writes=[prod.k()])
                P.dve(lambda e, s_=s_, mt=mt: e.tensor_reduce(out=scv[:, s_, mt, :], in_=prod.ap.rearrange("p (h d) -> p h d", h=4), axis=AX.X, op=ALU.add),
                      reads=[prod.k()], writes=[sc.k()])
        ar.free(KS[0], KS[1], prod)
        VS = [ar.alloc("VS%d" % i, 2048, BF16) for i in range(2)]
        scm = ar.alloc("scm", 128, F32)
        P.dve(lambda e: e.memset(scm.ap, 0.0), writes=[scm.k()])
        scmv = scm.ap[:, 0:64].rearrange("p (s h) -> p s h", s=16)
        P.dve(lambda e: e.tensor_tensor(out=scmv, in0=scv[:, :, 0, :], in1=scv[:, :, 1, :], op=ALU.max), reads=[sc.k()], writes=[scm.k()])
        b = self.bank()
        P.pe(lambda e, b=b: e.transpose(self.psb(b)[:, 0:128], scm.ap, self.ident.ap), reads=[scm.k(), self.ident.k()], writes=[self.pk(b)])
        mxT = ar.alloc("mxT", 1, F32)
        P.dve(lambda e, b=b: e.reduce_max(out=mxT.ap, in_=self.psb(b)[:, 0:128], axis=AX.X), reads=[self.pk(b)], writes=[mxT.k()])
        dg = ar.alloc("dg", 64, F32)
        P.dve(lambda e: e.tensor_scalar(out=dg.ap, in0=self.ident.ap[:, 0:64], scalar1=mxT.ap[:, 0:1], scalar2=None, op0=ALU.mult),
              reads=[mxT.k(), self.ident.k()], writes=[dg.k()])
        b = self.bank()
        P.pe(lambda e, b=b: e.matmul(self.psb(b)[:, 0:64], lhsT=self.ones.ap, rhs=dg.ap, start=True, stop=True),
             reads=[dg.k(), self.ones.k()], writes=[self.pk(b)])
        self.copy("dve", scm.ap[:, 0:64], self.psb(b)[:, 0:64], [self.pk(b)], [scm.k()])
        P.dve(lambda e: e.tensor_tensor(out=scv, in0=scv, in1=scmv.unsqueeze(2).to_broadcast([128, 16, 2, 4]), op=ALU.subtract),
              reads=[sc.k(), scm.k()], writes=[sc.k()])
        P.act(lambda e: e.activation(out=sc.ap, in_=sc.ap, func=AF.Exp), reads=[sc.k()], writes=[sc.k()])
        b = self.bank()
        P.pe(lambda e, b=b: e.matmul(self.psb(b)[:, 0:128], lhsT=self.ones.ap, rhs=sc.ap, start=True, stop=True), reads=[sc.k(), self.ones.k()], writes=[self.pk(b)])
        cs = ar.alloc("cs", 128, F32)
        csv = cs.ap.rearrange("p (s m h) -> p s m h", s=16, m=2)
        self.copy("dve", cs.ap, self.psb(b)[:, 0:128], [self.pk(b)], [cs.k()])
        P.dve(lambda e: e.tensor_tensor(out=scmv, in0=csv[:, :, 0, :], in1=csv[:, :, 1, :], op=ALU.add), reads=[cs.k()], writes=[scm.k()])
        P.dve(lambda e: e.reciprocal(out=scm.ap[:, 0:64], in_=scm.ap[:, 0:64]), reads=[scm.k()], writes=[scm.k()])
        P.dve(lambda e: e.tensor_tensor(out=scv, in0=scv, in1=scmv.unsqueeze(2).to_broadcast([128, 16, 2, 4]), op=ALU.mult),
              reads=[sc.k(), scm.k()], writes=[sc.k()])
        pnz = ar.alloc("pnz", 16 * 8 * 16, BF16)
        pnzv = pnz.ap.rearrange("p (s g j) -> p s g j", s=16, g=8)
        eye16 = self.cst.ap[:, 128:384].rearrange("p (s j) -> p s j", s=16)
        P.dve(lambda e: e.tensor_tensor(out=pnzv, in0=sc.ap.rearrange("p (s g) -> p s g", s=16).unsqueeze(3).to_broadcast([128, 16, 8, 16]),
                                        in1=eye16.unsqueeze(2).to_broadcast([128, 16, 8, 16]), op=ALU.mult),
              reads=[sc.k(), self.cst.k()], writes=[pnz.k()])
        bo = self.bank2()
        self.reserved.add(bo)
        self.reserved.add(bo + 1)
        for k in range(2):
            P.pe(lambda e, k=k: e.matmul(self.ps[0:16, (bo + k) * 512:(bo + k + 1) * 512], lhsT=self.zerob.ap[:, 0:16], rhs=Ov[:, 0, 0:512], start=True, stop=False),
                 reads=[self.zerob.k(), O.k(0)], writes=[self.pk(bo + k)])
        for s_ in range(NS):
            vs = VS[s_ % 2]
            vsv = vs.ap.rearrange("p (m f) -> p m f", m=2)
            P.dma("pool", ("vs", s_ % 2), lambda e, vsv=vsv, s_=s_: e.dma_start(out=vsv, in_=self.cv_d[s_].rearrange("(m p) f -> p m f", p=128)), writes=[vs.k()])
            for mt in range(2):
                for h in range(4):
                    P.pe(lambda e, s_=s_, mt=mt, h=h, vsv=vsv: e.matmul(
                        self.ps[0:16, bo * 512 + h * 256:bo * 512 + (h + 1) * 256], lhsT=pnzv[:, s_, mt * 4 + h, :], rhs=vsv[:, mt, h * 256:(h + 1) * 256],
                        start=False, stop=(s_ == NS - 1 and mt == 1)),
                        reads=[pnz.k(), vs.k()], writes=[self.pk(bo), self.pk(bo + 1)])
        otm = ar.alloc("otm", 1024, F32)
        P.dve(lambda e: e.memset(otm.ap, 0.0), writes=[otm.k()])
        for k in range(2):
            self.copy("dve", otm.ap[0:16, k * 512:(k + 1) * 512], self.ps[0:16, (bo + k) * 512:(bo + k + 1) * 512], [self.pk(bo + k)], [otm.k()])
        self.reserved.discard(bo)
        self.reserved.discard(bo + 1)
        for hb in range(2):
            b = self.bank()
            for j in range(4):
                c = 4 * hb + j
                P.pe(lambda e, b=b, j=j, c=c: e.transpose(self.psb(b)[:, j * 128:(j + 1) * 128], otm.ap[:, c * 128:(c + 1) * 128], self.ident.ap),
                     reads=[otm.k(), self.ident.k()], writes=[self.pk(b)])
            self.copy("dve", Ov[:, 4 * hb:4 * hb + 4, NPR:NM], self.psb(b).rearrange("p (j n) -> p j n", j=4)[:, :, 0:16], [self.pk(b)],
                      [O.k(c) for c in range(4 * hb, 4 * hb + 4)])
        ar.free(qtm, sel, VS[0], VS[1], sc, scm, mxT, dg, cs, pnz, Q, otm)
        return O

    def ssm_setup(self):
        P, ar = self.P, self.ar
        NSL = 64
        sm = ar.alloc("ssmsm", NSL * 32, F32)
        self.sm = sm
        smv = sm.ap.rearrange("p (i c) -> p i c", i=NSL)
        smi = ar.alloc("ssmi", 32, I32)
        K = [sm.k()]
        sl = lambda i: smv[:, i, :]
        (ARE, AIM, DT, MAG, ANG, T1, T2, T3, SIN, COS, LBRE, LBIM, DEN, NRE, FRE, FIM, RHO8, E8RE, E8IM) = range(19)
        PW = 20
        EM = 38
        XIN = 52
        self.SL = dict(sl=sl, PW=PW, EM=EM, XIN=XIN, INIT=54, XFIN=56, RHO8=RHO8, LBRE=LBRE, LBIM=LBIM, FRE=FRE, FIM=FIM, T1=T1, T2=T2, T3=T3)
        for g2 in range(2):
            ps_ = slice(g2 * 64, (g2 + 1) * 64)
            P.dma("sp", "ssm_a", lambda e, ps_=ps_, g2=g2: e.dma_start(out=smv[ps_, ARE, :], in_=self.are_d.rearrange("(pr g) n -> g n pr", g=2)[g2],
                                                                       allow_slow_non_contiguous=True), writes=K)
            P.dma("sp", "ssm_a", lambda e, ps_=ps_, g2=g2: e.dma_start(out=smv[ps_, AIM, :], in_=self.aim_d.rearrange("(pr g) n -> g n pr", g=2)[g2],
                                                                       allow_slow_non_contiguous=True), writes=K)
            P.dma("sp", "ssm_a", lambda e, ps_=ps_, g2=g2: e.dma_start(out=smv[ps_, DT, :], in_=self.lstep_d.rearrange("(pr g) -> g pr", g=2)[g2:g2 + 1, :].to_broadcast([64, 32]),
                                                                       allow_slow_non_contiguous=True), writes=K)

        def tt(o, a, b, op):
            P.dve(lambda e: e.tensor_tensor(out=sl(o), in0=sl(a), in1=sl(b), op=op), reads=K, writes=K)

        def ts(o, a, s1, op0, s2=None, op1=None):
            if op1 is None:
                P.dve(lambda e: e.tensor_scalar(out=sl(o), in0=sl(a), scalar1=s1, scalar2=None, op0=op0), reads=K, writes=K)
            else:
                P.dve(lambda e: e.tensor_scalar(out=sl(o), in0=sl(a), scalar1=s1, scalar2=s2, op0=op0, op1=op1), reads=K, writes=K)

        def act(o, a, f, scale=1.0):
            P.act(lambda e: e.activation(out=sl(o), in_=sl(a), func=f, scale=scale), reads=K, writes=K)

        def cmul(ore, oim, are_, aim_, bre_, bim_):
            tt(T1, are_, bre_, ALU.mult)
            tt(T2, aim_, bim_, ALU.mult)
            tt(T3, are_, bim_, ALU.mult)
            tt(ore, T1, T2, ALU.subtract)
            tt(T1, aim_, bre_, ALU.mult)
            tt(oim, T3, T1, ALU.add)
        self.cmul_small = cmul
        self.tt_small, self.ts_small = tt, ts

        X_, N_, R_, ACC, P2, TB = 58, 59, 60, 61, 62, 63

        def stt(o, a, sc, op0, b_, op1):
            P.dve(lambda e: e.scalar_tensor_tensor(out=sl(o), in0=sl(a), scalar=sc, in1=sl(b_), op0=op0, op1=op1), reads=K, writes=K)

        def rnd(o, a):
            P.dve(lambda e: e.tensor_copy(out=smi.ap, in_=sl(a)), reads=K, writes=[smi.k()])
            P.dve(lambda e: e.tensor_copy(out=sl(o), in_=smi.ap), reads=[smi.k()], writes=K)

        def exp_acc(o, a, scale, reduce):
            ts(X_, a, scale, ALU.mult)
            if reduce:
                ts(T1, X_, 1.0 / math.log(2.0), ALU.mult)
                rnd(N_, T1)
                stt(R_, N_, -0.693359375, ALU.mult, X_, ALU.add)
                stt(R_, N_, 2.12194440e-4, ALU.mult, R_, ALU.add)
                ts(T1, N_, -1.0, ALU.mult, 0.0, ALU.max)
                ts(T1, T1, 31.0, ALU.min)
                P.dve(lambda e: e.memset(sl(P2), 1.0), writes=K)
                for kb in (4, 3, 2, 1, 0):
                    ts(TB, T1, -float((1 << kb) - 1), ALU.add, 0.0, ALU.max)
                    ts(TB, TB, 1.0, ALU.min)
                    stt(T1, TB, -float(1 << kb), ALU.mult, T1, ALU.add)
                    ts(TB, TB, -(1.0 - 2.0 ** (-(1 << kb))), ALU.mult, 1.0, ALU.add)
                    tt(P2, P2, TB, ALU.mult)
                rr = R_
            else:
                rr = X_
            ts(ACC, rr, 1.0 / math.factorial(9), ALU.mult)
            for k in range(8, 0, -1):
                stt(ACC, ACC, 1.0 / math.factorial(k), ALU.add, rr, ALU.mult)
            ts(ACC, ACC, 1.0, ALU.add)
            if reduce:
                tt(o, ACC, P2, ALU.mult)
            else:
                self.copy("dve", sl(o), sl(ACC), K, K)

        def sincos_acc(os_, oc_, a):
            ts(T1, a, 1.0 / (2 * math.pi), ALU.mult)
            rnd(N_, T1)
            stt(R_, N_, -6.28125, ALU.mult, a, ALU.add)
            stt(R_, N_, -0.0019353071795864769, ALU.mult, R_, ALU.add)
            ts(R_, R_, 0.125, ALU.mult)
            tt(X_, R_, R_, ALU.mult)
            ts(ACC, X_, 1.0 / 362880.0, ALU.mult)
            for c_ in (-1.0 / 5040.0, 1.0 / 120.0, -1.0 / 6.0):
                stt(ACC, ACC, c_, ALU.add, X_, ALU.mult)
            ts(ACC, ACC, 1.0, ALU.add)
            tt(os_, ACC, R_, ALU.mult)
            ts(ACC, X_, -1.0 / 3628800.0, ALU.mult)
            for c_ in (1.0 / 40320.0, -1.0 / 720.0, 1.0 / 24.0, -0.5):
                stt(ACC, ACC, c_, ALU.add, X_, ALU.mult)
            ts(oc_, ACC, 1.0, ALU.add)
            for _ in range(3):
                tt(T1, os_, oc_, ALU.mult)
                tt(T2, os_, os_, ALU.mult)
                ts(os_, T1, 2.0, ALU.mult)
                ts(oc_, T2, -2.0, ALU.mult, 1.0, ALU.add)

        exp_acc(DT, DT, 1.0, True)
        tt(T3, ARE, DT, ALU.mult)
        self.copy("dve", sl(ANG), sl(T3), K, K)
        exp_acc(MAG, ANG, 1.0, False)
        exp_acc(RHO8, ANG, 8.0, False)
        tt(ANG, AIM, DT, ALU.mult)
        sincos_acc(SIN, COS, ANG)
        tt(LBRE, MAG, COS, ALU.mult)
        tt(LBIM, MAG, SIN, ALU.mult)
        tt(T1, ARE, ARE, ALU.mult)
        tt(T2, AIM, AIM, ALU.mult)
        tt(DEN, T1, T2, ALU.add)
        P.dve(lambda e: e.reciprocal(out=sl(DEN), in_=sl(DEN)), reads=K, writes=K)
        ts(NRE, LBRE, -1.0, ALU.add)
        tt(T1, NRE, ARE, ALU.mult)
        tt(T2, LBIM, AIM, ALU.mult)
        tt(T1, T1, T2, ALU.add)
        tt(FRE, T1, DEN, ALU.mult)
        tt(T1, LBIM, ARE, ALU.mult)
        tt(T2, NRE, AIM, ALU.mult)
        tt(T1, T1, T2, ALU.subtract)
        tt(FIM, T1, DEN, ALU.mult)
        P.dve(lambda e: e.memset(sl(PW), 1.0), writes=K)
        P.dve(lambda e: e.memset(sl(PW + 1), 0.0), writes=K)
        self.copy("dve", sl(PW + 2), sl(LBRE), K, K)
        self.copy("dve", sl(PW + 3), sl(LBIM), K, K)
        for k in range(2, 9):
            cmul(PW + 2 * k, PW + 2 * k + 1, PW + 2 * (k - 1), PW + 2 * (k - 1) + 1, LBRE, LBIM)
        P.dve(lambda e: e.reciprocal(out=sl(T3), in_=sl(RHO8)), reads=K, writes=K)
        tt(E8RE, PW + 16, T3, ALU.mult)
        tt(E8IM, PW + 17, T3, ALU.mult)
        self.copy("dve", sl(EM), sl(E8RE), K, K)
        self.copy("dve", sl(EM + 1), sl(E8IM), K, K)
        for m in range(1, 7):
            cmul(EM + 2 * m, EM + 2 * m + 1, EM + 2 * (m - 1), EM + 2 * (m - 1) + 1, EM + 2 * (m - 1), EM + 2 * (m - 1) + 1)
        P.dve(lambda e: e.memset(smv[:, XIN:XIN + 2, :], 0.0), writes=K)
        BA = ar.alloc("BA", 2 * 512, F32)
        BAv = BA.ap.rearrange("p (r pr h) -> p r pr h", r=2, pr=32)
        for r, src in ((0, self.bre_d), (1, self.bim_d)):
            for g2 in range(2):
                P.dma("sp", "ssm_b", lambda e, r=r, g2=g2, src=src: e.dma_start(out=BAv[g2 * 64:(g2 + 1) * 64, r, :, :],
                                                                                in_=src.rearrange("(pr g) n h -> g n pr h", g=2)[g2]), writes=[BA.k()])
        self.BB = ar.alloc("BB", 2 * 512, F32)
        BBv = self.BB.ap.rearrange("p (r pr h) -> p r pr h", r=2, pr=32)
        ZT1 = ar.alloc("ZT1", 512, F32)
        ZT2 = ar.alloc("ZT2", 512, F32)
        z1 = ZT1.ap.rearrange("p (pr h) -> p pr h", pr=32)
        z2 = ZT2.ap.rearrange("p (pr h) -> p pr h", pr=32)
        bc = lambda i: sl(i).unsqueeze(2).to_broadcast([128, 32, 16])
        for (o, x0, s0, x1, s1, op) in ((BBv[:, 0], BAv[:, 0], FRE, BAv[:, 1], FIM, ALU.subtract), (BBv[:, 1], BAv[:, 0], FIM, BAv[:, 1], FRE, ALU.add)):
            P.dve(lambda e, x0=x0, s0=s0: e.tensor_tensor(out=z1, in0=x0, in1=bc(s0), op=ALU.mult), reads=[BA.k()] + K, writes=[ZT1.k()])
            P.dve(lambda e, x1=x1, s1=s1: e.tensor_tensor(out=z2, in0=x1, in1=bc(s1), op=ALU.mult), reads=[BA.k()] + K, writes=[ZT2.k()])
            P.dve(lambda e, o=o, op=op: e.tensor_tensor(out=o, in0=z1, in1=z2, op=op), reads=[ZT1.k(), ZT2.k()], writes=[self.BB.k()])
        ar.free(BA, ZT1, ZT2, smi)
        self.Cm = ar.alloc("Cm", 2 * 32 * 32, F32)
        CN = ar.alloc("CN", 8 * 64, F32)
        CX = ar.alloc("CX", 8 * 128, F32)
        CNv = CN.ap.rearrange("p (f n) -> p f n", f=8)
        CXv = CX.ap.rearrange("p (f g n) -> p f g n", f=8, g=2)
        for r, src in ((0, self.cre_d), (1, self.cim_d)):
            P.dma("sp", "ssm_c", lambda e, src=src: e.dma_start(out=CNv, in_=src.rearrange("(f g8) h n -> (g8 h) f n", g8=8)), writes=[CN.k()])
            for g2 in range(2):
                P.dve(lambda e, r=r, g2=g2: e.tensor_scalar(out=CXv[:, :, g2, :], in0=CNv, scalar1=self.cst.ap[:, 451 + g2:452 + g2],
                                                            scalar2=(1.0 if r == 0 else -1.0), op0=ALU.mult, op1=ALU.mult),
                      reads=[CN.k(), self.cst.k()], writes=[CX.k()])
            for hb in range(2):
                b = self.bank()
                for j in range(4):
                    fcx = 4 * hb + j
                    P.pe(lambda e, b=b, j=j, fcx=fcx: e.transpose(self.psb(b)[:, j * 128:(j + 1) * 128], CXv[:, fcx].rearrange("p g n -> p (g n)"), self.ident.ap),
                         reads=[CX.k(), self.ident.k()], writes=[self.pk(b)])
                self.copy(self.evac_eng(), self.Cm.ap[:, r * 1024 + hb * 512:r * 1024 + (hb + 1) * 512], self.psb(b), [self.pk(b)], [self.Cm.k()])
        ar.free(CN, CX)

    def ssm_run(self, U, main):
        SW = int(os.environ.get('MK_SSM', '127'))
        P, ar = self.P, self.ar
        S = self.SL
        sl, PW, XIN, INIT, XFIN, RHO8, EM = S["sl"], S["PW"], S["XIN"], S["INIT"], S["XFIN"], S["RHO8"], S["EM"]
        K = [self.sm.k()]
        smv = self.sm.ap.rearrange("p (i c) -> p i c", i=64)
        Uv = U.ap.rearrange("p (c n) -> p c n", c=8)
        G, Gv = U, Uv
        BBv = self.BB.ap.rearrange("p (r pr h) -> p r pr h", r=2, pr=32)
        Cmv = self.Cm.ap.rearrange("p (r pr g h) -> p r pr (g h)", r=2, pr=32, g=2)
        self.cmul_small(INIT, INIT + 1, XIN, XIN + 1, EM, EM + 1)
        if main:
            HS = ar.alloc("HS", 32 * 2 * 16, F32)
            HSv = HS.ap.rearrange("p (pr r s) -> p pr r s", pr=32, r=2)
            HSb = ar.alloc("HSb", 32 * 2 * 16, BF16)
            HSbv = HSb.ap.rearrange("p (pr r s) -> p pr r s", pr=32, r=2)
            BUs = ar.alloc("BUs", 32 * 2 * 16, F32)
            BUv = BUs.ap.rearrange("p (pr r s) -> p pr r s", pr=32, r=2)
            stm = ar.alloc("stm", 4096, F32)
            P.dve(lambda e: e.memset(stm.ap, 0.0), writes=[stm.k()])
            for r, src in ((0, self.sre_d), (1, self.sim_d)):
                P.dma("sp", "stm", lambda e, src=src: e.dma_start(out=stm.ap[0:16, :], in_=src), writes=[stm.k()])
                for g in range(8):
                    b = self.bank()
                    for j in range(4):
                        pr = 4 * g + j
                        P.pe(lambda e, b=b, j=j, pr=pr: e.transpose(self.psb(b)[:, j * 128:(j + 1) * 128], stm.ap[:, pr * 128:(pr + 1) * 128], self.ident.ap),
                             reads=[stm.k(), self.ident.k()], writes=[self.pk(b)])
                    self.copy(self.evac_eng(), HSv[:, 4 * g:4 * g + 4, r, :], self.psb(b).rearrange("p (j n) -> p j n", j=4)[:, :, 0:16], [self.pk(b)], [HS.k()])
            self.copy("dve", HSb.ap, HS.ap, [HS.k()], [HSb.k()])
            ar.free(stm)
        Zf = ar.alloc("Zf", 8 * 2 * 64, F32)
        Zfv = Zf.ap.rearrange("p (t r q h) -> p t r q h", t=8, r=2, q=4)
        CT1 = ar.alloc("CT1", 4 * 8 * 32, F32)
        CT2 = ar.alloc("CT2", 4 * 8 * 32, F32)
        ZT1, ZT2 = CT1, CT2
        z1 = ZT1.ap[:, 0:512].rearrange("p (t q h) -> p t q h", t=8, q=4)
        z2 = ZT2.ap[:, 0:512].rearrange("p (t q h) -> p t q h", t=8, q=4)
        zm = ar.alloc("Zm", 8 * 2 * 128, F32)
        zmb = ar.alloc("Zmb", 8 * 2 * 128, BF16)
        Cmb = ar.alloc("Cmb", 2 * 32 * 32, BF16)
        self.copy("dve", Cmb.ap, self.Cm.ap, [self.Cm.k()], [Cmb.k()])
        Cmbv = Cmb.ap.rearrange("p (r pr gh) -> p r pr gh", r=2, pr=32)
        ma = ar.alloc("MA", 8 * 2 * 128, BF16)
        Rb = [ar.alloc("Rtab%d" % i, 2 * 4 * 128, F32) for i in range(2)]
        PT1 = ar.alloc("PT1", 4 * 64, F32)
        PT2 = ar.alloc("PT2", 4 * 64, F32)
        Ssb = ar.alloc("Ssb", 4 * 2 * 128, F32)
        BP = ar.alloc("BP", 4 * 2 * 128, F32)
        Wt = ar.alloc("Wt", 4 * 2 * 128, F32)
        Xt = Ssb
        RT1 = ar.alloc("ST1", 4 * 128, F32)
        RT2 = ar.alloc("ST2", 4 * 128, F32)
        v4 = lambda b: b.ap.rearrange("p (q r c) -> p q r c", q=4, r=2)
        Ssv, BPv, Wv, Xv = v4(Ssb), v4(BP), v4(Wt), v4(Xt)
        r1 = RT1.ap.rearrange("p (q c) -> p q c", q=4)
        r2 = RT2.ap.rearrange("p (q c) -> p q c", q=4)
        if main:
            Xp = ar.alloc("Xp", 4 * 2 * 128, BF16)
            Xpv = v4(Xp)
            KT = ar.alloc("KT", 8 * 128, BF16)
            KTv = KT.ap.rearrange("p (t c) -> p t c", t=8)
            CL = ar.alloc("CL", 4 * 8 * 2 * 32, BF16)
            CLv = CL.ap.rearrange("p (q j r h) -> p q j r h", q=4, j=8, r=2)
            c1 = CT1.ap.rearrange("p (q j h) -> p q j h", q=4, j=8)
            c2 = CT2.ap.rearrange("p (q j h) -> p q j h", q=4, j=8)
            Yt = ar.alloc("Yt", NM, F32)
        for fc in range(8):
            prs = slice(4 * fc, 4 * fc + 4)
            pwre = smv[:, PW:PW + 16:2, prs].unsqueeze(3).to_broadcast([128, 8, 4, 16])
            pwim = smv[:, PW + 1:PW + 17:2, prs].unsqueeze(3).to_broadcast([128, 8, 4, 16])
            bre = BBv[:, 0, prs, :].unsqueeze(1).to_broadcast([128, 8, 4, 16])
            bim = BBv[:, 1, prs, :].unsqueeze(1).to_broadcast([128, 8, 4, 16])
            zk = [self.BB.k()] + K
            for th in range(2):
                ts_ = slice(4 * th, 4 * th + 4)
                q1 = PT1.ap.rearrange("p (t q h) -> p t q h", t=4, q=4)
                q2 = PT2.ap.rearrange("p (t q h) -> p t q h", t=4, q=4)
                for (o, p0, b0, p1, b1, op) in ((Zfv[:, ts_, 0], pwre[:, ts_], bre[:, ts_], pwim[:, ts_], bim[:, ts_], ALU.subtract),
                                                (Zfv[:, ts_, 1], pwre[:, ts_], bim[:, ts_], pwim[:, ts_], bre[:, ts_], ALU.add)):
                    P.pool(lambda e, p0=p0, b0=b0, q1=q1: e.tensor_tensor(out=q1, in0=p0, in1=b0, op=ALU.mult), reads=zk, writes=[PT1.k()])
                    P.pool(lambda e, p1=p1, b1=b1, q2=q2: e.tensor_tensor(out=q2, in0=p1, in1=b1, op=ALU.mult), reads=zk, writes=[PT2.k()])
                    P.pool(lambda e, o=o, op=op, q1=q1, q2=q2: e.tensor_tensor(out=o, in0=q1, in1=q2, op=op), reads=[PT1.k(), PT2.k()], writes=[Zf.k()])
            zmv = zm.ap.rearrange("p (t r q g h) -> p t r q g h", t=8, r=2, q=4, g=2)
            mav = ma.ap.rearrange("p (t r c) -> p t r c", t=8, r=2)
            for g2 in range(2):
                P.act(lambda e, g2=g2, zmv=zmv: e.mul(out=zmv[:, :, :, :, g2, :], in_=Zfv, mul=self.cst.ap[:, 384 + g2:385 + g2]),
                      reads=[Zf.k(), self.cst.k()], writes=[zm.k()])
            self.copy("act", zmb.ap, zm.ap, [zm.k()], [zmb.k()])
            zmbv = zmb.ap.rearrange("p (t r q gh) -> p t r q gh", t=8, r=2, q=4)
            R = Rb[fc % 2]
            Rv = R.ap.rearrange("p (r q c) -> p r q c", r=2, q=4)
            P.pool(lambda e, Rv=Rv: e.memset(Rv[:, 0, :, 0:1], 1.0), writes=[R.k()])
            P.pool(lambda e, Rv=Rv: e.memset(Rv[:, 1, :, 0:1], 0.0), writes=[R.k()])
            for m in range(7):
                n_ = 1 << m
                t1 = PT1.ap[:, 0:4 * n_].rearrange("p (q c) -> p q c", q=4)
                t2 = PT2.ap[:, 0:4 * n_].rearrange("p (q c) -> p q c", q=4)
                ere = sl(EM + 2 * m)[:, prs].unsqueeze(2).to_broadcast([128, 4, n_])
                eim = sl(EM + 2 * m + 1)[:, prs].unsqueeze(2).to_broadcast([128, 4, n_])
                sre, sim = Rv[:, 0, :, 0:n_], Rv[:, 1, :, 0:n_]
                dre, dim = Rv[:, 0, :, n_:2 * n_], Rv[:, 1, :, n_:2 * n_]
                P.pool(lambda e, t1=t1, sre=sre, ere=ere: e.tensor_tensor(out=t1, in0=sre, in1=ere, op=ALU.mult), reads=[R.k()] + K, writes=[PT1.k()])
                P.pool(lambda e, t2=t2, sim=sim, eim=eim: e.tensor_tensor(out=t2, in0=sim, in1=eim, op=ALU.mult), reads=[R.k()] + K, writes=[PT2.k()])
                P.pool(lambda e, t1=t1, t2=t2, dre=dre: e.tensor_tensor(out=dre, in0=t1, in1=t2, op=ALU.subtract), reads=[PT1.k(), PT2.k()], writes=[R.k()])
                P.pool(lambda e, t1=t1, sre=sre, eim=eim: e.tensor_tensor(out=t1, in0=sre, in1=eim, op=ALU.mult), reads=[R.k()] + K, writes=[PT1.k()])
                P.pool(lambda e, t2=t2, sim=sim, ere=ere: e.tensor_tensor(out=t2, in0=sim, in1=ere, op=ALU.mult), reads=[R.k()] + K, writes=[PT2.k()])
                P.pool(lambda e, t1=t1, t2=t2, dim=dim: e.tensor_tensor(out=dim, in0=t1, in1=t2, op=ALU.add), reads=[PT1.k(), PT2.k()], writes=[R.k()])
            zflat = zm.ap.rearrange("p (t r c) -> p t r c", t=8, r=2)
            for kb in range(4):
                b = self.bank()
                for j in range(4):
                    t, r = (4 * kb + j) // 2, (4 * kb + j) % 2
                    P.pe(lambda e, b=b, j=j, t=t, r=r, zflat=zflat: e.transpose(self.psb(b)[:, j * 128:(j + 1) * 128], zflat[:, t, r, :], self.ident.ap),
                         reads=[zm.k(), self.ident.k()], writes=[self.pk(b)])
                self.copy(self.evac_eng(), ma.ap[:, kb * 512:(kb + 1) * 512], self.psb(b), [self.pk(b)], [ma.k()])
            bqs = [self.bank() for _ in range(4)]
            for q in range(4):
                rows = slice(32 * q, 32 * q + 32)
                for r in range(2):
                    for i in range(8):
                        P.pe(lambda e, b=bqs[q], q=q, r=r, i=i, rows=rows, mav=mav, fc=fc: e.matmul(
                            self.psb(b)[:, r * 128:(r + 1) * 128], lhsT=mav[rows, 7 - i, r, :], rhs=Uv[rows, fc, i:NPR:8],
                            start=(i == 0), stop=(i == 7), tile_position=(32 * q, 0)),
                            reads=[ma.k(), U.k(fc)], writes=[self.pk(bqs[q])])
                    if main and (SW & 2):
                        P.pe(lambda e, b=bqs[q], q=q, r=r, rows=rows, mav=mav, fc=fc: e.matmul(
                            self.psb(b)[:, 256 + r * 16:256 + (r + 1) * 16], lhsT=mav[rows, 0, r, :], rhs=Uv[rows, fc, NPR:NM],
                            start=True, stop=True, tile_position=(32 * q, 0)),
                            reads=[ma.k(), U.k(fc)], writes=[self.pk(bqs[q])])
                self.copy("act", Ssv[:, q], self.psb(bqs[q])[:, 0:256].rearrange("p (r c) -> p r c", r=2), [self.pk(bqs[q])], [Ssb.k()])
                if main and (SW & 2):
                    self.copy("act", BUv[:, 4 * fc + q], self.psb(bqs[q])[:, 256:288].rearrange("p (r s) -> p r s", r=2), [self.pk(bqs[q])], [BUs.k()])
            Rre, Rim = Rv[:, 0], Rv[:, 1]
            Sre, Sim = Ssv[:, :, 0, :], Ssv[:, :, 1, :]

            def rot(ore, oim, are_, aim_, conj, rk, wk, Rre=Rre, Rim=Rim, R=R):
                P.dve(lambda e: e.tensor_tensor(out=r1, in0=Rre, in1=are_, op=ALU.mult), reads=rk + [R.k()], writes=[RT1.k()])
                P.dve(lambda e: e.tensor_tensor(out=r2, in0=Rim, in1=aim_, op=ALU.mult), reads=rk + [R.k()], writes=[RT2.k()])
                P.dve(lambda e: e.tensor_tensor(out=ore, in0=r1, in1=r2, op=(ALU.add if conj else ALU.subtract)), reads=[RT1.k(), RT2.k()], writes=wk)
                P.dve(lambda e: e.tensor_tensor(out=r1, in0=Rre, in1=aim_, op=ALU.mult), reads=rk + [R.k()], writes=[RT1.k()])
                P.dve(lambda e: e.tensor_tensor(out=r2, in0=Rim, in1=are_, op=ALU.mult), reads=rk + [R.k()], writes=[RT2.k()])
                P.dve(lambda e: e.tensor_tensor(out=oim, in0=r1, in1=r2, op=(ALU.subtract if conj else ALU.add)), reads=[RT1.k(), RT2.k()], writes=wk)
            rot(BPv[:, :, 0, :], BPv[:, :, 1, :], Sre, Sim, True, [Ssb.k()], [BP.k()])
            for q in range(4):
                pr = 4 * fc + q
                for r in range(2):
                    P.dve(lambda e, q=q, r=r, pr=pr: e.tensor_tensor_scan(out=Wv[:, q, r, :], data0=sl(RHO8)[:, pr:pr + 1].to_broadcast([128, 128]),
                                                                          data1=BPv[:, q, r, :], initial=sl(INIT + r)[:, pr:pr + 1], op0=ALU.mult, op1=ALU.add),
                          reads=[BP.k()] + K, writes=[Wt.k()])
            rot(Xv[:, :, 0, :], Xv[:, :, 1, :], Wv[:, :, 0, :], Wv[:, :, 1, :], False, [Wt.k()], [Xt.k()])
            for r in range(2):
                self.copy("dve", sl(XFIN + r)[:, 4 * fc:4 * fc + 4], Xv[:, :, r, 127], [Xt.k()], K)
            if not main:
                continue
            for r in range(2):
                self.copy("act", Xpv[:, :, r, 1:128], Xv[:, :, r, 0:127], [Xt.k()], [Xp.k()])
                self.copy("act", Xpv[:, :, r, 0], sl(XIN + r)[:, 4 * fc:4 * fc + 4], K, [Xp.k()])
            KTq = KT.ap.rearrange("p (t q c) -> p t q c", t=8, q=4)
            for q in range(4 if (SW & 4) else 0):
                pr = 4 * fc + q
                bkq = self.bank()
                P.pe(lambda e, bkq=bkq, fc=fc: e.matmul(self.psb(bkq)[:, 0:256], lhsT=self.zerob.ap, rhs=Uv[:, fc, 0:256], start=True, stop=False),
                     reads=[self.zerob.k(), U.k(fc)], writes=[self.pk(bkq)])
                for t in range(8):
                    for r in range(2):
                        P.pe(lambda e, t=t, q=q, r=r, pr=pr, zmbv=zmbv, bkq=bkq: e.matmul(
                            self.ps[32 * q:32 * q + 32, bkq * 512 + t * 32:bkq * 512 + (t + 1) * 32],
                            lhsT=zmbv[:, t, r, q, :], rhs=Cmbv[:, r, pr, :], start=False, stop=(t == 7 and r == 1), tile_position=(0, 32 * q)),
                            reads=[zmb.k(), Cmb.k()], writes=[self.pk(bkq)])
                P.act(lambda e, q=q, bkq=bkq: e.mul(out=KTq[:, :, q, :], in_=self.psb(bkq)[:, 0:256].rearrange("p (t c) -> p t c", t=8),
                                                   mul=self.cst.ap[:, 453 + q:454 + q]),
                      reads=[self.pk(bkq), self.cst.k()], writes=[KT.k()])
            pwre = smv[:, PW + 2:PW + 18:2, 4 * fc:4 * fc + 4].rearrange("p j q -> p q j").unsqueeze(3).to_broadcast([128, 4, 8, 32])
            pwim = smv[:, PW + 3:PW + 19:2, 4 * fc:4 * fc + 4].rearrange("p j q -> p q j").unsqueeze(3).to_broadcast([128, 4, 8, 32])
            cre = Cmv[:, 0, 4 * fc:4 * fc + 4, :].unsqueeze(2).to_broadcast([128, 4, 8, 32])
            cimn = Cmv[:, 1, 4 * fc:4 * fc + 4, :].unsqueeze(2).to_broadcast([128, 4, 8, 32])
            ck = [self.Cm.k()] + K
            P.dve(lambda e, cre=cre, pwre=pwre: e.tensor_tensor(out=c1, in0=cre, in1=pwre, op=ALU.mult), reads=ck, writes=[CT1.k()])
            P.dve(lambda e, cimn=cimn, pwim=pwim: e.tensor_tensor(out=c2, in0=cimn, in1=pwim, op=ALU.mult), reads=ck, writes=[CT2.k()])
            P.dve(lambda e: e.tensor_tensor(out=CLv[:, :, :, 0, :], in0=c1, in1=c2, op=ALU.add), reads=[CT1.k(), CT2.k()], writes=[CL.k()])
            P.dve(lambda e, cimn=cimn, pwre=pwre: e.tensor_tensor(out=c1, in0=cimn, in1=pwre, op=ALU.mult), reads=ck, writes=[CT1.k()])
            P.dve(lambda e, cre=cre, pwim=pwim: e.tensor_tensor(out=c2, in0=cre, in1=pwim, op=ALU.mult), reads=ck, writes=[CT2.k()])
            P.dve(lambda e: e.tensor_tensor(out=CLv[:, :, :, 1, :], in0=c1, in1=c2, op=ALU.subtract), reads=[CT1.k(), CT2.k()], writes=[CL.k()])
            by = self.bank2()
            for j in range(8 if (SW & 16) else 0):
                reg0 = by * 512 + j * 128
                for i in range(j + 1):
                    P.pe(lambda e, j=j, i=i, reg0=reg0, fc=fc: e.matmul(self.ps[:, reg0:reg0 + 128], lhsT=KTv[:, j - i, :], rhs=Uv[:, fc, i:NPR:8],
                                                                       start=(i == 0), stop=False),
                         reads=[KT.k(), U.k(fc)], writes=[self.pk(by + j // 4)])
                for q in range(4):
                    for r in range(2):
                        P.pe(lambda e, j=j, q=q, r=r, reg0=reg0: e.matmul(self.ps[32 * q:32 * q + 32, reg0:reg0 + 128], lhsT=CLv[:, q, j, r, :], rhs=Xpv[:, q, r, :],
                                                                          start=False, stop=(q == 3 and r == 1), tile_position=(0, 32 * q)),
                             reads=[CL.k(), Xp.k()], writes=[self.pk(by + j // 4)])
            bs = self.bank()
            if SW & 16:
                P.pe(lambda e, bs=bs, fc=fc: e.matmul(self.psb(bs)[:, 0:16], lhsT=KTv[:, 0, :], rhs=Uv[:, fc, NPR:NM], start=True, stop=False),
                     reads=[KT.k(), U.k(fc)], writes=[self.pk(bs)])
            for q in range(4 if (SW & 16) else 0):
                for r in range(2):
                    P.pe(lambda e, bs=bs, q=q, r=r, fc=fc: e.matmul(self.ps[32 * q:32 * q + 32, bs * 512:bs * 512 + 16], lhsT=CLv[:, q, 0, r, :], rhs=HSbv[:, 4 * fc + q, r, :],
                                                                    start=False, stop=(q == 3 and r == 1), tile_position=(0, 32 * q)),
                         reads=[CL.k(), HSb.k()], writes=[self.pk(bs)])
            dcol = self.pvec.ap[:, 8 + fc:9 + fc]
            for k in range(2):
                P.dve(lambda e, k=k, fc=fc, dcol=dcol, by=by: e.scalar_tensor_tensor(
                    out=Yt.ap[:, 0:NPR].rearrange("p (c j) -> p j c", j=8)[:, 4 * k:4 * k + 4, :],
                    in0=Uv[:, fc, 0:NPR].rearrange("p (c j) -> p j c", j=8)[:, 4 * k:4 * k + 4, :], scalar=dcol,
                    in1=self.psb(by + k).rearrange("p (j c) -> p j c", j=4), op0=ALU.mult, op1=ALU.add),
                    reads=[U.k(fc), self.pk(by + k), self.pvec.k()], writes=[Yt.k()])
            P.dve(lambda e, fc=fc, dcol=dcol, bs=bs: e.scalar_tensor_tensor(out=Yt.ap[:, NPR:NM], in0=Uv[:, fc, NPR:NM], scalar=dcol, in1=self.psb(bs)[:, 0:16],
                                                                            op0=ALU.mult, op1=ALU.add),
                  reads=[U.k(fc), self.pk(bs), self.pvec.k()], writes=[Yt.k()])
            P.act(lambda e, fc=fc: e.activation(out=Gv[:, fc, :], in_=Yt.ap, func=AF.Gelu), reads=[Yt.k()], writes=[G.k(fc)])
        for r, dst in ((0, self.srep_d), (1, self.simp_d)):
            for g2 in range(2):
                P.dma("sp", "xfin", lambda e, r=r, g2=g2, dst=dst: e.dma_start(out=dst.rearrange("(pr g) n -> g n pr", g=2)[g2], in_=sl(XFIN + r)[g2 * 64:(g2 + 1) * 64, :],
                                                                               allow_slow_non_contiguous=True),
                      reads=K, writes=[("dram", "xfin")])
        ar.free(Rb[0], Rb[1], PT1, PT2, zm, zmb, Cmb, ma, Zf, CT1, CT2, Ssb, BP, Wt, RT1, RT2)
        if main:
            ar.free(Xp, KT, CL, Yt)
            stm = ar.alloc("stm2", 4096, F32)
            XN = ar.alloc("XN", 32 * 2 * 16, F32)
            XNv = XN.ap.rearrange("p (pr r s) -> p pr r s", pr=32, r=2)
            T1 = ar.alloc("XT1", 512, F32)
            t1 = T1.ap.rearrange("p (pr s) -> p pr s", pr=32)
            lre = sl(S["LBRE"]).unsqueeze(2).to_broadcast([128, 32, 16])
            lim = sl(S["LBIM"]).unsqueeze(2).to_broadcast([128, 32, 16])
            hre, him = HSv[:, :, 0, :], HSv[:, :, 1, :]
            for (o, a, b_, sgn, bu) in ((XNv[:, :, 0, :], hre, him, ALU.subtract, BUv[:, :, 0, :]), (XNv[:, :, 1, :], him, hre, ALU.add, BUv[:, :, 1, :])):
                P.dve(lambda e, o=o, a=a: e.tensor_tensor(out=o, in0=a, in1=lre, op=ALU.mult), reads=[HS.k()] + K, writes=[XN.k()])
                P.dve(lambda e, b_=b_: e.tensor_tensor(out=t1, in0=b_, in1=lim, op=ALU.mult), reads=[HS.k()] + K, writes=[T1.k()])
                P.dve(lambda e, o=o, sgn=sgn: e.tensor_tensor(out=o, in0=o, in1=t1, op=sgn), reads=[XN.k(), T1.k()], writes=[XN.k()])
                P.dve(lambda e, o=o, bu=bu: e.tensor_tensor(out=o, in0=o, in1=bu, op=ALU.add), reads=[XN.k(), BUs.k()], writes=[XN.k()])
            TPs = [ar.alloc("TP%d" % i, 128, F32) for i in range(4)]
            for tp in TPs:
                P.dve(lambda e, tp=tp: e.memset(tp.ap, 0.0), writes=[tp.k()])
            for r, dst in ((0, self.sres_d), (1, self.sims_d)):
                for g in range(8):
                    b = self.bank()
                    for j in range(4):
                        pr = 4 * g + j
                        tp = TPs[j]
                        self.copy("dve", tp.ap[:, 0:16], XNv[:, pr, r, :], [XN.k()], [tp.k()])
                        P.pe(lambda e, b=b, j=j, tp=tp: e.transpose(self.psb(b)[:, j * 128:(j + 1) * 128], tp.ap, self.ident.ap),
                             reads=[tp.k(), self.ident.k()], writes=[self.pk(b)])
                    self.copy(self.evac_eng(), stm.ap[0:16, g * 512:(g + 1) * 512], self.psb(b)[0:16, :], [self.pk(b)], [stm.k()])
                P.dma("sp", "stm", lambda e, dst=dst: e.dma_start(out=dst, in_=stm.ap[0:16, :]), reads=[stm.k()], writes=[("dram", "sstate")])
            ar.free(*TPs)
            ar.free(XN, T1, HS, HSb, BUs, stm)


TOTAL_SBUF = 206 * 1024


def build_program():
    nc = bass.Bass("TRN2", target_bir_lowering=False)
    mk = MK(nc)
    P = mk.P

    def din(name, shape):
        return nc.dram_tensor(name, list(shape), F32, kind="ExternalInput").ap()

    def dout(name, shape):
        return nc.dram_tensor(name, list(shape), F32, kind="ExternalOutput").ap()

    xin_d = din("xin", [NM, D])
    xpre_d = din("xpre", [NPR, D])
    mk.mem_d = din("mem", [256, D])
    mk.ck_d = din("ck", [NS, 256, 1024])
    mk.cv_d = din("cv", [NS, 256, 1024])
    mk.spool_d = din("spool", [NS, 15, 1024])
    mk.sre_d = din("sre", [NS, 4096])
    mk.sim_d = din("sim", [NS, 4096])
    w1g, w1u, w1d = din("w1g", [D, DFF]), din("w1u", [D, DFF]), din("w1d", [DFF, D])
    w2g, w2u, w2d = din("w2g", [D, DFF]), din("w2u", [D, DFF]), din("w2d", [DFF, D])
    mk.win = din("win", [D, INW])
    mk.wgrp = din("wgrp", [4, 256, 256])
    mk.wpo = din("wpo", [1024, D])
    mk.wgv = din("wgv", [1024, D])
    mk.wgg = din("wgg", [1024, D])
    mk.wmk = din("wmk", [D, 1024])
    mk.wmv = din("wmv", [D, 1024])
    mk.wxo = din("wxo", [1024, D])
    wout = din("wout", [D, D])
    mk.gains_d = din("gains", [7, D])
    mk.pscale_d = din("pscale", [1024])
    mk.ssmd_d = din("ssmd", [1024])
    mk.are_d = din("are", [64, 64])
    mk.aim_d = din("aim", [64, 64])
    mk.lstep_d = din("lstep", [64])
    mk.bre_d = din("bre", [64, 64, 16])
    mk.bim_d = din("bim", [64, 64, 16])
    mk.cre_d = din("cre", [64, 16, 64])
    mk.cim_d = din("cim", [64, 16, 64])
    mk.cst_d = din("cst", [128, CW])
    y_d = dout("y", [NM, D])
    mk.mk_d = dout("mk", [256, 1024])
    mk.mv_d = dout("mv", [256, 1024])
    mk.poolp_d = dout("poolp", [15, 1024])
    mk.srep_d = dout("srep", [64, 64])
    mk.simp_d = dout("simp", [64, 64])
    mk.pools_d = dout("pools", [NS, 15, 1024])
    mk.sres_d = dout("sres", [NS, 4096])
    mk.sims_d = dout("sims", [NS, 4096])
    mk.scrA = nc.dram_tensor("scrA", [128, 16, NM], F32, kind="Internal").ap()
    mk.scrB = nc.dram_tensor("scrB", [128, 16, NM], F32, kind="Internal").ap()

    with contextlib.ExitStack() as st:
        big = st.enter_context(nc.sbuf_tensor("big", [128, TOTAL_SBUF // 4], F32))
        mk.ps = st.enter_context(nc.psum_tensor("ps", [128, 8 * 512], F32))
        ar = mk.ar = Arena(nc, big, TOTAL_SBUF)
        mk.wi = 0
        mk.wslots = [ar.alloc("wslot%d" % i, WSLOT, BF16) for i in range(NWSLOT)]
        mk.tmp = [ar.alloc("tmp%d" % i, 352, F32) for i in range(2)]
        mk.tmp2 = [ar.alloc("tmpb%d" % i, 352, F32) for i in range(2)]
        mk.t16 = ar.alloc("t16", 32, F32)
        mk.halo = ar.alloc("halo", 8 * 15, F32)
        xkeep = ar.alloc("xkeep", 64, F32)
        mk.setup_consts()
        flag = mk.cst.ap[:, 450:451]

        def scoped(fn):
            mk.xt = [ar.alloc("xt%d" % i, 2048, F32) for i in range(2)]
            mk.sq = [ar.alloc("sq%d" % i, NM, BF16) for i in range(2)]
            mk.rt = [ar.alloc("rt%d" % i, NM, F32) for i in range(2)]
            fn()
            ar.free(*mk.xt, *mk.sq, *mk.rt)

        def ffn_block(src_d, n, gi_pre, W, gpost_col, resid, resid_key, out, out_key, gi_next):
            pass

        def run_front(src_d, n, gi):
            xn = ar.alloc("xn", 16 * n, BF16)
            scoped(lambda: mk.front(src_d, n, gi, xn, mk.scrA, "scrA"))
            return xn

        def run_ffn(xn, n, Wg, Wu, Wd):
            fo = ar.alloc("fo", 16 * n, F32)
            fov = fo.ap.rearrange("p (c n) -> p c n", c=16)
            mk.ffn(xn.ap.rearrange("p (c n) -> p c n", c=16), lambda c: xn.k(c), n, Wg, Wu, Wd, fov, lambda c: fo.k(c))
            ar.free(xn)
            return fo

        def run_epi(fo, n, gpost_col, resid, resid_key, out, out_key, gi_next):
            xn_next = ar.alloc("xn", 16 * n, BF16) if gi_next is not None else None
            scoped(lambda: mk.epilogue(fo, n, gpost_col, resid, resid_key, out, out_key, gi_next, xn_next))
            return xn_next

        try:
            mk.stage(1)
            xn = run_front(xpre_d, NPR, 0)
            mk.stage(2)
            fo = run_ffn(xn, NPR, w1g, w1u, w1d)
            mk.stage(3)
            xn2 = run_epi(fo, NPR, 0, mk.scrA, "scrA", None, None, 2)
            mk.stage(4)
            ar.free(fo)
            U = ar.alloc("U", 8 * NPR, BF16)
            Uv = U.ap.rearrange("p (c n) -> p c n", c=8)
            mk.win_proj(xn2, NPR, OFF_SSM, 8, lambda oc, ti, s, e_, b: mk.copy(mk.evac_eng(), Uv[:, oc, s:e_], mk.psb(b)[:, 0:e_ - s], [mk.pk(b)], [U.k(oc)]))
            hv = mk.halo.ap.rearrange("p (c n) -> p c n", c=8)

            def halo_ev(oc, ti, s, e_, b):
                P.dve(lambda e: e.tensor_scalar(out=hv[:, oc, :], in0=mk.psb(b)[:, 1:16], scalar1=flag, scalar2=None, op0=ALU.mult),
                      reads=[mk.pk(b), mk.cst.k()], writes=[mk.halo.k()])
            mk.win_proj(xn2, NPR, 0, 8, halo_ev, tts=[(NPR - 16, NPR)])
            ar.free(xn2)
            mk.stage(5)
            mk.ssm_setup()
            mk.stage(6)
            mk.ssm_run(U, False)
            mk.stage(7)
            sl = mk.SL["sl"]
            for r in range(2):
                P.dve(lambda e, r=r, src=sl(mk.SL["XFIN"] + r): e.tensor_scalar(out=xkeep.ap[:, r * 32:(r + 1) * 32], in0=src, scalar1=flag, scalar2=None, op0=ALU.mult),
                      reads=[mk.sm.k(), mk.cst.k()], writes=[xkeep.k()])
            ar.free(U, mk.sm, mk.BB, mk.Cm)

            mk.stage(8)
            xn = run_front(xin_d, NM, 0)
            fo = run_ffn(xn, NM, w1g, w1u, w1d)
            xn2 = run_epi(fo, NM, 0, mk.scrA, "scrA", mk.scrB, "scrB", 2)
            ar.free(fo)
            U = ar.alloc("U", 8 * NM, BF16)
            Uv = U.ap.rearrange("p (c n) -> p c n", c=8)
            mk.win_proj(xn2, NM, OFF_SSM, 8, lambda oc, ti, s, e_, b: mk.copy(mk.evac_eng(), Uv[:, oc, s:e_], mk.psb(b)[:, 0:e_ - s], [mk.pk(b)], [U.k(oc)]))
            mk.ssm_setup()
            sl = mk.SL["sl"]
            for r in range(2):
                mk.copy("dve", sl(mk.SL["XIN"] + r), xkeep.ap[:, r * 32:(r + 1) * 32], [xkeep.k()], [mk.sm.k()])
            mk.stage(9)
            mk.ssm_run(U, True)
            mk.stage(10)
            ar.free(mk.sm, mk.BB, mk.Cm)
            G = U
            scoped_pool = mk.pool_branch(xn2)
            Z = scoped_pool
            mk.stage(11)
            scoped(lambda: mk.mem_kv())
            mk.stage(12)
            O = mk.attn_branch(xn2)
            mk.stage(13)
            ar.free(mk.kT, mk.vb)
            mb = ar.alloc("mb", 16 * NM, BF16)
            extra = [ar.alloc("wslotx%d" % i, WSLOT, BF16) for i in range(2)]
            mk.wslots = mk.wslots + extra
            mk.acc = [ar.alloc("acc%d" % i, 2 * 352, F32) for i in range(3)]
            mk.merge_all(xn2, Z, G, O, mb)
            mk.stage(14)
            ar.free(xn2, Z, G, O, *mk.acc)
            fo = ar.alloc("fo", 16 * NM, F32)
            fov = fo.ap.rearrange("p (c n) -> p c n", c=16)
            mbv = mb.ap.rearrange("p (c n) -> p c n", c=16)
            tts = tt_list(NM)
            for t in range(8):
                wv, wk = mk.wload(wout[:, t * 256:(t + 1) * 256].rearrange("(c p) n -> p c n", p=128), 16, 256)
                mk.project(lambda oc, wv=wv, wk=wk: ((lambda kc, oc=oc: wv[:, kc, oc * 128:(oc + 1) * 128]), [wk]),
                           16, lambda kc, s, e_: (mbv[:, kc, s:e_], [mb.k(kc)]), 2, tts,
                           lambda oc, ti, s, e_, b, t=t: mk.copy(mk.evac_eng(), fov[:, 2 * t + oc, s:e_], mk.psb(b)[:, 0:e_ - s], [mk.pk(b)], [fo.k(2 * t + oc)]))
            ar.free(mb)
            mk.wslots = mk.wslots[:NWSLOT]
            ar.free(*extra)
            mk.stage(15)
            xn3 = run_epi(fo, NM, 1, mk.scrB, "scrB", mk.scrA, "scrA", 4)
            ar.free(fo)
            fo = run_ffn(xn3, NM, w2g, w2u, w2d)
            run_epi(fo, NM, 2, mk.scrA, "scrA", None, None, None)
            scoped(lambda: mk.transpose_out(fo.ap.rearrange("p (c n) -> p c n", c=16), lambda c: fo.k(c), NM, y_d))
            ar.free(fo)
        except _Stop:
            pass
        P.emit()
    return nc


_NC_CACHE = {}


def _consts(core):
    half = core % 2
    c = np.zeros((128, CW), np.float32)
    c[:, 0:128] = np.eye(128, dtype=np.float32)
    c[:, 128:384] = np.tile(np.eye(16, dtype=np.float32).reshape(1, 256), (128, 1))
    p = np.arange(128)
    c[:, 384] = (p // 64 == 0)
    c[:, 385] = (p // 64 == 1)
    inv = np.zeros((4, 16), np.float32)
    for k in range(4):
        w = 2 << k
        for j in range(16):
            inv[k, j] = 1.0 / (min(j + 1, w) if half == 0 else w)
    c[:, 386:450] = inv.reshape(1, 64)
    c[:, 450] = float(half)
    c[:, 451] = ((p // 16) % 2 == 0)
    c[:, 452] = ((p // 16) % 2 == 1)
    for q in range(4):
        c[:, 453 + q] = (p // 32 == q)
    return c


def kernel(**inp):
    f = lambda a: np.ascontiguousarray(np.asarray(a, dtype=np.float32))
    x_prompt, x_sample, mem_prompt = f(inp["x_prompt"]), f(inp["x_sample"]), f(inp["mem_prompt"])
    shared = dict(
        w1g=f(inp["w_ff1_gate"][0]), w1u=f(inp["w_ff1_up"][0]), w1d=f(inp["w_ff1_down"][0]),
        w2g=f(inp["w_ff2_gate"][0]), w2u=f(inp["w_ff2_up"][0]), w2d=f(inp["w_ff2_down"][0]),
        win=f(inp["w_in"][0]), wgrp=f(inp["w_pool_grp"][0]), wpo=f(inp["w_pool_out"][0]),
        wgv=f(inp["w_glu_val"][0]), wgg=f(inp["w_glu_gate"][0]), wmk=f(inp["w_mem_k"][0]), wmv=f(inp["w_mem_v"][0]),
        wxo=f(inp["w_xa_out"][0]), wout=f(inp["w_out"][0]),
        gains=f(np.stack([inp["g_ff1_pre"][0], inp["g_ff1_post"][0], inp["g_mix_pre"][0], inp["g_mix_post"][0],
                          inp["g_ff2_pre"][0], inp["g_ff2_post"][0], inp["g_mem"][0]])),
        pscale=f(inp["pool_scale"][0]), ssmd=f(inp["ssm_d"][0]), are=f(inp["ssm_a_re"][0]), aim=f(inp["ssm_a_im"][0]),
        lstep=f(inp["ssm_log_step"][0]), bre=f(inp["ssm_b_re"][0]), bim=f(inp["ssm_b_im"][0]),
        cre=f(inp["ssm_c_re"][0]), cim=f(inp["ssm_c_im"][0]),
    )
    ck = f(inp["cache_mem_k"][0]).reshape(128, 256, 1024)
    cv = f(inp["cache_mem_v"][0]).reshape(128, 256, 1024)
    spool = f(inp["state_pool"][0])
    sre = f(inp["state_ssm_re"][0]).reshape(128, 4096)
    sim = f(inp["state_ssm_im"][0]).reshape(128, 4096)
    in_maps = []
    for c in range(8):
        b, half = c // 2, c % 2
        ss = slice(NS * c, NS * (c + 1))
        m = dict(shared)
        m["xin"] = np.ascontiguousarray(np.concatenate([x_prompt[b, half * NPR:(half + 1) * NPR], x_sample[ss, 0]], axis=0))
        m["xpre"] = np.ascontiguousarray(x_prompt[b, 0:NPR])
        m["mem"] = mem_prompt[b]
        m["ck"], m["cv"] = ck[ss], cv[ss]
        m["spool"], m["sre"], m["sim"] = spool[ss], sre[ss], sim[ss]
        m["cst"] = _consts(c)
        in_maps.append(m)
    if "nc" not in _NC_CACHE:
        _NC_CACHE["nc"] = build_program()
    res = run_bass_kernel_spmd(_NC_CACHE["nc"], in_maps, core_ids=list(range(8))).results
    y_p = np.zeros((4, 2048, D), np.float32)
    y_s = np.zeros((128, 1, D), np.float32)
    mk_o = np.zeros((1, 4, 256, 4, 256), np.float32)
    mv_o = np.zeros((1, 4, 256, 4, 256), np.float32)
    pp = np.zeros((1, 4, 15, 1024), np.float32)
    rp = np.zeros((1, 4, 64, 64), np.float32)
    ip = np.zeros((1, 4, 64, 64), np.float32)
    ps_ = np.zeros((1, 128, 15, 1024), np.float32)
    rs = np.zeros((1, 128, 64, 64), np.float32)
    is_ = np.zeros((1, 128, 64, 64), np.float32)
    for c in range(8):
        b, half = c // 2, c % 2
        r = res[c]
        y_p[b, half * NPR:(half + 1) * NPR] = r["y"][0:NPR]
        y_s[NS * c:NS * (c + 1), 0] = r["y"][NPR:NM]
        ps_[0, NS * c:NS * (c + 1)] = r["pools"]
        rs[0, NS * c:NS * (c + 1)] = r["sres"].reshape(NS, 64, 64)
        is_[0, NS * c:NS * (c + 1)] = r["sims"].reshape(NS, 64, 64)
        if half == 0:
            mk_o[0, b] = r["mk"].reshape(256, 4, 256)
            mv_o[0, b] = r["mv"].reshape(256, 4, 256)
        else:
            pp[0, b] = r["poolp"]
            rp[0, b] = r["srep"]
            ip[0, b] = r["simp"]
    return (y_p, y_s, mk_o, mv_o, pp, rp, ip, ps_, rs, is_)
```

```python
import numpy as np
import concourse.bass as bass
import concourse.mybir as mybir

F32 = mybir.dt.float32
BF16 = mybir.dt.bfloat16
I32 = mybir.dt.int32
ALU = mybir.AluOpType
AF = mybir.ActivationFunctionType
AX = mybir.AxisListType


class _Op:
    __slots__ = ("eng", "fn", "reads", "writes", "dma", "deps", "waits", "inc", "seq", "dval", "idx")


class Prog:
    ENGS = ("pe", "act", "dve", "pool", "sp")

    def __init__(self, nc, same_engine_sync=True):
        self.nc = nc
        self.ops = []
        self.same_engine_sync = same_engine_sync

    def add(self, eng, fn, reads=(), writes=(), dma=None):
        op = _Op()
        op.eng = eng
        op.fn = fn
        op.reads = tuple(reads)
        op.writes = tuple(writes)
        op.dma = dma
        op.inc = False
        op.idx = len(self.ops)
        self.ops.append(op)
        return op

    def pe(self, fn, reads=(), writes=()):
        return self.add("pe", fn, reads, writes)

    def act(self, fn, reads=(), writes=()):
        return self.add("act", fn, reads, writes)

    def dve(self, fn, reads=(), writes=()):
        return self.add("dve", fn, reads, writes)

    def pool(self, fn, reads=(), writes=()):
        return self.add("pool", fn, reads, writes)

    def dma(self, q, key, fn, reads=(), writes=()):
        return self.add(q, fn, reads, writes, dma=key)

    def emit(self, final_wait_keys=None):
        nc = self.nc
        ops = self.ops
        last_w = {}
        readers = {}
        for i, op in enumerate(ops):
            deps = set()
            for k in op.reads:
                j = last_w.get(k)
                if j is not None:
                    deps.add(j)
            for k in op.writes:
                j = last_w.get(k)
                if j is not None:
                    deps.add(j)
                lastr = {}
                for j in readers.get(k, ()):
                    oj = ops[j]
                    if oj.dma is not None:
                        deps.add(j)
                    else:
                        lastr[oj.eng] = j
                deps.update(lastr.values())
            deps.discard(i)
            op.deps = deps
            for k in op.reads:
                readers.setdefault(k, []).append(i)
            for k in op.writes:
                last_w[k] = i
                readers[k] = []
        buf_ops = {}
        for i, op in enumerate(ops):
            for k in op.reads + op.writes:
                if isinstance(k, tuple) and k and isinstance(k[0], Buf):
                    buf_ops.setdefault(k[0], []).append(i)
        alias_deps = {}

        def final_deps(A):
            if A not in alias_deps:
                last = {}
                for i in buf_ops.get(A, ()):
                    o = ops[i]
                    kk = ("d", o.dma) if o.dma is not None else ("e", o.eng)
                    last[kk] = i
                alias_deps[A] = list(last.values())
            return alias_deps[A]

        for i, op in enumerate(ops):
            seen = set()
            for k in op.reads + op.writes:
                if isinstance(k, tuple) and k and isinstance(k[0], Buf):
                    B = k[0]
                    if B in seen:
                        continue
                    seen.add(B)
                    for A in B.aliases:
                        op.deps.update(j for j in final_deps(A) if j < i)
        dcount = {}
        for op in ops:
            if op.dma is not None:
                dcount[op.dma] = dcount.get(op.dma, 0) + 1
                op.dval = 16 * dcount[op.dma]
        for op in ops:
            keep = []
            for j in op.deps:
                pj = ops[j]
                if pj.dma is None and pj.eng == op.eng:
                    if op.eng in ("pe", "sp") or not self.same_engine_sync:
                        continue
                keep.append(j)
                if pj.dma is None:
                    pj.inc = True
            op.deps = keep
        seqc = {e: 0 for e in self.ENGS}
        for op in ops:
            if op.dma is None and op.inc:
                seqc[op.eng] += 1
                op.seq = seqc[op.eng]
        dma_keys = sorted(dcount.keys(), key=str)
        self.n_sems = len(dma_keys) + 4
        import contextlib

        with contextlib.ExitStack() as st:
            esem = {e: st.enter_context(nc.semaphore("s_" + e)) for e in ("pe", "act", "dve", "pool")}
            dsem = {k: st.enter_context(nc.semaphore("d_%d" % n)) for n, k in enumerate(dma_keys)}
            per_eng = {e: [] for e in self.ENGS}
            for op in ops:
                per_eng[op.eng].append(op)
            for e in self.ENGS:
                waited = {}
                for op in per_eng[e]:
                    w = {}
                    for j in op.deps:
                        pj = ops[j]
                        if pj.dma is not None:
                            sk, sv = ("d", pj.dma), pj.dval
                        else:
                            sk, sv = ("e", pj.eng), pj.seq
                        if waited.get(sk, 0) >= sv:
                            continue
                        if w.get(sk, 0) < sv:
                            w[sk] = sv
                    for sk, sv in w.items():
                        waited[sk] = sv
                    op.waits = [((dsem[sk[1]] if sk[0] == "d" else esem[sk[1]]), sv) for sk, sv in w.items()]
            fin = []
            if final_wait_keys is None:
                final_wait_keys = dma_keys
            for k in final_wait_keys:
                fin.append((dsem[k], 16 * dcount[k]))
            block = st.enter_context(nc.Block())

            def run(eng_obj, e):
                for op in per_eng[e]:
                    for s, v in op.waits:
                        eng_obj.wait_ge(s, v)
                    ins = op.fn(eng_obj)
                    if op.dma is not None:
                        ins.then_inc(dsem[op.dma], 16)
                    elif op.inc:
                        ins.then_inc(esem[e], 1)
                if e == "sp":
                    for s, v in fin:
                        eng_obj.wait_ge(s, v)

            @block.tensor
            def _(eng):
                run(eng, "pe")

            @block.scalar
            def _(eng):
                run(eng, "act")

            @block.vector
            def _(eng):
                run(eng, "dve")

            @block.gpsimd
            def _(eng):
                run(eng, "pool")

            @block.sync
            def _(eng):
                run(eng, "sp")


class Buf:
    def __init__(self, arena, name, off, nbytes, dtype, nelem):
        self.arena = arena
        self.name = name
        self.off = off
        self.nbytes = nbytes
        self.dtype = dtype
        self.nelem = nelem
        self.aliases = []
        esz = 4 if dtype in (F32, I32) else 2
        nw = (nelem * esz + 3) // 4
        base = arena.big[:, off // 4:off // 4 + nw]
        self.ap = base if dtype == F32 else base.bitcast(dtype)[:, 0:nelem]

    def k(self, *sub):
        return (self,) + tuple(sub)

    def __repr__(self):
        return "Buf(%s)" % self.name


class Arena:
    def __init__(self, nc, big, total_bytes):
        self.nc = nc
        self.big = big
        self.total = total_bytes
        self.live = []
        self.freed = []

    def alloc(self, name, nelem, dtype=F32):
        esz = 4 if dtype in (F32, I32) else 2
        nbytes = (nelem * esz + 63) // 64 * 64
        spans = sorted((b.off, b.off + b.nbytes) for b in self.live)
        pos = 0
        off = None
        for a, e in spans:
            if a - pos >= nbytes:
                off = pos
                break
            pos = max(pos, e)
        if off is None:
            if self.total - pos >= nbytes:
                off = pos
            else:
                raise RuntimeError("arena OOM allocating %s (%d B); live=%s" % (
                    name, nbytes, [(b.name, b.off, b.nbytes) for b in self.live]))
        b = Buf(self, name, off, nbytes, dtype, nelem)
        b.aliases = [f for f in self.freed if f.off < off + nbytes and off < f.off + f.nbytes]
        self.live.append(b)
        return b

    def free(self, *bufs):
        for b in bufs:
            self.live.remove(b)
            self.freed.append(b)

import contextlib
import math
from concourse.bass_utils import run_bass_kernel_spmd

D = 2048
DFF = 5632
NPR = 1024
NS = 16
NM = NPR + NS
INW = 9216
OFF_SSM, OFF_XA, OFF_GATE = 1024, 2048, 3072
EPS = 1e-6
XA_SCALE = 256 ** -0.5
WSLOT = 5632
NWSLOT = 4
CW = 457


def tt_list(n, step=352):
    out, s = [], 0
    while s < n:
        e = min(n, s + step)
        out.append((s, e))
        s = e
    return out


import os


class _Stop(Exception):
    pass


class MK:
    def stage(self, k):
        if k > int(os.environ.get('MK_STOP', '999')):
            raise _Stop()

    def __init__(self, nc):
        self.nc = nc
        self.P = Prog(nc)
        self.rr = 0
        self.reserved = set()
        self.ev = 0

    def bank(self):
        while True:
            b = self.rr % 8
            self.rr += 1
            if b not in self.reserved:
                return b

    def psb(self, b):
        return self.ps[:, b * 512:(b + 1) * 512]

    def pk(self, b):
        return ("ps", b)

    def wload(self, view, kc, ncol):
        i = self.wi % len(self.wslots)
        self.wi += 1
        slot = self.wslots[i]
        v = slot.ap[:, 0:kc * ncol].rearrange("p (c n) -> p c n", c=kc)
        self.P.dma("pool", ("w", i), lambda e, v=v, view=view: e.dma_start(out=v, in_=view), writes=[slot.k()])
        return v, slot.k()

    def evac_eng(self):
        self.ev += 1
        return "act" if self.ev % 2 else "dve"

    def copy(self, eng, out, in_, reads, writes):
        if eng == "act":
            self.P.act(lambda e: e.copy(out=out, in_=in_), reads, writes)
        elif eng == "dve":
            self.P.dve(lambda e: e.tensor_copy(out=out, in_=in_), reads, writes)
        else:
            self.P.pool(lambda e: e.tensor_copy(out=out, in_=in_), reads, writes)

    def row_tiles(self, nrows):
        r = list(range(0, nrows - 127, 128))
        if r[-1] + 128 < nrows:
            r.append(nrows - 128)
        return r

    def transpose_in(self, src, nrows, dst, dkey):
        P = self.P
        for ti, r0 in enumerate(self.row_tiles(nrows)):
            xt = self.xt[ti % 2]
            P.dma("sp", ("xt", ti % 2), lambda e, xt=xt, r0=r0: e.dma_start(out=xt.ap, in_=src[r0:r0 + 128, :]), writes=[xt.k()])
            for c4 in range(4):
                b = self.bank()
                for j in range(4):
                    c = 4 * c4 + j
                    P.pe(lambda e, b=b, j=j, c=c, xt=xt: e.transpose(self.psb(b)[:, j * 128:(j + 1) * 128], xt.ap[:, c * 128:(c + 1) * 128], self.ident.ap),
                         reads=[xt.k(), self.ident.k()], writes=[self.pk(b)])
                o = dst[:, 4 * c4:4 * c4 + 4, r0:r0 + 128]
                i = self.psb(b).rearrange("p (j n) -> p j n", j=4)
                self.copy(self.evac_eng(), o, i, [self.pk(b)], [dkey(c) for c in range(4 * c4, 4 * c4 + 4)])

    def transpose_out(self, srcv, skey, nrows, dst):
        P = self.P
        for ti, r0 in enumerate(self.row_tiles(nrows)):
            yt = self.xt[ti % 2]
            for c4 in range(4):
                b = self.bank()
                for j in range(4):
                    c = 4 * c4 + j
                    P.pe(lambda e, b=b, j=j, c=c, r0=r0: e.transpose(self.psb(b)[:, j * 128:(j + 1) * 128], srcv[:, c, r0:r0 + 128], self.ident.ap),
                         reads=[skey(c), self.ident.k()], writes=[self.pk(b)])
                self.copy(self.evac_eng(), yt.ap[:, c4 * 512:(c4 + 1) * 512], self.psb(b), [self.pk(b)], [yt.k()])
            P.dma("sp", ("xt", ti % 2), lambda e, yt=yt, r0=r0: e.dma_start(out=dst[r0:r0 + 128, :], in_=yt.ap),
                  reads=[yt.k()], writes=[("dram", "y")])

    def stats(self, srcv, skey, n, nch, rstdb, denom):
        P = self.P
        tts = tt_list(n)
        bs = [self.bank() for _ in tts]
        for b in bs:
            self.reserved.add(b)
        for c in range(nch):
            sq = self.sq[c % 2]
            P.act(lambda e, sq=sq, c=c: e.activation(out=sq.ap[:, 0:n], in_=srcv[:, c, :], func=AF.Square),
                  reads=[skey(c)], writes=[sq.k()])
            for ti, (s, e_) in enumerate(tts):
                P.pe(lambda e, b=bs[ti], sq=sq, s=s, e_=e_, c=c: e.matmul(
                    self.psb(b)[:, 0:e_ - s], lhsT=self.onesb.ap, rhs=sq.ap[:, s:e_], start=(c == 0), stop=(c == nch - 1)),
                    reads=[sq.k(), self.onesb.k()], writes=[self.pk(bs[ti])])
        for ti, (s, e_) in enumerate(tts):
            b = bs[ti]
            P.act(lambda e, b=b, s=s, e_=e_: e.activation(out=rstdb.ap[:, s:e_], in_=self.psb(b)[:, 0:e_ - s], func=AF.Sqrt,
                                                          bias=self.epsb.ap[:, 0:1], scale=1.0 / denom),
                  reads=[self.pk(b), self.epsb.k()], writes=[rstdb.k()])
            self.reserved.discard(b)
        P.dve(lambda e: e.reciprocal(out=rstdb.ap[:, 0:n], in_=rstdb.ap[:, 0:n]), reads=[rstdb.k()], writes=[rstdb.k()])

    def normalize(self, srcv, skey, n, nch, rstdb, gi, dstv, dkey):
        for c in range(nch):
            self.P.dve(lambda e, c=c: e.scalar_tensor_tensor(out=dstv[:, c, :], in0=srcv[:, c, :], scalar=self.gT.ap[:, gi * 16 + c:gi * 16 + c + 1],
                                                             in1=rstdb.ap[:, 0:n], op0=ALU.mult, op1=ALU.mult),
                       reads=[skey(c), rstdb.k(), self.gT.k()], writes=[dkey(c)])

    def project(self, wfn, kcn, rhsfn, n_oc, tts, evac):
        for oc in range(n_oc):
            lfn, wkeys = wfn(oc)
            for ti, (s, e_) in enumerate(tts):
                b = self.bank()
                for kc in range(kcn):
                    rhs, rkeys = rhsfn(kc, s, e_)
                    self.P.pe(lambda e, b=b, lfn=lfn, kc=kc, rhs=rhs, w=e_ - s: e.matmul(
                        self.psb(b)[:, 0:w], lhsT=lfn(kc), rhs=rhs, start=(kc == 0), stop=(kc == kcn - 1)),
                        reads=list(wkeys) + list(rkeys), writes=[self.pk(b)])
                evac(oc, ti, s, e_, b)

    def ffn(self, xnv, xkey, n, Wg, Wu, Wd, fov, fkey):
        P = self.P
        tts = tt_list(n)
        HB = 22
        for half in range(2):
            hid = self.ar.alloc("hid", HB * n, BF16)
            hv = hid.ap.rearrange("p (c n) -> p c n", c=HB)
            for t in range(11):
                col0 = half * 2816 + t * 256
                wg, wgk = self.wload(Wg[:, col0:col0 + 256].rearrange("(c p) n -> p c n", p=128), 16, 256)
                wu, wuk = self.wload(Wu[:, col0:col0 + 256].rearrange("(c p) n -> p c n", p=128), 16, 256)
                for b2 in range(2):
                    blk = 2 * t + b2
                    for ti, (s, e_) in enumerate(tts):
                        w = e_ - s
                        bg, bu = self.bank(), self.bank()
                        for kc in range(16):
                            P.pe(lambda e, bg=bg, wg=wg, kc=kc, b2=b2, s=s, e_=e_, w=w: e.matmul(
                                self.psb(bg)[:, 0:w], lhsT=wg[:, kc, b2 * 128:(b2 + 1) * 128], rhs=xnv[:, kc, s:e_], start=(kc == 0), stop=(kc == 15)),
                                reads=[wgk, xkey(kc)], writes=[self.pk(bg)])
                        for kc in range(16):
                            P.pe(lambda e, bu=bu, wu=wu, kc=kc, b2=b2, s=s, e_=e_, w=w: e.matmul(
                                self.psb(bu)[:, 0:w], lhsT=wu[:, kc, b2 * 128:(b2 + 1) * 128], rhs=xnv[:, kc, s:e_], start=(kc == 0), stop=(kc == 15)),
                                reads=[wuk, xkey(kc)], writes=[self.pk(bu)])
                        tmp = self.tmp[self.ev % 2]
                        self.ev += 1
                        P.act(lambda e, bg=bg, tmp=tmp, w=w: e.activation(out=tmp.ap[:, 0:w], in_=self.psb(bg)[:, 0:w], func=AF.Silu),
                              reads=[self.pk(bg)], writes=[tmp.k()])
                        P.dve(lambda e, bu=bu, tmp=tmp, blk=blk, s=s, e_=e_, w=w: e.tensor_tensor(
                            out=hv[:, blk, s:e_], in0=tmp.ap[:, 0:w], in1=self.psb(bu)[:, 0:w], op=ALU.mult),
                            reads=[self.pk(bu), tmp.k()], writes=[hid.k(blk)])
            for t in range(8):
                wd, wdk = self.wload(Wd[half * 2816:(half + 1) * 2816, t * 256:(t + 1) * 256].rearrange("(c p) n -> p c n", p=128), HB, 256)
                for f2 in range(2):
                    f = 2 * t + f2
                    for ti, (s, e_) in enumerate(tts):
                        w = e_ - s
                        b = self.bank()
                        for kc in range(HB):
                            P.pe(lambda e, b=b, wd=wd, kc=kc, f2=f2, s=s, e_=e_, w=w: e.matmul(
                                self.psb(b)[:, 0:w], lhsT=wd[:, kc, f2 * 128:(f2 + 1) * 128], rhs=hv[:, kc, s:e_], start=(kc == 0), stop=(kc == HB - 1)),
                                reads=[wdk, hid.k(kc)], writes=[self.pk(b)])
                        if half == 0:
                            P.act(lambda e, b=b, f=f, s=s, e_=e_, w=w: e.copy(out=fov[:, f, s:e_], in_=self.psb(b)[:, 0:w]),
                                  reads=[self.pk(b)], writes=[fkey(f)])
                        else:
                            P.dve(lambda e, b=b, f=f, s=s, e_=e_, w=w: e.tensor_tensor(
                                out=fov[:, f, s:e_], in0=fov[:, f, s:e_], in1=self.psb(b)[:, 0:w], op=ALU.add),
                                reads=[self.pk(b), fkey(f)], writes=[fkey(f)])
            self.ar.free(hid)

    def epilogue(self, fo, n, gpost_col, resid_dram, resid_key, out_dram, out_key, gi_next, xn_next):
        P = self.P
        fov = fo.ap.rearrange("p (c n) -> p c n", c=16)
        fkey = lambda c: fo.k(c)
        rs1 = self.ar.alloc("rs1", n, F32)
        self.stats(fov, fkey, n, 16, rs1, float(D))
        for c in range(16):
            rt = self.rt[c % 2]
            P.dma("sp", ("rt", c % 2), lambda e, rt=rt, c=c: e.dma_start(out=rt.ap[:, 0:n], in_=resid_dram[:, c, 0:n]),
                  reads=[("dram", resid_key)], writes=[rt.k()])
            P.dve(lambda e, c=c: e.tensor_tensor(out=fov[:, c, :], in0=fov[:, c, :], in1=rs1.ap[:, 0:n], op=ALU.mult),
                  reads=[fkey(c), rs1.k()], writes=[fkey(c)])
            P.dve(lambda e, c=c, rt=rt: e.scalar_tensor_tensor(out=fov[:, c, :], in0=fov[:, c, :], scalar=self.gS.ap[:, gpost_col * 16 + c:gpost_col * 16 + c + 1],
                                                               in1=rt.ap[:, 0:n], op0=ALU.mult, op1=ALU.add),
                  reads=[fkey(c), rt.k(), self.gS.k()], writes=[fkey(c)])
            if out_dram is not None:
                P.dma("sp", ("hout", c % 2), lambda e, c=c: e.dma_start(out=out_dram[:, c, 0:n], in_=fov[:, c, :]),
                      reads=[fkey(c)], writes=[("dram", out_key)])
        self.ar.free(rs1)
        if xn_next is not None:
            rs2 = self.ar.alloc("rs2", n, F32)
            self.stats(fov, fkey, n, 16, rs2, float(D))
            xv = xn_next.ap.rearrange("p (c n) -> p c n", c=16)
            self.normalize(fov, fkey, n, 16, rs2, gi_next, xv, lambda c: xn_next.k(c))
            self.ar.free(rs2)

    def front(self, src, n, gi, xn, scr, scr_key):
        xT = self.ar.alloc("xT", 16 * n, F32)
        xTv = xT.ap.rearrange("p (c n) -> p c n", c=16)
        self.transpose_in(src, n, xTv, lambda c: xT.k(c))
        rs = self.ar.alloc("rs0", n, F32)
        self.stats(xTv, lambda c: xT.k(c), n, 16, rs, float(D))
        xv = xn.ap.rearrange("p (c n) -> p c n", c=16)
        self.normalize(xTv, lambda c: xT.k(c), n, 16, rs, gi, xv, lambda c: xn.k(c))
        for c in range(16 if scr is not None else 0):
            self.P.dma("sp", ("hout", c % 2), lambda e, c=c: e.dma_start(out=scr[:, c, 0:n], in_=xTv[:, c, :]),
                       reads=[xT.k(c)], writes=[("dram", scr_key)])
        self.ar.free(rs, xT)

    def win_proj(self, xn, n, col0, n_oc, evac, tts=None):
        xv = xn.ap.rearrange("p (c n) -> p c n", c=16)
        tts = tts or tt_list(n)
        for t in range(0, n_oc, 2):
            wv, wk = self.wload(self.win[:, col0 + t * 128:col0 + t * 128 + 256].rearrange("(c p) n -> p c n", p=128), 16, 256)
            self.project(lambda oc, wv=wv, wk=wk: ((lambda kc, oc=oc: wv[:, kc, oc * 128:(oc + 1) * 128]), [wk]),
                         16, lambda kc, s, e_: (xv[:, kc, s:e_], [xn.k(kc)]), 2, tts,
                         lambda oc, ti, s, e_, b, t=t: evac(t + oc, ti, s, e_, b))

    def setup_consts(self):
        P, ar = self.P, self.ar
        self.cst = ar.alloc("cst", CW, F32)
        P.dma("sp", "cst", lambda e: e.dma_start(out=self.cst.ap, in_=self.cst_d), writes=[self.cst.k()])
        self.ident = ar.alloc("ident", 128, F32)
        self.copy("dve", self.ident.ap, self.cst.ap[:, 0:128], [self.cst.k()], [self.ident.k()])
        self.identb = ar.alloc("identb", 128, BF16)
        self.copy("dve", self.identb.ap, self.cst.ap[:, 0:128], [self.cst.k()], [self.identb.k()])
        self.ones = ar.alloc("ones", 128, F32)
        P.dve(lambda e: e.memset(self.ones.ap, 1.0), writes=[self.ones.k()])
        self.onesb = ar.alloc("onesb", 128, BF16)
        P.dve(lambda e: e.memset(self.onesb.ap, 1.0), writes=[self.onesb.k()])
        self.zerob = ar.alloc("zerob", 128, BF16)
        P.dve(lambda e: e.memset(self.zerob.ap, 0.0), writes=[self.zerob.k()])
        self.epsb = ar.alloc("epsb", 1, F32)
        P.dve(lambda e: e.memset(self.epsb.ap, EPS), writes=[self.epsb.k()])
        self.gT = ar.alloc("gT", 7 * 16, F32)
        P.dma("sp", "gT", lambda e: e.dma_start(out=self.gT.ap.rearrange("p (g c) -> p g c", g=7),
                                                in_=self.gains_d.rearrange("g (c p) -> p g c", p=128), allow_slow_non_contiguous=True),
              writes=[self.gT.k()])
        self.gS = ar.alloc("gS", 3 * 16, F32)
        for k, (gi, sc) in enumerate(((1, 0.5), (3, 1.0), (5, 0.5))):
            P.dve(lambda e, k=k, gi=gi, sc=sc: e.tensor_scalar(out=self.gS.ap[:, k * 16:(k + 1) * 16], in0=self.gT.ap[:, gi * 16:(gi + 1) * 16],
                                                               scalar1=sc, scalar2=None, op0=ALU.mult),
                  reads=[self.gT.k()], writes=[self.gS.k()])
        self.pvec = ar.alloc("pvec", 16, F32)
        P.dma("sp", "pvec", lambda e: e.dma_start(out=self.pvec.ap[:, 0:8], in_=self.pscale_d.rearrange("(c p) -> p c", p=128),
                                                  allow_slow_non_contiguous=True), writes=[self.pvec.k()])
        P.dma("sp", "pvec", lambda e: e.dma_start(out=self.pvec.ap[:, 8:16], in_=self.ssmd_d.rearrange("(c p) -> p c", p=128),
                                                  allow_slow_non_contiguous=True), writes=[self.pvec.k()])

    def mem_kv(self):
        P, ar = self.P, self.ar
        n = 256
        mn = ar.alloc("mn", 16 * n, BF16)
        self.front(self.mem_d, n, 6, mn, None, None)
        mv = mn.ap.rearrange("p (c n) -> p c n", c=16)
        self.kT = ar.alloc("kT", 8 * 256, BF16)
        self.vb = ar.alloc("vb", 2 * 1024, BF16)
        kTv = self.kT.ap.rearrange("p (c n) -> p c n", c=8)
        vbv = self.vb.ap.rearrange("p (m f) -> p m f", m=2)
        ktm = ar.alloc("ktm", 2 * 1024, F32)
        ktmv = ktm.ap.rearrange("p (m f) -> p m f", m=2)
        kbf = ar.alloc("kbf", 2 * 1024, BF16)
        kbv = kbf.ap.rearrange("p (m f) -> p m f", m=2)
        for which, W, outd in ((0, self.wmk, self.mk_d), (1, self.wmv, self.mv_d)):
            for t in range(4):
                wv, wk = self.wload(W[:, t * 256:(t + 1) * 256].rearrange("(c p) n -> p c n", p=128), 16, 256)
                for mt in range(2):
                    b = self.bank()
                    for kc in range(16):
                        P.pe(lambda e, b=b, kc=kc, mt=mt, wv=wv: e.matmul(self.psb(b)[:, 0:256], lhsT=mv[:, kc, mt * 128:(mt + 1) * 128], rhs=wv[:, kc, :],
                                                                          start=(kc == 0), stop=(kc == 15)),
                             reads=[wk, mn.k(kc)], writes=[self.pk(b)])
                    self.copy("act", ktmv[:, mt, t * 256:(t + 1) * 256], self.psb(b)[:, 0:256], [self.pk(b)], [ktm.k()])
                    dst = (kbv if which == 0 else vbv)[:, mt, t * 256:(t + 1) * 256]
                    self.copy("dve", dst, ktmv[:, mt, t * 256:(t + 1) * 256], [ktm.k()], [kbf.k() if which == 0 else self.vb.k()])
            P.dma("sp", ("kv", which), lambda e, outd=outd: e.dma_start(out=outd.rearrange("(m p) f -> p m f", p=128), in_=ktmv),
                  reads=[ktm.k()], writes=[("dram", "mkv", which)])
            if which == 0:
                for c in range(8):
                    b = self.bank()
                    pb = self.psb(b)[:, 0:128].bitcast(BF16)
                    for mt in range(2):
                        P.pe(lambda e, pb=pb, c=c, mt=mt: e.transpose(pb[:, mt * 128:(mt + 1) * 128], kbv[:, mt, c * 128:(c + 1) * 128], self.identb.ap),
                             reads=[kbf.k(), self.identb.k()], writes=[self.pk(b)])
                    self.copy(self.evac_eng(), kTv[:, c, :], pb, [self.pk(b)], [self.kT.k()])
        ar.free(mn, ktm, kbf)

    def pool_branch(self, xn2):
        P, ar = self.P, self.ar
        L = 15 + NPR
        UPW = 15 + NM + 112
        UP = ar.alloc("UP", 8 * UPW, F32)
        UPv = UP.ap.rearrange("p (c n) -> p c n", c=8)
        P.dve(lambda e: e.memset(UPv[:, :, 15 + NM:UPW], 0.0), writes=[UP.k(c) for c in range(8)])
        ukey = lambda c: UP.k(c)

        def ev(oc, ti, s, e_, b):
            self.copy(self.evac_eng(), UPv[:, oc, 15 + s:15 + e_], self.psb(b)[:, 0:e_ - s], [self.pk(b)], [ukey(oc)])
        self.win_proj(xn2, NM, 0, 8, ev)
        self.copy("dve", UPv[:, :, 0:15], self.halo.ap.rearrange("p (c n) -> p c n", c=8), [self.halo.k()], [ukey(c) for c in range(8)])
        pp = ar.alloc("pp", 1024, F32)
        utm = ar.alloc("utm", 1024, F32)
        for dst, c0 in ((pp, 15 + NPR - 128), (utm, 15 + NPR)):
            for hb in range(2):
                b = self.bank()
                for j in range(4):
                    c = 4 * hb + j
                    P.pe(lambda e, b=b, j=j, c=c, c0=c0: e.transpose(self.psb(b)[:, j * 128:(j + 1) * 128], UPv[:, c, c0:c0 + 128], self.ident.ap),
                         reads=[ukey(c), self.ident.k()], writes=[self.pk(b)])
                self.copy(self.evac_eng(), dst.ap[:, hb * 512:(hb + 1) * 512], self.psb(b), [self.pk(b)], [dst.k()])
        P.dma("sp", "pp", lambda e: e.dma_start(out=self.poolp_d, in_=pp.ap[113:128, :]), reads=[pp.k()], writes=[("dram", "poolp")])
        P.dma("sp", "ps1", lambda e: e.dma_start(out=self.pools_d[:, 0:14, :], in_=self.spool_d[:, 1:15, :]), writes=[("dram", "pools")])
        P.dma("sp", "ps2", lambda e: e.dma_start(out=self.pools_d[:, 14, :], in_=utm.ap[0:16, :]), reads=[utm.k()], writes=[("dram", "pools2")])
        DF = ar.alloc("DF", 8 * NM, BF16)
        DFv = DF.ap.rearrange("p (c n) -> p c n", c=8)
        WA = ar.alloc("WA", 2 * L, F32)
        WB = ar.alloc("WB", 2 * L, F32)
        WAv = WA.ap.rearrange("p (c n) -> p c n", c=2)
        WBv = WB.ap.rearrange("p (c n) -> p c n", c=2)
        invc = self.cst.ap[:, 386:450].rearrange("p (k j) -> p k j", k=4)
        for k in range(4):
            w = 2 << k
            u2 = UPv[:, 2 * k:2 * k + 2, :]
            uk = [ukey(2 * k), ukey(2 * k + 1)]
            P.dve(lambda e, u2=u2: e.tensor_tensor(out=WAv[:, :, 1:L], in0=u2[:, :, 1:L], in1=u2[:, :, 0:L - 1], op=ALU.add), reads=uk, writes=[WA.k()])
            cur, curb = WAv, WA
            if k >= 1:
                P.dve(lambda e: e.tensor_tensor(out=WBv[:, :, 3:L], in0=WAv[:, :, 3:L], in1=WAv[:, :, 1:L - 2], op=ALU.add), reads=[WA.k()], writes=[WB.k()])
                cur, curb = WBv, WB
            if k >= 2:
                P.dve(lambda e: e.tensor_tensor(out=WAv[:, :, 7:L], in0=WBv[:, :, 7:L], in1=WBv[:, :, 3:L - 4], op=ALU.add), reads=[WB.k()], writes=[WA.k()])
                cur, curb = WAv, WA
            if k >= 3:
                P.dve(lambda e: e.tensor_tensor(out=WBv[:, :, 15:L], in0=WAv[:, :, 15:L], in1=WAv[:, :, 7:L - 8], op=ALU.add), reads=[WA.k()], writes=[WB.k()])
                cur, curb = WBv, WB
            dk = [DF.k(2 * k), DF.k(2 * k + 1)]
            P.dve(lambda e, cur=cur, u2=u2, k=k, w=w: e.scalar_tensor_tensor(out=DFv[:, 2 * k:2 * k + 2, 16:NPR], in0=cur[:, :, 31:L], scalar=1.0 / w,
                                                                             in1=u2[:, :, 31:L], op0=ALU.mult, op1=ALU.subtract),
                  reads=[curb.k()] + uk, writes=dk)
            t16 = self.t16
            P.dve(lambda e, cur=cur, k=k: e.tensor_tensor(out=t16.ap.rearrange("p (c n) -> p c n", c=2), in0=cur[:, :, 15:31],
                                                          in1=invc[:, k:k + 1, :].to_broadcast([128, 2, 16]), op=ALU.mult),
                  reads=[curb.k(), self.cst.k()], writes=[t16.k()])
            P.dve(lambda e, u2=u2, k=k: e.tensor_tensor(out=DFv[:, 2 * k:2 * k + 2, 0:16], in0=t16.ap.rearrange("p (c n) -> p c n", c=2),
                                                        in1=u2[:, :, 15:31], op=ALU.subtract),
                  reads=[t16.k()] + uk, writes=dk)
        ar.free(WA, WB)
        SPB = ar.alloc("SPB", 26 * 256, F32)
        bsum = ar.alloc("bsum", 1024, F32)
        P.dve(lambda e: e.memset(bsum.ap, 0.0), writes=[bsum.k()])
        off = 0
        for k in range(4):
            w = 2 << k
            r = w - 1
            v = SPB.ap[0:16, off * 256:(off + r) * 256].rearrange("p (r c) -> p r c", r=r)
            P.dma("sp", ("spb", k), lambda e, v=v, k=k, r=r: e.dma_start(out=v, in_=self.spool_d[:, 15 - r:15, k * 256:(k + 1) * 256]), writes=[SPB.k(k)])
            if r == 1:
                self.copy("dve", bsum.ap[0:16, k * 256:(k + 1) * 256], v[:, 0, :], [SPB.k(k)], [bsum.k()])
            else:
                P.dve(lambda e, v=v, k=k: e.tensor_reduce(out=bsum.ap[0:16, k * 256:(k + 1) * 256], in_=v.rearrange("p r c -> p c r"), axis=AX.X, op=ALU.add),
                      reads=[SPB.k(k)], writes=[bsum.k()])
            off += r
        P.dve(lambda e: e.tensor_tensor(out=bsum.ap[0:16, :], in0=bsum.ap[0:16, :], in1=utm.ap[0:16, :], op=ALU.add), reads=[bsum.k(), utm.k()], writes=[bsum.k()])
        for k in range(4):
            w = 2 << k
            P.dve(lambda e, k=k, w=w: e.scalar_tensor_tensor(out=bsum.ap[0:16, k * 256:(k + 1) * 256], in0=bsum.ap[0:16, k * 256:(k + 1) * 256], scalar=1.0 / w,
                                                             in1=utm.ap[0:16, k * 256:(k + 1) * 256], op0=ALU.mult, op1=ALU.subtract),
                  reads=[bsum.k(), utm.k()], writes=[bsum.k()])
        for hb in range(2):
            b = self.bank()
            for j in range(4):
                c = 4 * hb + j
                P.pe(lambda e, b=b, j=j, c=c: e.transpose(self.psb(b)[:, j * 128:(j + 1) * 128], bsum.ap[:, c * 128:(c + 1) * 128], self.ident.ap),
                     reads=[bsum.k(), self.ident.k()], writes=[self.pk(b)])
            self.copy("dve", DFv[:, 4 * hb:4 * hb + 4, NPR:NM], self.psb(b).rearrange("p (j n) -> p j n", j=4)[:, :, 0:16], [self.pk(b)],
                      [DF.k(c) for c in range(4 * hb, 4 * hb + 4)])
        ar.free(SPB, bsum, pp, utm, UP)
        Z = ar.alloc("Z", 8 * NM, BF16)
        Zv = Z.ap.rearrange("p (c n) -> p c n", c=8)
        tts = tt_list(NM)
        for k in range(4):
            wv, wk = self.wload(self.wgrp[k].rearrange("(c p) n -> p c n", p=128), 2, 256)

            def ev(oc, ti, s, e_, b, k=k):
                o = 2 * k + oc
                if self.evac_eng() == "act":
                    P.act(lambda e: e.mul(out=Zv[:, o, s:e_], in_=self.psb(b)[:, 0:e_ - s], mul=self.pvec.ap[:, o:o + 1]),
                          reads=[self.pk(b), self.pvec.k()], writes=[Z.k(o)])
                else:
                    P.dve(lambda e: e.tensor_scalar(out=Zv[:, o, s:e_], in0=self.psb(b)[:, 0:e_ - s], scalar1=self.pvec.ap[:, o:o + 1], scalar2=None, op0=ALU.mult),
                          reads=[self.pk(b), self.pvec.k()], writes=[Z.k(o)])
            self.project(lambda oc, wv=wv, wk=wk: ((lambda kc, oc=oc: wv[:, kc, oc * 128:(oc + 1) * 128]), [wk]),
                         2, lambda kc, s, e_, k=k: (DFv[:, 2 * k + kc, s:e_], [DF.k(2 * k + kc)]), 2, tts, ev)
        ar.free(DF)
        return Z

    def merge_all(self, xn2, Z, G, O, mb):
        P = self.P
        xv = xn2.ap.rearrange("p (c n) -> p c n", c=16)
        mv = mb.ap.rearrange("p (c n) -> p c n", c=16)
        tts = tt_list(NM)
        srcs = {0: Z, 1: G, 2: O}
        for t in range(8):
            tiles = {}
            for br, wA in ((1, self.wgv), (0, self.wpo), (2, self.wxo)):
                c0 = OFF_GATE + br * D + t * 256
                wg = self.wload(self.win[:, c0:c0 + 256].rearrange("(c p) n -> p c n", p=128), 16, 256)
                wa = self.wload(wA[:, t * 256:(t + 1) * 256].rearrange("(c p) n -> p c n", p=128), 8, 256)
                tiles[br] = (wg, wa)
                for f2 in range(2):
                    f = 2 * t + f2
                    if br == 1 and f2 == 0:
                        wb2 = self.wload(self.wgg[:, t * 256:(t + 1) * 256].rearrange("(c p) n -> p c n", p=128), 8, 256)
                    sv = srcs[br].ap.rearrange("p (c n) -> p c n", c=8)
                    for ti, (s, e_) in enumerate(tts):
                        w = e_ - s
                        acc = self.acc[ti]
                        bA = self.bank()
                        for kc in range(8):
                            P.pe(lambda e, bA=bA, kc=kc, wa=wa, f2=f2, s=s, e_=e_, w=w, sv=sv: e.matmul(
                                self.psb(bA)[:, 0:w], lhsT=wa[0][:, kc, f2 * 128:(f2 + 1) * 128], rhs=sv[:, kc, s:e_], start=(kc == 0), stop=(kc == 7)),
                                reads=[wa[1], srcs[br].k(kc)], writes=[self.pk(bA)])
                        if br == 1:
                            bB = self.bank()
                            for kc in range(8):
                                P.pe(lambda e, bB=bB, kc=kc, wb2=wb2, f2=f2, s=s, e_=e_, w=w, sv=sv: e.matmul(
                                    self.psb(bB)[:, 0:w], lhsT=wb2[0][:, kc, f2 * 128:(f2 + 1) * 128], rhs=sv[:, kc, s:e_], start=(kc == 0), stop=(kc == 7)),
                                    reads=[wb2[1], srcs[br].k(kc)], writes=[self.pk(bB)])
                        bG = self.bank()
                        for kc in range(16):
                            P.pe(lambda e, bG=bG, kc=kc, wg=wg, f2=f2, s=s, e_=e_, w=w: e.matmul(
                                self.psb(bG)[:, 0:w], lhsT=wg[0][:, kc, f2 * 128:(f2 + 1) * 128], rhs=xv[:, kc, s:e_], start=(kc == 0), stop=(kc == 15)),
                                reads=[wg[1], xn2.k(kc)], writes=[self.pk(bG)])
                        tg = self.tmp[self.ev % 2]
                        tb = self.tmp2[self.ev % 2]
                        self.ev += 1
                        P.act(lambda e, bG=bG, tg=tg, w=w: e.activation(out=tg.ap[:, 0:w], in_=self.psb(bG)[:, 0:w], func=AF.Sigmoid),
                              reads=[self.pk(bG)], writes=[tg.k()])
                        if br == 1:
                            P.act(lambda e, bB=bB, tb=tb, w=w: e.activation(out=tb.ap[:, 0:w], in_=self.psb(bB)[:, 0:w], func=AF.Sigmoid),
                                  reads=[self.pk(bB)], writes=[tb.k()])
                            P.dve(lambda e, bA=bA, tb=tb, w=w: e.tensor_tensor(out=tb.ap[:, 0:w], in0=tb.ap[:, 0:w], in1=self.psb(bA)[:, 0:w], op=ALU.mult),
                                  reads=[self.pk(bA), tb.k()], writes=[tb.k()])
                            P.dve(lambda e, tg=tg, tb=tb, acc=acc, f2=f2, w=w: e.tensor_tensor(out=acc.ap[:, f2 * 352:f2 * 352 + w], in0=tb.ap[:, 0:w], in1=tg.ap[:, 0:w], op=ALU.mult),
                                  reads=[tg.k(), tb.k()], writes=[acc.k(f2)])
                        else:
                            P.dve(lambda e, bA=bA, tg=tg, w=w: e.tensor_tensor(out=tg.ap[:, 0:w], in0=tg.ap[:, 0:w], in1=self.psb(bA)[:, 0:w], op=ALU.mult),
                                  reads=[self.pk(bA), tg.k()], writes=[tg.k()])
                            if br == 0:
                                P.dve(lambda e, tg=tg, acc=acc, f2=f2, w=w: e.tensor_tensor(out=acc.ap[:, f2 * 352:f2 * 352 + w], in0=acc.ap[:, f2 * 352:f2 * 352 + w], in1=tg.ap[:, 0:w], op=ALU.add),
                                      reads=[tg.k(), acc.k(f2)], writes=[acc.k(f2)])
                            else:
                                P.dve(lambda e, tg=tg, acc=acc, f=f, f2=f2, s=s, e_=e_, w=w: e.tensor_tensor(out=mv[:, f, s:e_], in0=acc.ap[:, f2 * 352:f2 * 352 + w], in1=tg.ap[:, 0:w], op=ALU.add),
                                      reads=[tg.k(), acc.k(f2)], writes=[mb.k(f)])

    def bank2(self):
        while True:
            b = self.rr % 8
            if b % 2 == 0 and b not in self.reserved and (b + 1) not in self.reserved:
                self.rr += 2
                return b
            self.rr += 1

    def attn_branch(self, xn2):
        P, ar = self.P, self.ar
        Q = ar.alloc("Q", 8 * NM, BF16)
        Qv = Q.ap.rearrange("p (c n) -> p c n", c=8)

        def ev(oc, ti, s, e_, b):
            if self.evac_eng() == "act":
                P.act(lambda e: e.mul(out=Qv[:, oc, s:e_], in_=self.psb(b)[:, 0:e_ - s], mul=XA_SCALE), reads=[self.pk(b)], writes=[Q.k(oc)])
            else:
                P.dve(lambda e: e.tensor_scalar(out=Qv[:, oc, s:e_], in0=self.psb(b)[:, 0:e_ - s], scalar1=XA_SCALE, scalar2=None, op0=ALU.mult),
                      reads=[self.pk(b)], writes=[Q.k(oc)])
        self.win_proj(xn2, NM, OFF_XA, 8, ev)
        O = ar.alloc("O", 8 * NM, BF16)
        Ov = O.ap.rearrange("p (c n) -> p c n", c=8)
        kTv = self.kT.ap.rearrange("p (c n) -> p c n", c=8)
        vbv = self.vb.ap.rearrange("p (m f) -> p m f", m=2)
        pT = ar.alloc("pT", 2 * 4 * 128, BF16)
        pTv = pT.ap.rearrange("p (m h n) -> p m h n", m=2, h=4)
        ex = ar.alloc("ex", 256, F32)
        pn = ar.alloc("pn", 256, BF16)
        st = ar.alloc("st", 16, F32)
        for tq in range(NPR // 128):
            t0 = tq * 128
            P.dve(lambda e: e.memset(st.ap[:, 8:12], 0.0), writes=[st.k()])
            for h in range(4):
                b = self.bank()
                for dc in range(2):
                    P.pe(lambda e, b=b, h=h, dc=dc, t0=t0: e.matmul(self.psb(b)[:, 0:256], lhsT=Qv[:, 2 * h + dc, t0:t0 + 128], rhs=kTv[:, 2 * h + dc, :],
                                                                    start=(dc == 0), stop=(dc == 1)),
                         reads=[Q.k(2 * h + dc), self.kT.k()], writes=[self.pk(b)])
                P.dve(lambda e, b=b, h=h: e.reduce_max(out=st.ap[:, h:h + 1], in_=self.psb(b)[:, 0:256], axis=AX.X), reads=[self.pk(b)], writes=[st.k()])
                P.dve(lambda e, h=h: e.tensor_scalar(out=st.ap[:, 4 + h:5 + h], in0=st.ap[:, h:h + 1], scalar1=-1.0, scalar2=None, op0=ALU.mult),
                      reads=[st.k()], writes=[st.k()])
                P.act(lambda e, b=b, h=h: e.activation(out=ex.ap, in_=self.psb(b)[:, 0:256], func=AF.Exp, bias=st.ap[:, 4 + h:5 + h], scale=1.0,
                                                       accum_out=st.ap[:, 8 + h:9 + h]),
                      reads=[self.pk(b), st.k()], writes=[ex.k(), st.k()])
                P.dve(lambda e, h=h: e.reciprocal(out=st.ap[:, 12 + h:13 + h], in_=st.ap[:, 8 + h:9 + h]), reads=[st.k()], writes=[st.k()])
                P.dve(lambda e, h=h: e.tensor_scalar(out=pn.ap, in0=ex.ap, scalar1=st.ap[:, 12 + h:13 + h], scalar2=None, op0=ALU.mult),
                      reads=[ex.k(), st.k()], writes=[pn.k()])
                b2 = self.bank()
                pb = self.psb(b2)[:, 0:128].bitcast(BF16)
                for mt in range(2):
                    P.pe(lambda e, pb=pb, mt=mt: e.transpose(pb[:, mt * 128:(mt + 1) * 128], pn.ap[:, mt * 128:(mt + 1) * 128], self.identb.ap),
                         reads=[pn.k(), self.identb.k()], writes=[self.pk(b2)])
                self.copy("act", pTv[:, :, h, :], pb.rearrange("p (m n) -> p m n", m=2), [self.pk(b2)], [pT.k()])
            for g in range(2):
                b = self.bank()
                for j in range(4):
                    c = 4 * g + j
                    for mt in range(2):
                        P.pe(lambda e, b=b, j=j, c=c, mt=mt: e.matmul(self.psb(b)[:, j * 128:(j + 1) * 128], lhsT=vbv[:, mt, c * 128:(c + 1) * 128],
                                                                      rhs=pTv[:, mt, c // 2, :], start=(mt == 0), stop=(mt == 1)),
                             reads=[self.vb.k(), pT.k()], writes=[self.pk(b)])
                self.copy(self.evac_eng(), Ov[:, 4 * g:4 * g + 4, t0:t0 + 128], self.psb(b).rearrange("p (j n) -> p j n", j=4), [self.pk(b)],
                          [O.k(c) for c in range(4 * g, 4 * g + 4)])
        ar.free(pT, ex, pn, st)
        qtm = ar.alloc("qtm", 1024, BF16)
        b = self.bank()
        pb = self.psb(b).bitcast(BF16)
        for c in range(8):
            P.pe(lambda e, pb=pb, c=c: e.transpose(pb[0:16, c * 128:(c + 1) * 128], Qv[:, c, NPR:NM], self.identb.ap),
                 reads=[Q.k(c), self.identb.k()], writes=[self.pk(b)])
        self.copy("dve", qtm.ap[0:16, :], pb[0:16, :], [self.pk(b)], [qtm.k()])
        sel = ar.alloc("sel", 16 * 128, BF16)
        selv = sel.ap.rearrange("p (s n) -> p s n", s=16)
        self.copy("dve", selv[0:16], self.ident.ap[0:16, 0:16].unsqueeze(2).to_broadcast([16, 16, 128]), [self.ident.k()], [sel.k()])
        KS = [ar.alloc("KS%d" % i, 2048, F32) for i in range(2)]
        prod = ar.alloc("prod", 1024, F32)
        sc = ar.alloc("sc", 128, F32)
        scv = sc.ap.rearrange("p (s m h) -> p s m h", s=16, m=2)
        for s_ in range(NS):
            ks = KS[s_ % 2]
            ksv = ks.ap.rearrange("p (m f) -> p m f", m=2)
            P.dma("sp", ("ks", s_ % 2), lambda e, ksv=ksv, s_=s_: e.dma_start(out=ksv, in_=self.ck_d[s_].rearrange("(m p) f -> p m f", p=128)), writes=[ks.k()])
            bq = self.bank2()
            for k in range(2):
                P.pe(lambda e, bq=bq, k=k, s_=s_: e.matmul(self.psb(bq + k), lhsT=selv[0:16, s_, :], rhs=qtm.ap[0:16, k * 512:(k + 1) * 512], start=True, stop=True),
                     reads=[sel.k(), qtm.k()], writes=[self.pk(bq + k)])
            for mt in range(2):
                for k in range(2):
                    P.dve(lambda e, ksv=ksv, mt=mt, k=k, bq=bq: e.tensor_tensor(out=prod.ap[:, k * 512:(k + 1) * 512], in0=ksv[:, mt, k * 512:(k + 1) * 512],
                                                                                 in1=self.psb(bq + k), op=ALU.mult),
                          reads=[ks.k(), self.pk(bq + k)], writes=[prod.k()])
                P.dve(lambda e, s_=s_, mt=mt: e.tensor_reduce(out=scv[:, s_, mt, :], in_=prod.ap.rearrange("p (h d) -> p h d", h=4), axis=AX.X, op=ALU.add),
                      reads=[prod.k()], writes=[sc.k()])
        ar.free(KS[0], KS[1], prod)
        VS = [ar.alloc("VS%d" % i, 2048, BF16) for i in range(2)]
        scm = ar.alloc("scm", 128, F32)
        P.dve(lambda e: e.memset(scm.ap, 0.0), writes=[scm.k()])
        scmv = scm.ap[:, 0:64].rearrange("p (s h) -> p s h", s=16)
        P.dve(lambda e: e.tensor_tensor(out=scmv, in0=scv[:, :, 0, :], in1=scv[:, :, 1, :], op=ALU.max), reads=[sc.k()], writes=[scm.k()])
        b = self.bank()
        P.pe(lambda e, b=b: e.transpose(self.psb(b)[:, 0:128], scm.ap, self.ident.ap), reads=[scm.k(), self.ident.k()], writes=[self.pk(b)])
        mxT = ar.alloc("mxT", 1, F32)
        P.dve(lambda e, b=b: e.reduce_max(out=mxT.ap, in_=self.psb(b)[:, 0:128], axis=AX.X), reads=[self.pk(b)], writes=[mxT.k()])
        dg = ar.alloc("dg", 64, F32)
        P.dve(lambda e: e.tensor_scalar(out=dg.ap, in0=self.ident.ap[:, 0:64], scalar1=mxT.ap[:, 0:1], scalar2=None, op0=ALU.mult),
              reads=[mxT.k(), self.ident.k()], writes=[dg.k()])
        b = self.bank()
        P.pe(lambda e, b=b: e.matmul(self.psb(b)[:, 0:64], lhsT=self.ones.ap, rhs=dg.ap, start=True, stop=True),
             reads=[dg.k(), self.ones.k()], writes=[self.pk(b)])
        self.copy("dve", scm.ap[:, 0:64], self.psb(b)[:, 0:64], [self.pk(b)], [scm.k()])
        P.dve(lambda e: e.tensor_tensor(out=scv, in0=scv, in1=scmv.unsqueeze(2).to_broadcast([128, 16, 2, 4]), op=ALU.subtract),
              reads=[sc.k(), scm.k()], writes=[sc.k()])
        P.act(lambda e: e.activation(out=sc.ap, in_=sc.ap, func=AF.Exp), reads=[sc.k()], writes=[sc.k()])
        b = self.bank()
        P.pe(lambda e, b=b: e.matmul(self.psb(b)[:, 0:128], lhsT=self.ones.ap, rhs=sc.ap, start=True, stop=True), reads=[sc.k(), self.ones.k()], writes=[self.pk(b)])
        cs = ar.alloc("cs", 128, F32)
        csv = cs.ap.rearrange("p (s m h) -> p s m h", s=16, m=2)
        self.copy("dve", cs.ap, self.psb(b)[:, 0:128], [self.pk(b)], [cs.k()])
        P.dve(lambda e: e.tensor_tensor(out=scmv, in0=csv[:, :, 0, :], in1=csv[:, :, 1, :], op=ALU.add), reads=[cs.k()], writes=[scm.k()])
        P.dve(lambda e: e.reciprocal(out=scm.ap[:, 0:64], in_=scm.ap[:, 0:64]), reads=[scm.k()], writes=[scm.k()])
        P.dve(lambda e: e.tensor_tensor(out=scv, in0=scv, in1=scmv.unsqueeze(2).to_broadcast([128, 16, 2, 4]), op=ALU.mult),
              reads=[sc.k(), scm.k()], writes=[sc.k()])
        pnz = ar.alloc("pnz", 16 * 8 * 16, BF16)
        pnzv = pnz.ap.rearrange("p (s g j) -> p s g j", s=16, g=8)
        eye16 = self.cst.ap[:, 128:384].rearrange("p (s j) -> p s j", s=16)
        P.dve(lambda e: e.tensor_tensor(out=pnzv, in0=sc.ap.rearrange("p (s g) -> p s g", s=16).unsqueeze(3).to_broadcast([128, 16, 8, 16]),
                                        in1=eye16.unsqueeze(2).to_broadcast([128, 16, 8, 16]), op=ALU.mult),
              reads=[sc.k(), self.cst.k()], writes=[pnz.k()])
        bo = self.bank2()
        self.reserved.add(bo)
        self.reserved.add(bo + 1)
        for k in range(2):
            P.pe(lambda e, k=k: e.matmul(self.ps[0:16, (bo + k) * 512:(bo + k + 1) * 512], lhsT=self.zerob.ap[:, 0:16], rhs=Ov[:, 0, 0:512], start=True, stop=False),
                 reads=[self.zerob.k(), O.k(0)], writes=[self.pk(bo + k)])
        for s_ in range(NS):
            vs = VS[s_ % 2]
            vsv = vs.ap.rearrange("p (m f) -> p m f", m=2)
            P.dma("pool", ("vs", s_ % 2), lambda e, vsv=vsv, s_=s_: e.dma_start(out=vsv, in_=self.cv_d[s_].rearrange("(m p) f -> p m f", p=128)), writes=[vs.k()])
            for mt in range(2):
                for h in range(4):
                    P.pe(lambda e, s_=s_, mt=mt, h=h, vsv=vsv: e.matmul(
                        self.ps[0:16, bo * 512 + h * 256:bo * 512 + (h + 1) * 256], lhsT=pnzv[:, s_, mt * 4 + h, :], rhs=vsv[:, mt, h * 256:(h + 1) * 256],
                        start=False, stop=(s_ == NS - 1 and mt == 1)),
                        reads=[pnz.k(), vs.k()], writes=[self.pk(bo), self.pk(bo + 1)])
        otm = ar.alloc("otm", 1024, F32)
        P.dve(lambda e: e.memset(otm.ap, 0.0), writes=[otm.k()])
        for k in range(2):
            self.copy("dve", otm.ap[0:16, k * 512:(k + 1) * 512], self.ps[0:16, (bo + k) * 512:(bo + k + 1) * 512], [self.pk(bo + k)], [otm.k()])
        self.reserved.discard(bo)
        self.reserved.discard(bo + 1)
        for hb in range(2):
            b = self.bank()
            for j in range(4):
                c = 4 * hb + j
                P.pe(lambda e, b=b, j=j, c=c: e.transpose(self.psb(b)[:, j * 128:(j + 1) * 128], otm.ap[:, c * 128:(c + 1) * 128], self.ident.ap),
                     reads=[otm.k(), self.ident.k()], writes=[self.pk(b)])
            self.copy("dve", Ov[:, 4 * hb:4 * hb + 4, NPR:NM], self.psb(b).rearrange("p (j n) -> p j n", j=4)[:, :, 0:16], [self.pk(b)],
                      [O.k(c) for c in range(4 * hb, 4 * hb + 4)])
        ar.free(qtm, sel, VS[0], VS[1], sc, scm, mxT, dg, cs, pnz, Q, otm)
        return O

    def ssm_setup(self):
        P, ar = self.P, self.ar
        NSL = 64
        sm = ar.alloc("ssmsm", NSL * 32, F32)
        self.sm = sm
        smv = sm.ap.rearrange("p (i c) -> p i c", i=NSL)
        smi = ar.alloc("ssmi", 32, I32)
        K = [sm.k()]
        sl = lambda i: smv[:, i, :]
        (ARE, AIM, DT, MAG, ANG, T1, T2, T3, SIN, COS, LBRE, LBIM, DEN, NRE, FRE, FIM, RHO8, E8RE, E8IM) = range(19)
        PW = 20
        EM = 38
        XIN = 52
        self.SL = dict(sl=sl, PW=PW, EM=EM, XIN=XIN, INIT=54, XFIN=56, RHO8=RHO8, LBRE=LBRE, LBIM=LBIM, FRE=FRE, FIM=FIM, T1=T1, T2=T2, T3=T3)
        for g2 in range(2):
            ps_ = slice(g2 * 64, (g2 + 1) * 64)
            P.dma("sp", "ssm_a", lambda e, ps_=ps_, g2=g2: e.dma_start(out=smv[ps_, ARE, :], in_=self.are_d.rearrange("(pr g) n -> g n pr", g=2)[g2],
                                                                       allow_slow_non_contiguous=True), writes=K)
            P.dma("sp", "ssm_a", lambda e, ps_=ps_, g2=g2: e.dma_start(out=smv[ps_, AIM, :], in_=self.aim_d.rearrange("(pr g) n -> g n pr", g=2)[g2],
                                                                       allow_slow_non_contiguous=True), writes=K)
            P.dma("sp", "ssm_a", lambda e, ps_=ps_, g2=g2: e.dma_start(out=smv[ps_, DT, :], in_=self.lstep_d.rearrange("(pr g) -> g pr", g=2)[g2:g2 + 1, :].to_broadcast([64, 32]),
                                                                       allow_slow_non_contiguous=True), writes=K)

        def tt(o, a, b, op):
            P.dve(lambda e: e.tensor_tensor(out=sl(o), in0=sl(a), in1=sl(b), op=op), reads=K, writes=K)

        def ts(o, a, s1, op0, s2=None, op1=None):
            if op1 is None:
                P.dve(lambda e: e.tensor_scalar(out=sl(o), in0=sl(a), scalar1=s1, scalar2=None, op0=op0), reads=K, writes=K)
            else:
                P.dve(lambda e: e.tensor_scalar(out=sl(o), in0=sl(a), scalar1=s1, scalar2=s2, op0=op0, op1=op1), reads=K, writes=K)

        def act(o, a, f, scale=1.0):
            P.act(lambda e: e.activation(out=sl(o), in_=sl(a), func=f, scale=scale), reads=K, writes=K)

        def cmul(ore, oim, are_, aim_, bre_, bim_):
            tt(T1, are_, bre_, ALU.mult)
            tt(T2, aim_, bim_, ALU.mult)
            tt(T3, are_, bim_, ALU.mult)
            tt(ore, T1, T2, ALU.subtract)
            tt(T1, aim_, bre_, ALU.mult)
            tt(oim, T3, T1, ALU.add)
        self.cmul_small = cmul
        self.tt_small, self.ts_small = tt, ts

        X_, N_, R_, ACC, P2, TB = 58, 59, 60, 61, 62, 63

        def stt(o, a, sc, op0, b_, op1):
            P.dve(lambda e: e.scalar_tensor_tensor(out=sl(o), in0=sl(a), scalar=sc, in1=sl(b_), op0=op0, op1=op1), reads=K, writes=K)

        def rnd(o, a):
            P.dve(lambda e: e.tensor_copy(out=smi.ap, in_=sl(a)), reads=K, writes=[smi.k()])
            P.dve(lambda e: e.tensor_copy(out=sl(o), in_=smi.ap), reads=[smi.k()], writes=K)

        def exp_acc(o, a, scale, reduce):
            ts(X_, a, scale, ALU.mult)
            if reduce:
                ts(T1, X_, 1.0 / math.log(2.0), ALU.mult)
                rnd(N_, T1)
                stt(R_, N_, -0.693359375, ALU.mult, X_, ALU.add)
                stt(R_, N_, 2.12194440e-4, ALU.mult, R_, ALU.add)
                ts(T1, N_, -1.0, ALU.mult, 0.0, ALU.max)
                ts(T1, T1, 31.0, ALU.min)
                P.dve(lambda e: e.memset(sl(P2), 1.0), writes=K)
                for kb in (4, 3, 2, 1, 0):
                    ts(TB, T1, -float((1 << kb) - 1), ALU.add, 0.0, ALU.max)
                    ts(TB, TB, 1.0, ALU.min)
                    stt(T1, TB, -float(1 << kb), ALU.mult, T1, ALU.add)
                    ts(TB, TB, -(1.0 - 2.0 ** (-(1 << kb))), ALU.mult, 1.0, ALU.add)
                    tt(P2, P2, TB, ALU.mult)
                rr = R_
            else:
                rr = X_
            ts(ACC, rr, 1.0 / math.factorial(9), ALU.mult)
            for k in range(8, 0, -1):
                stt(ACC, ACC, 1.0 / math.factorial(k), ALU.add, rr, ALU.mult)
            ts(ACC, ACC, 1.0, ALU.add)
            if reduce:
                tt(o, ACC, P2, ALU.mult)
            else:
                self.copy("dve", sl(o), sl(ACC), K, K)

        def sincos_acc(os_, oc_, a):
            ts(T1, a, 1.0 / (2 * math.pi), ALU.mult)
            rnd(N_, T1)
            stt(R_, N_, -6.28125, ALU.mult, a, ALU.add)
            stt(R_, N_, -0.0019353071795864769, ALU.mult, R_, ALU.add)
            ts(R_, R_, 0.125, ALU.mult)
            tt(X_, R_, R_, ALU.mult)
            ts(ACC, X_, 1.0 / 362880.0, ALU.mult)
            for c_ in (-1.0 / 5040.0, 1.0 / 120.0, -1.0 / 6.0):
                stt(ACC, ACC, c_, ALU.add, X_, ALU.mult)
            ts(ACC, ACC, 1.0, ALU.add)
            tt(os_, ACC, R_, ALU.mult)
            ts(ACC, X_, -1.0 / 3628800.0, ALU.mult)
            for c_ in (1.0 / 40320.0, -1.0 / 720.0, 1.0 / 24.0, -0.5):
                stt(ACC, ACC, c_, ALU.add, X_, ALU.mult)
            ts(oc_, ACC, 1.0, ALU.add)
            for _ in range(3):
                tt(T1, os_, oc_, ALU.mult)
                tt(T2, os_, os_, ALU.mult)
                ts(os_, T1, 2.0, ALU.mult)
                ts(oc_, T2, -2.0, ALU.mult, 1.0, ALU.add)

        exp_acc(DT, DT, 1.0, True)
        tt(T3, ARE, DT, ALU.mult)
        self.copy("dve", sl(ANG), sl(T3), K, K)
        exp_acc(MAG, ANG, 1.0, False)
        exp_acc(RHO8, ANG, 8.0, False)
        tt(ANG, AIM, DT, ALU.mult)
        sincos_acc(SIN, COS, ANG)
        tt(LBRE, MAG, COS, ALU.mult)
        tt(LBIM, MAG, SIN, ALU.mult)
        tt(T1, ARE, ARE, ALU.mult)
        tt(T2, AIM, AIM, ALU.mult)
        tt(DEN, T1, T2, ALU.add)
        P.dve(lambda e: e.reciprocal(out=sl(DEN), in_=sl(DEN)), reads=K, writes=K)
        ts(NRE, LBRE, -1.0, ALU.add)
        tt(T1, NRE, ARE, ALU.mult)
        tt(T2, LBIM, AIM, ALU.mult)
        tt(T1, T1, T2, ALU.add)
        tt(FRE, T1, DEN, ALU.mult)
        tt(T1, LBIM, ARE, ALU.mult)
        tt(T2, NRE, AIM, ALU.mult)
        tt(T1, T1, T2, ALU.subtract)
        tt(FIM, T1, DEN, ALU.mult)
        P.dve(lambda e: e.memset(sl(PW), 1.0), writes=K)
        P.dve(lambda e: e.memset(sl(PW + 1), 0.0), writes=K)
        self.copy("dve", sl(PW + 2), sl(LBRE), K, K)
        self.copy("dve", sl(PW + 3), sl(LBIM), K, K)
        for k in range(2, 9):
            cmul(PW + 2 * k, PW + 2 * k + 1, PW + 2 * (k - 1), PW + 2 * (k - 1) + 1, LBRE, LBIM)
        P.dve(lambda e: e.reciprocal(out=sl(T3), in_=sl(RHO8)), reads=K, writes=K)
        tt(E8RE, PW + 16, T3, ALU.mult)
        tt(E8IM, PW + 17, T3, ALU.mult)
        self.copy("dve", sl(EM), sl(E8RE), K, K)
        self.copy("dve", sl(EM + 1), sl(E8IM), K, K)
        for m in range(1, 7):
            cmul(EM + 2 * m, EM + 2 * m + 1, EM + 2 * (m - 1), EM + 2 * (m - 1) + 1, EM + 2 * (m - 1), EM + 2 * (m - 1) + 1)
        P.dve(lambda e: e.memset(smv[:, XIN:XIN + 2, :], 0.0), writes=K)
        BA = ar.alloc("BA", 2 * 512, F32)
        BAv = BA.ap.rearrange("p (r pr h) -> p r pr h", r=2, pr=32)
        for r, src in ((0, self.bre_d), (1, self.bim_d)):
            for g2 in range(2):
                P.dma("sp", "ssm_b", lambda e, r=r, g2=g2, src=src: e.dma_start(out=BAv[g2 * 64:(g2 + 1) * 64, r, :, :],
                                                                                in_=src.rearrange("(pr g) n h -> g n pr h", g=2)[g2]), writes=[BA.k()])
        self.BB = ar.alloc("BB", 2 * 512, F32)
        BBv = self.BB.ap.rearrange("p (r pr h) -> p r pr h", r=2, pr=32)
        ZT1 = ar.alloc("ZT1", 512, F32)
        ZT2 = ar.alloc("ZT2", 512, F32)
        z1 = ZT1.ap.rearrange("p (pr h) -> p pr h", pr=32)
        z2 = ZT2.ap.rearrange("p (pr h) -> p pr h", pr=32)
        bc = lambda i: sl(i).unsqueeze(2).to_broadcast([128, 32, 16])
        for (o, x0, s0, x1, s1, op) in ((BBv[:, 0], BAv[:, 0], FRE, BAv[:, 1], FIM, ALU.subtract), (BBv[:, 1], BAv[:, 0], FIM, BAv[:, 1], FRE, ALU.add)):
            P.dve(lambda e, x0=x0, s0=s0: e.tensor_tensor(out=z1, in0=x0, in1=bc(s0), op=ALU.mult), reads=[BA.k()] + K, writes=[ZT1.k()])
            P.dve(lambda e, x1=x1, s1=s1: e.tensor_tensor(out=z2, in0=x1, in1=bc(s1), op=ALU.mult), reads=[BA.k()] + K, writes=[ZT2.k()])
            P.dve(lambda e, o=o, op=op: e.tensor_tensor(out=o, in0=z1, in1=z2, op=op), reads=[ZT1.k(), ZT2.k()], writes=[self.BB.k()])
        ar.free(BA, ZT1, ZT2, smi)
        self.Cm = ar.alloc("Cm", 2 * 32 * 32, F32)
        CN = ar.alloc("CN", 8 * 64, F32)
        CX = ar.alloc("CX", 8 * 128, F32)
        CNv = CN.ap.rearrange("p (f n) -> p f n", f=8)
        CXv = CX.ap.rearrange("p (f g n) -> p f g n", f=8, g=2)
        for r, src in ((0, self.cre_d), (1, self.cim_d)):
            P.dma("sp", "ssm_c", lambda e, src=src: e.dma_start(out=CNv, in_=src.rearrange("(f g8) h n -> (g8 h) f n", g8=8)), writes=[CN.k()])
            for g2 in range(2):
                P.dve(lambda e, r=r, g2=g2: e.tensor_scalar(out=CXv[:, :, g2, :], in0=CNv, scalar1=self.cst.ap[:, 451 + g2:452 + g2],
                                                            scalar2=(1.0 if r == 0 else -1.0), op0=ALU.mult, op1=ALU.mult),
                      reads=[CN.k(), self.cst.k()], writes=[CX.k()])
            for hb in range(2):
                b = self.bank()
                for j in range(4):
                    fcx = 4 * hb + j
                    P.pe(lambda e, b=b, j=j, fcx=fcx: e.transpose(self.psb(b)[:, j * 128:(j + 1) * 128], CXv[:, fcx].rearrange("p g n -> p (g n)"), self.ident.ap),
                         reads=[CX.k(), self.ident.k()], writes=[self.pk(b)])
                self.copy(self.evac_eng(), self.Cm.ap[:, r * 1024 + hb * 512:r * 1024 + (hb + 1) * 512], self.psb(b), [self.pk(b)], [self.Cm.k()])
        ar.free(CN, CX)

    def ssm_run(self, U, main):
        SW = int(os.environ.get('MK_SSM', '127'))
        P, ar = self.P, self.ar
        S = self.SL
        sl, PW, XIN, INIT, XFIN, RHO8, EM = S["sl"], S["PW"], S["XIN"], S["INIT"], S["XFIN"], S["RHO8"], S["EM"]
        K = [self.sm.k()]
        smv = self.sm.ap.rearrange("p (i c) -> p i c", i=64)
        Uv = U.ap.rearrange("p (c n) -> p c n", c=8)
        G, Gv = U, Uv
        BBv = self.BB.ap.rearrange("p (r pr h) -> p r pr h", r=2, pr=32)
        Cmv = self.Cm.ap.rearrange("p (r pr g h) -> p r pr (g h)", r=2, pr=32, g=2)
        self.cmul_small(INIT, INIT + 1, XIN, XIN + 1, EM, EM + 1)
        if main:
            HS = ar.alloc("HS", 32 * 2 * 16, F32)
            HSv = HS.ap.rearrange("p (pr r s) -> p pr r s", pr=32, r=2)
            HSb = ar.alloc("HSb", 32 * 2 * 16, BF16)
            HSbv = HSb.ap.rearrange("p (pr r s) -> p pr r s", pr=32, r=2)
            BUs = ar.alloc("BUs", 32 * 2 * 16, F32)
            BUv = BUs.ap.rearrange("p (pr r s) -> p pr r s", pr=32, r=2)
            stm = ar.alloc("stm", 4096, F32)
            P.dve(lambda e: e.memset(stm.ap, 0.0), writes=[stm.k()])
            for r, src in ((0, self.sre_d), (1, self.sim_d)):
                P.dma("sp", "stm", lambda e, src=src: e.dma_start(out=stm.ap[0:16, :], in_=src), writes=[stm.k()])
                for g in range(8):
                    b = self.bank()
                    for j in range(4):
                        pr = 4 * g + j
                        P.pe(lambda e, b=b, j=j, pr=pr: e.transpose(self.psb(b)[:, j * 128:(j + 1) * 128], stm.ap[:, pr * 128:(pr + 1) * 128], self.ident.ap),
                             reads=[stm.k(), self.ident.k()], writes=[self.pk(b)])
                    self.copy(self.evac_eng(), HSv[:, 4 * g:4 * g + 4, r, :], self.psb(b).rearrange("p (j n) -> p j n", j=4)[:, :, 0:16], [self.pk(b)], [HS.k()])
            self.copy("dve", HSb.ap, HS.ap, [HS.k()], [HSb.k()])
            ar.free(stm)
        Zf = ar.alloc("Zf", 8 * 2 * 64, F32)
        Zfv = Zf.ap.rearrange("p (t r q h) -> p t r q h", t=8, r=2, q=4)
        CT1 = ar.alloc("CT1", 4 * 8 * 32, F32)
        CT2 = ar.alloc("CT2", 4 * 8 * 32, F32)
        ZT1, ZT2 = CT1, CT2
        z1 = ZT1.ap[:, 0:512].rearrange("p (t q h) -> p t q h", t=8, q=4)
        z2 = ZT2.ap[:, 0:512].rearrange("p (t q h) -> p t q h", t=8, q=4)
        zm = ar.alloc("Zm", 8 * 2 * 128, F32)
        zmb = ar.alloc("Zmb", 8 * 2 * 128, BF16)
        Cmb = ar.alloc("Cmb", 2 * 32 * 32, BF16)
        self.copy("dve", Cmb.ap, self.Cm.ap, [self.Cm.k()], [Cmb.k()])
        Cmbv = Cmb.ap.rearrange("p (r pr gh) -> p r pr gh", r=2, pr=32)
        ma = ar.alloc("MA", 8 * 2 * 128, BF16)
        Rb = [ar.alloc("Rtab%d" % i, 2 * 4 * 128, F32) for i in range(2)]
        PT1 = ar.alloc("PT1", 4 * 64, F32)
        PT2 = ar.alloc("PT2", 4 * 64, F32)
        Ssb = ar.alloc("Ssb", 4 * 2 * 128, F32)
        BP = ar.alloc("BP", 4 * 2 * 128, F32)
        Wt = ar.alloc("Wt", 4 * 2 * 128, F32)
        Xt = Ssb
        RT1 = ar.alloc("ST1", 4 * 128, F32)
        RT2 = ar.alloc("ST2", 4 * 128, F32)
        v4 = lambda b: b.ap.rearrange("p (q r c) -> p q r c", q=4, r=2)
        Ssv, BPv, Wv, Xv = v4(Ssb), v4(BP), v4(Wt), v4(Xt)
        r1 = RT1.ap.rearrange("p (q c) -> p q c", q=4)
        r2 = RT2.ap.rearrange("p (q c) -> p q c", q=4)
        if main:
            Xp = ar.alloc("Xp", 4 * 2 * 128, BF16)
            Xpv = v4(Xp)
            KT = ar.alloc("KT", 8 * 128, BF16)
            KTv = KT.ap.rearrange("p (t c) -> p t c", t=8)
            CL = ar.alloc("CL", 4 * 8 * 2 * 32, BF16)
            CLv = CL.ap.rearrange("p (q j r h) -> p q j r h", q=4, j=8, r=2)
            c1 = CT1.ap.rearrange("p (q j h) -> p q j h", q=4, j=8)
            c2 = CT2.ap.rearrange("p (q j h) -> p q j h", q=4, j=8)
            Yt = ar.alloc("Yt", NM, F32)
        for fc in range(8):
            prs = slice(4 * fc, 4 * fc + 4)
            pwre = smv[:, PW:PW + 16:2, prs].unsqueeze(3).to_broadcast([128, 8, 4, 16])
            pwim = smv[:, PW + 1:PW + 17:2, prs].unsqueeze(3).to_broadcast([128, 8, 4, 16])
            bre = BBv[:, 0, prs, :].unsqueeze(1).to_broadcast([128, 8, 4, 16])
            bim = BBv[:, 1, prs, :].unsqueeze(1).to_broadcast([128, 8, 4, 16])
            zk = [self.BB.k()] + K
            for (o, p0, b0, p1, b1, op) in ((Zfv[:, :, 0], pwre, bre, pwim, bim, ALU.subtract), (Zfv[:, :, 1], pwre, bim, pwim, bre, ALU.add)):
                P.dve(lambda e, p0=p0, b0=b0: e.tensor_tensor(out=z1, in0=p0, in1=b0, op=ALU.mult), reads=zk, writes=[ZT1.k()])
                P.dve(lambda e, p1=p1, b1=b1: e.tensor_tensor(out=z2, in0=p1, in1=b1, op=ALU.mult), reads=zk, writes=[ZT2.k()])
                P.dve(lambda e, o=o, op=op: e.tensor_tensor(out=o, in0=z1, in1=z2, op=op), reads=[ZT1.k(), ZT2.k()], writes=[Zf.k()])
            zmv = zm.ap.rearrange("p (t r q g h) -> p t r q g h", t=8, r=2, q=4, g=2)
            mav = ma.ap.rearrange("p (t r c) -> p t r c", t=8, r=2)
            for g2 in range(2):
                P.act(lambda e, g2=g2: e.mul(out=zmb.ap.rearrange("p (t r q g h) -> p t r q g h", t=8, r=2, q=4, g=2)[:, :, :, :, g2, :], in_=Zfv,
                                             mul=self.cst.ap[:, 384 + g2:385 + g2]),
                      reads=[Zf.k(), self.cst.k()], writes=[zmb.k()])
            zmbv = zmb.ap.rearrange("p (t r q gh) -> p t r q gh", t=8, r=2, q=4)
            R = Rb[fc % 2]
            Rv = R.ap.rearrange("p (r q c) -> p r q c", r=2, q=4)
            P.pool(lambda e, Rv=Rv: e.memset(Rv[:, 0, :, 0:1], 1.0), writes=[R.k()])
            P.pool(lambda e, Rv=Rv: e.memset(Rv[:, 1, :, 0:1], 0.0), writes=[R.k()])
            for m in range(7):
                n_ = 1 << m
                t1 = PT1.ap[:, 0:4 * n_].rearrange("p (q c) -> p q c", q=4)
                t2 = PT2.ap[:, 0:4 * n_].rearrange("p (q c) -> p q c", q=4)
                ere = sl(EM + 2 * m)[:, prs].unsqueeze(2).to_broadcast([128, 4, n_])
                eim = sl(EM + 2 * m + 1)[:, prs].unsqueeze(2).to_broadcast([128, 4, n_])
                sre, sim = Rv[:, 0, :, 0:n_], Rv[:, 1, :, 0:n_]
                dre, dim = Rv[:, 0, :, n_:2 * n_], Rv[:, 1, :, n_:2 * n_]
                P.pool(lambda e, t1=t1, sre=sre, ere=ere: e.tensor_tensor(out=t1, in0=sre, in1=ere, op=ALU.mult), reads=[R.k()] + K, writes=[PT1.k()])
                P.pool(lambda e, t2=t2, sim=sim, eim=eim: e.tensor_tensor(out=t2, in0=sim, in1=eim, op=ALU.mult), reads=[R.k()] + K, writes=[PT2.k()])
                P.pool(lambda e, t1=t1, t2=t2, dre=dre: e.tensor_tensor(out=dre, in0=t1, in1=t2, op=ALU.subtract), reads=[PT1.k(), PT2.k()], writes=[R.k()])
                P.pool(lambda e, t1=t1, sre=sre, eim=eim: e.tensor_tensor(out=t1, in0=sre, in1=eim, op=ALU.mult), reads=[R.k()] + K, writes=[PT1.k()])
                P.pool(lambda e, t2=t2, sim=sim, ere=ere: e.tensor_tensor(out=t2, in0=sim, in1=ere, op=ALU.mult), reads=[R.k()] + K, writes=[PT2.k()])
                P.pool(lambda e, t1=t1, t2=t2, dim=dim: e.tensor_tensor(out=dim, in0=t1, in1=t2, op=ALU.add), reads=[PT1.k(), PT2.k()], writes=[R.k()])
            zflat = zmb.ap.rearrange("p (t r c) -> p t r c", t=8, r=2)
            for kb in range(4):
                b = self.bank()
                pbb = self.psb(b).bitcast(BF16)
                for j in range(4):
                    t, r = (4 * kb + j) // 2, (4 * kb + j) % 2
                    P.pe(lambda e, pbb=pbb, j=j, t=t, r=r, zflat=zflat: e.transpose(pbb[:, j * 128:(j + 1) * 128], zflat[:, t, r, :], self.identb.ap),
                         reads=[zmb.k(), self.identb.k()], writes=[self.pk(b)])
                self.copy(self.evac_eng(), ma.ap[:, kb * 512:(kb + 1) * 512], pbb[:, 0:512], [self.pk(b)], [ma.k()])
            bqs = [self.bank() for _ in range(4)]
            for q in range(4):
                rows = slice(32 * q, 32 * q + 32)
                for r in range(2):
                    for i in range(8):
                        P.pe(lambda e, b=bqs[q], q=q, r=r, i=i, rows=rows, mav=mav, fc=fc: e.matmul(
                            self.psb(b)[:, r * 128:(r + 1) * 128], lhsT=mav[rows, 7 - i, r, :], rhs=Uv[rows, fc, i:NPR:8],
                            start=(i == 0), stop=(i == 7), tile_position=(32 * q, 0)),
                            reads=[ma.k(), U.k(fc)], writes=[self.pk(bqs[q])])
                    if main and (SW & 2):
                        P.pe(lambda e, b=bqs[q], q=q, r=r, rows=rows, mav=mav, fc=fc: e.matmul(
                            self.psb(b)[:, 256 + r * 16:256 + (r + 1) * 16], lhsT=mav[rows, 0, r, :], rhs=Uv[rows, fc, NPR:NM],
                            start=True, stop=True, tile_position=(32 * q, 0)),
                            reads=[ma.k(), U.k(fc)], writes=[self.pk(bqs[q])])
                self.copy("act", Ssv[:, q], self.psb(bqs[q])[:, 0:256].rearrange("p (r c) -> p r c", r=2), [self.pk(bqs[q])], [Ssb.k()])
                if main and (SW & 2):
                    self.copy("act", BUv[:, 4 * fc + q], self.psb(bqs[q])[:, 256:288].rearrange("p (r s) -> p r s", r=2), [self.pk(bqs[q])], [BUs.k()])
            Rre, Rim = Rv[:, 0], Rv[:, 1]
            Sre, Sim = Ssv[:, :, 0, :], Ssv[:, :, 1, :]

            def rot(ore, oim, are_, aim_, conj, rk, wk, Rre=Rre, Rim=Rim, R=R):
                P.dve(lambda e: e.tensor_tensor(out=r1, in0=Rre, in1=are_, op=ALU.mult), reads=rk + [R.k()], writes=[RT1.k()])
                P.dve(lambda e: e.tensor_tensor(out=r2, in0=Rim, in1=aim_, op=ALU.mult), reads=rk + [R.k()], writes=[RT2.k()])
                P.dve(lambda e: e.tensor_tensor(out=ore, in0=r1, in1=r2, op=(ALU.add if conj else ALU.subtract)), reads=[RT1.k(), RT2.k()], writes=wk)
                P.dve(lambda e: e.tensor_tensor(out=r1, in0=Rre, in1=aim_, op=ALU.mult), reads=rk + [R.k()], writes=[RT1.k()])
                P.dve(lambda e: e.tensor_tensor(out=r2, in0=Rim, in1=are_, op=ALU.mult), reads=rk + [R.k()], writes=[RT2.k()])
                P.dve(lambda e: e.tensor_tensor(out=oim, in0=r1, in1=r2, op=(ALU.subtract if conj else ALU.add)), reads=[RT1.k(), RT2.k()], writes=wk)
            rot(BPv[:, :, 0, :], BPv[:, :, 1, :], Sre, Sim, True, [Ssb.k()], [BP.k()])
            for q in range(4):
                pr = 4 * fc + q
                for r in range(2):
                    P.dve(lambda e, q=q, r=r, pr=pr: e.tensor_tensor_scan(out=Wv[:, q, r, :], data0=sl(RHO8)[:, pr:pr + 1].to_broadcast([128, 128]),
                                                                          data1=BPv[:, q, r, :], initial=sl(INIT + r)[:, pr:pr + 1], op0=ALU.mult, op1=ALU.add),
                          reads=[BP.k()] + K, writes=[Wt.k()])
            rot(Xv[:, :, 0, :], Xv[:, :, 1, :], Wv[:, :, 0, :], Wv[:, :, 1, :], False, [Wt.k()], [Xt.k()])
            for r in range(2):
                self.copy("dve", sl(XFIN + r)[:, 4 * fc:4 * fc + 4], Xv[:, :, r, 127], [Xt.k()], K)
            if not main:
                continue
            for r in range(2):
                self.copy("act", Xpv[:, :, r, 1:128], Xv[:, :, r, 0:127], [Xt.k()], [Xp.k()])
                self.copy("act", Xpv[:, :, r, 0], sl(XIN + r)[:, 4 * fc:4 * fc + 4], K, [Xp.k()])
            KTq = KT.ap.rearrange("p (t q c) -> p t q c", t=8, q=4)
            for q in range(4 if (SW & 4) else 0):
                pr = 4 * fc + q
                bkq = self.bank()
                P.pe(lambda e, bkq=bkq, fc=fc: e.matmul(self.psb(bkq)[:, 0:256], lhsT=self.zerob.ap, rhs=Uv[:, fc, 0:256], start=True, stop=False),
                     reads=[self.zerob.k(), U.k(fc)], writes=[self.pk(bkq)])
                for t in range(8):
                    for r in range(2):
                        P.pe(lambda e, t=t, q=q, r=r, pr=pr, zmbv=zmbv, bkq=bkq: e.matmul(
                            self.ps[32 * q:32 * q + 32, bkq * 512 + t * 32:bkq * 512 + (t + 1) * 32],
                            lhsT=zmbv[:, t, r, q, :], rhs=Cmbv[:, r, pr, :], start=False, stop=(t == 7 and r == 1), tile_position=(0, 32 * q)),
                            reads=[zmb.k(), Cmb.k()], writes=[self.pk(bkq)])
                P.act(lambda e, q=q, bkq=bkq: e.mul(out=KTq[:, :, q, :], in_=self.psb(bkq)[:, 0:256].rearrange("p (t c) -> p t c", t=8),
                                                   mul=self.cst.ap[:, 453 + q:454 + q]),
                      reads=[self.pk(bkq), self.cst.k()], writes=[KT.k()])
            pwre = smv[:, PW + 2:PW + 18:2, 4 * fc:4 * fc + 4].rearrange("p j q -> p q j").unsqueeze(3).to_broadcast([128, 4, 8, 32])
            pwim = smv[:, PW + 3:PW + 19:2, 4 * fc:4 * fc + 4].rearrange("p j q -> p q j").unsqueeze(3).to_broadcast([128, 4, 8, 32])
            cre = Cmv[:, 0, 4 * fc:4 * fc + 4, :].unsqueeze(2).to_broadcast([128, 4, 8, 32])
            cimn = Cmv[:, 1, 4 * fc:4 * fc + 4, :].unsqueeze(2).to_broadcast([128, 4, 8, 32])
            ck = [self.Cm.k()] + K
            P.dve(lambda e, cre=cre, pwre=pwre: e.tensor_tensor(out=c1, in0=cre, in1=pwre, op=ALU.mult), reads=ck, writes=[CT1.k()])
            P.dve(lambda e, cimn=cimn, pwim=pwim: e.tensor_tensor(out=c2, in0=cimn, in1=pwim, op=ALU.mult), reads=ck, writes=[CT2.k()])
            P.dve(lambda e: e.tensor_tensor(out=CLv[:, :, :, 0, :], in0=c1, in1=c2, op=ALU.add), reads=[CT1.k(), CT2.k()], writes=[CL.k()])
            P.dve(lambda e, cimn=cimn, pwre=pwre: e.tensor_tensor(out=c1, in0=cimn, in1=pwre, op=ALU.mult), reads=ck, writes=[CT1.k()])
            P.dve(lambda e, cre=cre, pwim=pwim: e.tensor_tensor(out=c2, in0=cre, in1=pwim, op=ALU.mult), reads=ck, writes=[CT2.k()])
            P.dve(lambda e: e.tensor_tensor(out=CLv[:, :, :, 1, :], in0=c1, in1=c2, op=ALU.subtract), reads=[CT1.k(), CT2.k()], writes=[CL.k()])
            by = self.bank2()
            for j in range(8 if (SW & 16) else 0):
                reg0 = by * 512 + j * 128
                for i in range(j + 1):
                    P.pe(lambda e, j=j, i=i, reg0=reg0, fc=fc: e.matmul(self.ps[:, reg0:reg0 + 128], lhsT=KTv[:, j - i, :], rhs=Uv[:, fc, i:NPR:8],
                                                                       start=(i == 0), stop=False),
                         reads=[KT.k(), U.k(fc)], writes=[self.pk(by + j // 4)])
                for q in range(4):
                    for r in range(2):
                        P.pe(lambda e, j=j, q=q, r=r, reg0=reg0: e.matmul(self.ps[32 * q:32 * q + 32, reg0:reg0 + 128], lhsT=CLv[:, q, j, r, :], rhs=Xpv[:, q, r, :],
                                                                          start=False, stop=(q == 3 and r == 1), tile_position=(0, 32 * q)),
                             reads=[CL.k(), Xp.k()], writes=[self.pk(by + j // 4)])
            bs = self.bank()
            if SW & 16:
                P.pe(lambda e, bs=bs, fc=fc: e.matmul(self.psb(bs)[:, 0:16], lhsT=KTv[:, 0, :], rhs=Uv[:, fc, NPR:NM], start=True, stop=False),
                     reads=[KT.k(), U.k(fc)], writes=[self.pk(bs)])
            for q in range(4 if (SW & 16) else 0):
                for r in range(2):
                    P.pe(lambda e, bs=bs, q=q, r=r, fc=fc: e.matmul(self.ps[32 * q:32 * q + 32, bs * 512:bs * 512 + 16], lhsT=CLv[:, q, 0, r, :], rhs=HSbv[:, 4 * fc + q, r, :],
                                                                    start=False, stop=(q == 3 and r == 1), tile_position=(0, 32 * q)),
                         reads=[CL.k(), HSb.k()], writes=[self.pk(bs)])
            dcol = self.pvec.ap[:, 8 + fc:9 + fc]
            for k in range(2):
                P.dve(lambda e, k=k, fc=fc, dcol=dcol, by=by: e.scalar_tensor_tensor(
                    out=Yt.ap[:, 0:NPR].rearrange("p (c j) -> p j c", j=8)[:, 4 * k:4 * k + 4, :],
                    in0=Uv[:, fc, 0:NPR].rearrange("p (c j) -> p j c", j=8)[:, 4 * k:4 * k + 4, :], scalar=dcol,
                    in1=self.psb(by + k).rearrange("p (j c) -> p j c", j=4), op0=ALU.mult, op1=ALU.add),
                    reads=[U.k(fc), self.pk(by + k), self.pvec.k()], writes=[Yt.k()])
            P.dve(lambda e, fc=fc, dcol=dcol, bs=bs: e.scalar_tensor_tensor(out=Yt.ap[:, NPR:NM], in0=Uv[:, fc, NPR:NM], scalar=dcol, in1=self.psb(bs)[:, 0:16],
                                                                            op0=ALU.mult, op1=ALU.add),
                  reads=[U.k(fc), self.pk(bs), self.pvec.k()], writes=[Yt.k()])
            P.act(lambda e, fc=fc: e.activation(out=Gv[:, fc, :], in_=Yt.ap, func=AF.Gelu), reads=[Yt.k()], writes=[G.k(fc)])
        for r, dst in ((0, self.srep_d), (1, self.simp_d)):
            for g2 in range(2):
                P.dma("sp", "xfin", lambda e, r=r, g2=g2, dst=dst: e.dma_start(out=dst.rearrange("(pr g) n -> g n pr", g=2)[g2], in_=sl(XFIN + r)[g2 * 64:(g2 + 1) * 64, :],
                                                                               allow_slow_non_contiguous=True),
                      reads=K, writes=[("dram", "xfin")])
        ar.free(Rb[0], Rb[1], PT1, PT2, zm, zmb, Cmb, ma, Zf, CT1, CT2, Ssb, BP, Wt, RT1, RT2)
        if main:
            ar.free(Xp, KT, CL, Yt)
            stm = ar.alloc("stm2", 4096, F32)
            XN = ar.alloc("XN", 32 * 2 * 16, F32)
            XNv = XN.ap.rearrange("p (pr r s) -> p pr r s", pr=32, r=2)
            T1 = ar.alloc("XT1", 512, F32)
            t1 = T1.ap.rearrange("p (pr s) -> p pr s", pr=32)
            lre = sl(S["LBRE"]).unsqueeze(2).to_broadcast([128, 32, 16])
            lim = sl(S["LBIM"]).unsqueeze(2).to_broadcast([128, 32, 16])
            hre, him = HSv[:, :, 0, :], HSv[:, :, 1, :]
            for (o, a, b_, sgn, bu) in ((XNv[:, :, 0, :], hre, him, ALU.subtract, BUv[:, :, 0, :]), (XNv[:, :, 1, :], him, hre, ALU.add, BUv[:, :, 1, :])):
                P.dve(lambda e, o=o, a=a: e.tensor_tensor(out=o, in0=a, in1=lre, op=ALU.mult), reads=[HS.k()] + K, writes=[XN.k()])
                P.dve(lambda e, b_=b_: e.tensor_tensor(out=t1, in0=b_, in1=lim, op=ALU.mult), reads=[HS.k()] + K, writes=[T1.k()])
                P.dve(lambda e, o=o, sgn=sgn: e.tensor_tensor(out=o, in0=o, in1=t1, op=sgn), reads=[XN.k(), T1.k()], writes=[XN.k()])
                P.dve(lambda e, o=o, bu=bu: e.tensor_tensor(out=o, in0=o, in1=bu, op=ALU.add), reads=[XN.k(), BUs.k()], writes=[XN.k()])
            TPs = [ar.alloc("TP%d" % i, 128, F32) for i in range(4)]
            for tp in TPs:
                P.dve(lambda e, tp=tp: e.memset(tp.ap, 0.0), writes=[tp.k()])
            for r, dst in ((0, self.sres_d), (1, self.sims_d)):
                for g in range(8):
                    b = self.bank()
                    for j in range(4):
                        pr = 4 * g + j
                        tp = TPs[j]
                        self.copy("dve", tp.ap[:, 0:16], XNv[:, pr, r, :], [XN.k()], [tp.k()])
                        P.pe(lambda e, b=b, j=j, tp=tp: e.transpose(self.psb(b)[:, j * 128:(j + 1) * 128], tp.ap, self.ident.ap),
                             reads=[tp.k(), self.ident.k()], writes=[self.pk(b)])
                    self.copy(self.evac_eng(), stm.ap[0:16, g * 512:(g + 1) * 512], self.psb(b)[0:16, :], [self.pk(b)], [stm.k()])
                P.dma("sp", "stm", lambda e, dst=dst: e.dma_start(out=dst, in_=stm.ap[0:16, :]), reads=[stm.k()], writes=[("dram", "sstate")])
            ar.free(*TPs)
            ar.free(XN, T1, HS, HSb, BUs, stm)


TOTAL_SBUF = 206 * 1024


def build_program():
    nc = bass.Bass("TRN2", target_bir_lowering=False)
    mk = MK(nc)
    P = mk.P

    def din(name, shape):
        return nc.dram_tensor(name, list(shape), F32, kind="ExternalInput").ap()

    def dout(name, shape):
        return nc.dram_tensor(name, list(shape), F32, kind="ExternalOutput").ap()

    xin_d = din("xin", [NM, D])
    xpre_d = din("xpre", [NPR, D])
    mk.mem_d = din("mem", [256, D])
    mk.ck_d = din("ck", [NS, 256, 1024])
    mk.cv_d = din("cv", [NS, 256, 1024])
    mk.spool_d = din("spool", [NS, 15, 1024])
    mk.sre_d = din("sre", [NS, 4096])
    mk.sim_d = din("sim", [NS, 4096])
    w1g, w1u, w1d = din("w1g", [D, DFF]), din("w1u", [D, DFF]), din("w1d", [DFF, D])
    w2g, w2u, w2d = din("w2g", [D, DFF]), din("w2u", [D, DFF]), din("w2d", [DFF, D])
    mk.win = din("win", [D, INW])
    mk.wgrp = din("wgrp", [4, 256, 256])
    mk.wpo = din("wpo", [1024, D])
    mk.wgv = din("wgv", [1024, D])
    mk.wgg = din("wgg", [1024, D])
    mk.wmk = din("wmk", [D, 1024])
    mk.wmv = din("wmv", [D, 1024])
    mk.wxo = din("wxo", [1024, D])
    wout = din("wout", [D, D])
    mk.gains_d = din("gains", [7, D])
    mk.pscale_d = din("pscale", [1024])
    mk.ssmd_d = din("ssmd", [1024])
    mk.are_d = din("are", [64, 64])
    mk.aim_d = din("aim", [64, 64])
    mk.lstep_d = din("lstep", [64])
    mk.bre_d = din("bre", [64, 64, 16])
    mk.bim_d = din("bim", [64, 64, 16])
    mk.cre_d = din("cre", [64, 16, 64])
    mk.cim_d = din("cim", [64, 16, 64])
    mk.cst_d = din("cst", [128, CW])
    y_d = dout("y", [NM, D])
    mk.mk_d = dout("mk", [256, 1024])
    mk.mv_d = dout("mv", [256, 1024])
    mk.poolp_d = dout("poolp", [15, 1024])
    mk.srep_d = dout("srep", [64, 64])
    mk.simp_d = dout("simp", [64, 64])
    mk.pools_d = dout("pools", [NS, 15, 1024])
    mk.sres_d = dout("sres", [NS, 4096])
    mk.sims_d = dout("sims", [NS, 4096])
    mk.scrA = nc.dram_tensor("scrA", [128, 16, NM], F32, kind="Internal").ap()
    mk.scrB = nc.dram_tensor("scrB", [128, 16, NM], F32, kind="Internal").ap()

    with contextlib.ExitStack() as st:
        big = st.enter_context(nc.sbuf_tensor("big", [128, TOTAL_SBUF // 4], F32))
        mk.ps = st.enter_context(nc.psum_tensor("ps", [128, 8 * 512], F32))
        ar = mk.ar = Arena(nc, big, TOTAL_SBUF)
        mk.wi = 0
        mk.wslots = [ar.alloc("wslot%d" % i, WSLOT, BF16) for i in range(NWSLOT)]
        mk.tmp = [ar.alloc("tmp%d" % i, 352, F32) for i in range(2)]
        mk.tmp2 = [ar.alloc("tmpb%d" % i, 352, F32) for i in range(2)]
        mk.t16 = ar.alloc("t16", 32, F32)
        mk.halo = ar.alloc("halo", 8 * 15, F32)
        xkeep = ar.alloc("xkeep", 64, F32)
        mk.setup_consts()
        flag = mk.cst.ap[:, 450:451]

        def scoped(fn):
            mk.xt = [ar.alloc("xt%d" % i, 2048, F32) for i in range(2)]
            mk.sq = [ar.alloc("sq%d" % i, NM, BF16) for i in range(2)]
            mk.rt = [ar.alloc("rt%d" % i, NM, F32) for i in range(2)]
            fn()
            ar.free(*mk.xt, *mk.sq, *mk.rt)

        def ffn_block(src_d, n, gi_pre, W, gpost_col, resid, resid_key, out, out_key, gi_next):
            pass

        def run_front(src_d, n, gi):
            xn = ar.alloc("xn", 16 * n, BF16)
            scoped(lambda: mk.front(src_d, n, gi, xn, mk.scrA, "scrA"))
            return xn

        def run_ffn(xn, n, Wg, Wu, Wd):
            fo = ar.alloc("fo", 16 * n, F32)
            fov = fo.ap.rearrange("p (c n) -> p c n", c=16)
            mk.ffn(xn.ap.rearrange("p (c n) -> p c n", c=16), lambda c: xn.k(c), n, Wg, Wu, Wd, fov, lambda c: fo.k(c))
            ar.free(xn)
            return fo

        def run_epi(fo, n, gpost_col, resid, resid_key, out, out_key, gi_next):
            xn_next = ar.alloc("xn", 16 * n, BF16) if gi_next is not None else None
            scoped(lambda: mk.epilogue(fo, n, gpost_col, resid, resid_key, out, out_key, gi_next, xn_next))
            return xn_next

        try:
            mk.stage(1)
            xn = run_front(xpre_d, NPR, 0)
            mk.stage(2)
            fo = run_ffn(xn, NPR, w1g, w1u, w1d)
            mk.stage(3)
            xn2 = run_epi(fo, NPR, 0, mk.scrA, "scrA", None, None, 2)
            mk.stage(4)
            ar.free(fo)
            U = ar.alloc("U", 8 * NPR, BF16)
            Uv = U.ap.rearrange("p (c n) -> p c n", c=8)
            mk.win_proj(xn2, NPR, OFF_SSM, 8, lambda oc, ti, s, e_, b: mk.copy(mk.evac_eng(), Uv[:, oc, s:e_], mk.psb(b)[:, 0:e_ - s], [mk.pk(b)], [U.k(oc)]))
            hv = mk.halo.ap.rearrange("p (c n) -> p c n", c=8)

            def halo_ev(oc, ti, s, e_, b):
                P.dve(lambda e: e.tensor_scalar(out=hv[:, oc, :], in0=mk.psb(b)[:, 1:16], scalar1=flag, scalar2=None, op0=ALU.mult),
                      reads=[mk.pk(b), mk.cst.k()], writes=[mk.halo.k()])
            mk.win_proj(xn2, NPR, 0, 8, halo_ev, tts=[(NPR - 16, NPR)])
            ar.free(xn2)
            mk.stage(5)
            mk.ssm_setup()
            mk.stage(6)
            mk.ssm_run(U, False)
            mk.stage(7)
            sl = mk.SL["sl"]
            for r in range(2):
                P.dve(lambda e, r=r, src=sl(mk.SL["XFIN"] + r): e.tensor_scalar(out=xkeep.ap[:, r * 32:(r + 1) * 32], in0=src, scalar1=flag, scalar2=None, op0=ALU.mult),
                      reads=[mk.sm.k(), mk.cst.k()], writes=[xkeep.k()])
            ar.free(U, mk.sm, mk.BB, mk.Cm)

            mk.stage(8)
            xn = run_front(xin_d, NM, 0)
            fo = run_ffn(xn, NM, w1g, w1u, w1d)
            xn2 = run_epi(fo, NM, 0, mk.scrA, "scrA", mk.scrB, "scrB", 2)
            ar.free(fo)
            U = ar.alloc("U", 8 * NM, BF16)
            Uv = U.ap.rearrange("p (c n) -> p c n", c=8)
            mk.win_proj(xn2, NM, OFF_SSM, 8, lambda oc, ti, s, e_, b: mk.copy(mk.evac_eng(), Uv[:, oc, s:e_], mk.psb(b)[:, 0:e_ - s], [mk.pk(b)], [U.k(oc)]))
            mk.ssm_setup()
            sl = mk.SL["sl"]
            for r in range(2):
                mk.copy("dve", sl(mk.SL["XIN"] + r), xkeep.ap[:, r * 32:(r + 1) * 32], [xkeep.k()], [mk.sm.k()])
            mk.stage(9)
            mk.ssm_run(U, True)
            mk.stage(10)
            ar.free(mk.sm, mk.BB, mk.Cm)
            G = U
            scoped_pool = mk.pool_branch(xn2)
            Z = scoped_pool
            mk.stage(11)
            scoped(lambda: mk.mem_kv())
            mk.stage(12)
            O = mk.attn_branch(xn2)
            mk.stage(13)
            ar.free(mk.kT, mk.vb)
            mb = ar.alloc("mb", 16 * NM, BF16)
            extra = [ar.alloc("wslotx%d" % i, WSLOT, BF16) for i in range(2)]
            mk.wslots = mk.wslots + extra
            mk.acc = [ar.alloc("acc%d" % i, 2 * 352, F32) for i in range(3)]
            mk.merge_all(xn2, Z, G, O, mb)
            mk.stage(14)
            ar.free(xn2, Z, G, O, *mk.acc)
            fo = ar.alloc("fo", 16 * NM, F32)
            fov = fo.ap.rearrange("p (c n) -> p c n", c=16)
            mbv = mb.ap.rearrange("p (c n) -> p c n", c=16)
            tts = tt_list(NM)
            for t in range(8):
                wv, wk = mk.wload(wout[:, t * 256:(t + 1) * 256].rearrange("(c p) n -> p c n", p=128), 16, 256)
                mk.project(lambda oc, wv=wv, wk=wk: ((lambda kc, oc=oc: wv[:, kc, oc * 128:(oc + 1) * 128]), [wk]),
                           16, lambda kc, s, e_: (mbv[:, kc, s:e_], [mb.k(kc)]), 2, tts,
                           lambda oc, ti, s, e_, b, t=t: mk.copy(mk.evac_eng(), fov[:, 2 * t + oc, s:e_], mk.psb(b)[:, 0:e_ - s], [mk.pk(b)], [fo.k(2 * t + oc)]))
            ar.free(mb)
            mk.wslots = mk.wslots[:NWSLOT]
            ar.free(*extra)
            mk.stage(15)
            xn3 = run_epi(fo, NM, 1, mk.scrB, "scrB", mk.scrA, "scrA", 4)
            ar.free(fo)
            fo = run_ffn(xn3, NM, w2g, w2u, w2d)
            run_epi(fo, NM, 2, mk.scrA, "scrA", None, None, None)
            scoped(lambda: mk.transpose_out(fo.ap.rearrange("p (c n) -> p c n", c=16), lambda c: fo.k(c), NM, y_d))
            ar.free(fo)
        except _Stop:
            pass
        P.emit()
    return nc


_NC_CACHE = {}


def _consts(core):
    half = core % 2
    c = np.zeros((128, CW), np.float32)
    c[:, 0:128] = np.eye(128, dtype=np.float32)
    c[:, 128:384] = np.tile(np.eye(16, dtype=np.float32).reshape(1, 256), (128, 1))
    p = np.arange(128)
    c[:, 384] = (p // 64 == 0)
    c[:, 385] = (p // 64 == 1)
    inv = np.zeros((4, 16), np.float32)
    for k in range(4):
        w = 2 << k
        for j in range(16):
            inv[k, j] = 1.0 / (min(j + 1, w) if half == 0 else w)
    c[:, 386:450] = inv.reshape(1, 64)
    c[:, 450] = float(half)
    c[:, 451] = ((p // 16) % 2 == 0)
    c[:, 452] = ((p // 16) % 2 == 1)
    for q in range(4):
        c[:, 453 + q] = (p // 32 == q)
    return c


def kernel(**inp):
    f = lambda a: np.ascontiguousarray(np.asarray(a, dtype=np.float32))
    x_prompt, x_sample, mem_prompt = f(inp["x_prompt"]), f(inp["x_sample"]), f(inp["mem_prompt"])
    shared = dict(
        w1g=f(inp["w_ff1_gate"][0]), w1u=f(inp["w_ff1_up"][0]), w1d=f(inp["w_ff1_down"][0]),
        w2g=f(inp["w_ff2_gate"][0]), w2u=f(inp["w_ff2_up"][0]), w2d=f(inp["w_ff2_down"][0]),
        win=f(inp["w_in"][0]), wgrp=f(inp["w_pool_grp"][0]), wpo=f(inp["w_pool_out"][0]),
        wgv=f(inp["w_glu_val"][0]), wgg=f(inp["w_glu_gate"][0]), wmk=f(inp["w_mem_k"][0]), wmv=f(inp["w_mem_v"][0]),
        wxo=f(inp["w_xa_out"][0]), wout=f(inp["w_out"][0]),
        gains=f(np.stack([inp["g_ff1_pre"][0], inp["g_ff1_post"][0], inp["g_mix_pre"][0], inp["g_mix_post"][0],
                          inp["g_ff2_pre"][0], inp["g_ff2_post"][0], inp["g_mem"][0]])),
        pscale=f(inp["pool_scale"][0]), ssmd=f(inp["ssm_d"][0]), are=f(inp["ssm_a_re"][0]), aim=f(inp["ssm_a_im"][0]),
        lstep=f(inp["ssm_log_step"][0]), bre=f(inp["ssm_b_re"][0]), bim=f(inp["ssm_b_im"][0]),
        cre=f(inp["ssm_c_re"][0]), cim=f(inp["ssm_c_im"][0]),
    )
    ck = f(inp["cache_mem_k"][0]).reshape(128, 256, 1024)
    cv = f(inp["cache_mem_v"][0]).reshape(128, 256, 1024)
    spool = f(inp["state_pool"][0])
    sre = f(inp["state_ssm_re"][0]).reshape(128, 4096)
    sim = f(inp["state_ssm_im"][0]).reshape(128, 4096)
    in_maps = []
    for c in range(8):
        b, half = c // 2, c % 2
        ss = slice(NS * c, NS * (c + 1))
        m = dict(shared)
        m["xin"] = np.ascontiguousarray(np.concatenate([x_prompt[b, half * NPR:(half + 1) * NPR], x_sample[ss, 0]], axis=0))
        m["xpre"] = np.ascontiguousarray(x_prompt[b, 0:NPR])
        m["mem"] = mem_prompt[b]
        m["ck"], m["cv"] = ck[ss], cv[ss]
        m["spool"], m["sre"], m["sim"] = spool[ss], sre[ss], sim[ss]
        m["cst"] = _consts(c)
        in_maps.append(m)
    if "nc" not in _NC_CACHE:
        _NC_CACHE["nc"] = build_program()
    res = run_bass_kernel_spmd(_NC_CACHE["nc"], in_maps, core_ids=list(range(8))).results
    y_p = np.zeros((4, 2048, D), np.float32)
    y_s = np.zeros((128, 1, D), np.float32)
    mk_o = np.zeros((1, 4, 256, 4, 256), np.float32)
    mv_o = np.zeros((1, 4, 256, 4, 256), np.float32)
    pp = np.zeros((1, 4, 15, 1024), np.float32)
    rp = np.zeros((1, 4, 64, 64), np.float32)
    ip = np.zeros((1, 4, 64, 64), np.float32)
    ps_ = np.zeros((1, 128, 15, 1024), np.float32)
    rs = np.zeros((1, 128, 64, 64), np.float32)
    is_ = np.zeros((1, 128, 64, 64), np.float32)
    for c in range(8):
        b, half = c // 2, c % 2
        r = res[c]
        y_p[b, half * NPR:(half + 1) * NPR] = r["y"][0:NPR]
        y_s[NS * c:NS * (c + 1), 0] = r["y"][NPR:NM]
        ps_[0, NS * c:NS * (c + 1)] = r["pools"]
        rs[0, NS * c:NS * (c + 1)] = r["sres"].reshape(NS, 64, 64)
        is_[0, NS * c:NS * (c + 1)] = r["sims"].reshape(NS, 64, 64)
        if half == 0:
            mk_o[0, b] = r["mk"].reshape(256, 4, 256)
            mv_o[0, b] = r["mv"].reshape(256, 4, 256)
        else:
            pp[0, b] = r["poolp"]
            rp[0, b] = r["srep"]
            ip[0, b] = r["simp"]
    return (y_p, y_s, mk_o, mv_o, pp, rp, ip, ps_, rs, is_)
```
